# Optimizing a Trainium2 kernel written in Bass

```python
import math
import jax, jax.numpy as jnp
from jax import lax
import numpy as np

D_MODEL = 1024
BATCH = 8
SEQ = 2048
DEPTH = 2
DEC_BATCH = 128
DEC_SEQ = 8
PAST_LEN = 16384
PAGE_SIZE = 128

D_CONV = D_MODEL
CONV_A_W = 3
D_RNN = D_MODEL
CONV_B_W = 4
N_BLK = 16
BLK = D_RNN // N_BLK
LRU_C = 8.0
N_MEM = 256
MEM_HEADS = 4
MEM_HEAD_DIM = D_MODEL // MEM_HEADS
D_FF = int(math.ceil(8 * D_MODEL / 3 / 256) * 256)
N_NORMS = 7
EPS = 1e-6
D_IN = 3 * D_CONV + D_RNN + 2 * D_MODEL

kernel_name = "hybrid_conv_rglru_xattn_decoder_step"


def rms_norm(x, g):
    xf = x.astype(jnp.float32)
    y = xf * lax.rsqrt(jnp.mean(xf * xf, axis=-1, keepdims=True) + EPS)
    return (y * g.astype(jnp.float32)).astype(x.dtype)


def causal_dwconv(x, buf, w):
    width = w.shape[0]
    t = x.shape[1]
    xp = jnp.concatenate([buf.astype(x.dtype), x], axis=1)
    y = xp[:, 0:t] * w[0]
    for k in range(1, width):
        y = y + xp[:, k:k + t] * w[k]
    return y, xp[:, t:]


def rg_lru(u, h0, w_a, b_a, w_x, b_x, lam):
    bsz, t, c = u.shape
    ub = u.reshape(bsz, t, N_BLK, BLK)
    r = jax.nn.sigmoid(jnp.einsum('btnk,nkj->btnj', ub, w_a).reshape(bsz, t, c) + b_a)
    i = jax.nn.sigmoid(jnp.einsum('btnk,nkj->btnj', ub, w_x).reshape(bsz, t, c) + b_x)
    log_a = -LRU_C * r.astype(jnp.float32) * jax.nn.softplus(-lam.astype(jnp.float32))
    a = jnp.exp(log_a)
    mult = jnp.sqrt(-jnp.expm1(2.0 * log_a))
    b = mult * (i * u).astype(jnp.float32)
    b = b.at[:, 0].add(a[:, 0] * h0.astype(jnp.float32))

    def combine(c1, c2):
        a1, b1 = c1
        a2, b2 = c2
        return a1 * a2, a2 * b1 + b2

    _, h = lax.associative_scan(combine, (a, b), axis=1)
    return h.astype(u.dtype), h[:, -1]


def mem_kv(mem, g_mem, w_kv):
    m = rms_norm(mem, g_mem)
    kv = (m @ w_kv).reshape(mem.shape[0], N_MEM, 2, MEM_HEADS, MEM_HEAD_DIM)
    return kv[:, :, 0], kv[:, :, 1]


def cross_attend(xn, mk, mv, w_q, w_o):
    bsz, t, _ = xn.shape
    q = (xn @ w_q).reshape(bsz, t, MEM_HEADS, MEM_HEAD_DIM)
    s = jnp.einsum('bthd,bmhd->bhtm', q.astype(jnp.float32), mk.astype(jnp.float32)) * (MEM_HEAD_DIM ** -0.5)
    p = jax.nn.softmax(s, axis=-1).astype(xn.dtype)
    o = jnp.einsum('bhtm,bmhd->bthd', p, mv.astype(xn.dtype)).reshape(bsz, t, D_MODEL)
    return o @ w_o


def layer(x, mk, mv, buf_a, buf_b, h0, lw):
    (g, w_in, conv_a_w, w_conv_out, conv_b_w, conv_b_b, w_gate_a, b_gate_a, w_gate_x, b_gate_x,
     lru_lambda, w_rnn_out, w_mix_out, w_q_x, w_o_x, w_ffn_in, w_ffn_out) = lw
    xn = rms_norm(x, g[0])
    proj = xn @ w_in
    cuts = [D_CONV, 2 * D_CONV, 3 * D_CONV, 3 * D_CONV + D_RNN, 3 * D_CONV + D_RNN + D_MODEL]
    hb, hc, hh, u, gc, gr = jnp.split(proj, cuts, axis=-1)
    ya, new_a = causal_dwconv(hc * hh, buf_a, conv_a_w)
    y_conv = (hb * ya) @ w_conv_out
    uc, new_b = causal_dwconv(u, buf_b, conv_b_w)
    uc = uc + conv_b_b
    hseq, h_last = rg_lru(uc, h0, w_gate_a, b_gate_a, w_gate_x, b_gate_x, lru_lambda)
    y_rnn = hseq @ w_rnn_out
    z = jax.nn.sigmoid(gc) * y_conv + jax.nn.sigmoid(gr) * y_rnn
    x = x + rms_norm(z @ w_mix_out, g[1])
    x = x + rms_norm(cross_attend(rms_norm(x, g[2]), mk, mv, w_q_x, w_o_x), g[3])
    xn = rms_norm(x, g[4])
    gate, up = jnp.split(xn @ w_ffn_in, 2, axis=-1)
    x = x + rms_norm((jax.nn.silu(gate) * up) @ w_ffn_out, g[5])
    return x, new_a, new_b, h_last


def setup_inputs(seed: int = 0) -> dict:
    key = jax.random.key(seed)
    ks = jax.random.split(key, 32)
    f32 = jnp.float32
    nrm = lambda k, shape, s: (jax.random.normal(k, shape, f32) * s)
    a_init = jax.random.uniform(ks[20], (DEPTH, D_RNN), f32, 0.9, 0.999)
    return {
        "x_prompt": nrm(ks[0], (BATCH, SEQ, D_MODEL), 1.0),
        "x_sample": nrm(ks[1], (DEC_BATCH, DEC_SEQ, D_MODEL), 1.0),
        "state_conv_a": nrm(ks[2], (DEPTH, DEC_BATCH, CONV_A_W - 1, D_CONV), 1.0),
        "state_conv_b": nrm(ks[3], (DEPTH, DEC_BATCH, CONV_B_W - 1, D_RNN), 1.0),
        "state_rglru": nrm(ks[4], (DEPTH, DEC_BATCH, D_RNN), 0.5),
        "cache_mem_k": nrm(ks[5], (DEPTH, DEC_BATCH, N_MEM, MEM_HEADS, MEM_HEAD_DIM), 1.0),
        "cache_mem_v": nrm(ks[6], (DEPTH, DEC_BATCH, N_MEM, MEM_HEADS, MEM_HEAD_DIM), 1.0),
        "mem_prompt": nrm(ks[7], (BATCH, N_MEM, D_MODEL), 1.0),
        "norm_gains": 1.0 + nrm(ks[8], (DEPTH, N_NORMS, D_MODEL), 0.05),
        "w_in": nrm(ks[9], (DEPTH, D_MODEL, D_IN), D_MODEL ** -0.5),
        "conv_a_w": nrm(ks[10], (DEPTH, CONV_A_W, D_CONV), CONV_A_W ** -0.5),
        "w_conv_out": nrm(ks[11], (DEPTH, D_CONV, D_MODEL), D_CONV ** -0.5),
        "conv_b_w": nrm(ks[12], (DEPTH, CONV_B_W, D_RNN), CONV_B_W ** -0.5),
        "conv_b_b": nrm(ks[13], (DEPTH, D_RNN), 0.01),
        "w_gate_a": nrm(ks[14], (DEPTH, N_BLK, BLK, BLK), BLK ** -0.5),
        "b_gate_a": nrm(ks[15], (DEPTH, D_RNN), 0.01),
        "w_gate_x": nrm(ks[16], (DEPTH, N_BLK, BLK, BLK), BLK ** -0.5),
        "b_gate_x": nrm(ks[17], (DEPTH, D_RNN), 0.01),
        "lru_lambda": jnp.log(a_init) - jnp.log1p(-a_init),
        "w_rnn_out": nrm(ks[18], (DEPTH, D_RNN, D_MODEL), D_RNN ** -0.5),
        "w_mix_out": nrm(ks[19], (DEPTH, D_MODEL, D_MODEL), D_MODEL ** -0.5),
        "w_kv_x": nrm(ks[21], (DEPTH, D_MODEL, 2 * D_MODEL), D_MODEL ** -0.5),
        "w_q_x": nrm(ks[22], (DEPTH, D_MODEL, D_MODEL), D_MODEL ** -0.5),
        "w_o_x": nrm(ks[23], (DEPTH, D_MODEL, D_MODEL), D_MODEL ** -0.5),
        "w_ffn_in": nrm(ks[24], (DEPTH, D_MODEL, 2 * D_FF), D_MODEL ** -0.5),
        "w_ffn_out": nrm(ks[25], (DEPTH, D_FF, D_MODEL), D_FF ** -0.5),
    }


def reference(x_prompt, x_sample, state_conv_a, state_conv_b, state_rglru, cache_mem_k, cache_mem_v,
              mem_prompt, norm_gains, w_in, conv_a_w, w_conv_out, conv_b_w, conv_b_b, w_gate_a, b_gate_a,
              w_gate_x, b_gate_x, lru_lambda, w_rnn_out, w_mix_out, w_kv_x, w_q_x, w_o_x, w_ffn_in, w_ffn_out):
    xp, xs = x_prompt, x_sample
    bsz = xp.shape[0]
    zero_a = jnp.zeros((bsz, CONV_A_W - 1, D_CONV), xp.dtype)
    zero_b = jnp.zeros((bsz, CONV_B_W - 1, D_RNN), xp.dtype)
    zero_h = jnp.zeros((bsz, D_RNN), jnp.float32)
    pa, pb, ph, pk, pv, sa, sb, sh = [], [], [], [], [], [], [], []
    for l in range(DEPTH):
        lw = (norm_gains[l], w_in[l], conv_a_w[l], w_conv_out[l], conv_b_w[l], conv_b_b[l], w_gate_a[l],
              b_gate_a[l], w_gate_x[l], b_gate_x[l], lru_lambda[l], w_rnn_out[l], w_mix_out[l], w_q_x[l],
              w_o_x[l], w_ffn_in[l], w_ffn_out[l])
        mk, mv = mem_kv(mem_prompt, norm_gains[l, 6], w_kv_x[l])
        xp, na, nb, nh = layer(xp, mk, mv, zero_a, zero_b, zero_h, lw)
        pa.append(na); pb.append(nb); ph.append(nh); pk.append(mk); pv.append(mv)
        xs, ma, mb, mh = layer(xs, cache_mem_k[l], cache_mem_v[l], state_conv_a[l], state_conv_b[l],
                               state_rglru[l], lw)
        sa.append(ma); sb.append(mb); sh.append(mh)
    return (xp, xs, jnp.stack(pa), jnp.stack(pb), jnp.stack(ph), jnp.stack(pk), jnp.stack(pv),
            jnp.stack(sa), jnp.stack(sb), jnp.stack(sh))
```

```python
import contextlib
import numpy as np
import concourse.bass as bass
import concourse.mybir as mybir
from concourse.bass_utils import run_bass_kernel_spmd

F32 = mybir.dt.float32
BF16 = mybir.dt.bfloat16
ALU = mybir.AluOpType
AF = mybir.ActivationFunctionType

ENGS = ("pe", "act", "dve", "pool", "sp")
N_DMA_SEMS = 12

D = 1024
NCH = 8
DFF = 2816
NJ = 22
DEPTH = 2
SEQ = 2048
TP = 512
NPT = SEQ // TP
NSEQ_S = 16
L_S = 8
TS = NSEQ_S * L_S
NMEM = 256
EPS = 1e-6


class Sched:
    def __init__(self, nc):
        self.nc = nc
        self.ops = {e: [] for e in ENGS}
        self.cnt = {e: 0 for e in ENGS}
        self.seen = {e: {} for e in ENGS}
        self.last_w = {}
        self.readers = {}
        self.dma_cnt = [0] * N_DMA_SEMS
        self.dma_rr = {"pool": 0, "sp": 0}

    def _need(self, deps, eng, ev, raw):
        semkey, val, src = ev
        if src == eng:
            if eng == "pe":
                return
        if deps.get(semkey, 0) < val:
            deps[semkey] = val

    def add(self, eng, fn, reads=(), writes=(), dma=False):
        deps = {}
        deng = None if dma else eng
        for t in reads:
            for ev in self.last_w.get(t, ()):
                self._need(deps, deng, ev, True)
        for t in writes:
            for ev in self.last_w.get(t, ()):
                self._need(deps, deng, ev, False)
            for sk, (v, src) in self.readers.get(t, {}).items():
                self._need(deps, deng, (sk, v, src), False)
        if dma:
            half = N_DMA_SEMS // 2
            k = self.dma_rr[eng]
            self.dma_rr[eng] = (k + 1) % half
            i = k + (0 if eng == "pool" else half)
            c = self.dma_cnt[i] + 1
            self.dma_cnt[i] = c
            if c > 1:
                sk = ("dma", i)
                if deps.get(sk, 0) < 16 * (c - 1):
                    deps[sk] = 16 * (c - 1)
            ev = (("dma", i), 16 * c, None)
        else:
            self.cnt[eng] += 1
            ev = (eng, self.cnt[eng], eng)
        waits = []
        seen = self.seen[eng]
        for sk, v in deps.items():
            if seen.get(sk, 0) < v:
                seen[sk] = v
                waits.append((sk, v))
        self.ops[eng].append((fn, waits, ev[0]))
        for t in writes:
            prev = self.last_w.get(t)
            if dma and prev and all(p[2] is None for p in prev):
                self.last_w[t] = (prev + [ev])[-8:]
            else:
                self.last_w[t] = [ev]
            self.readers[t] = {}
        for t in reads:
            if t in writes:
                continue
            r = self.readers.setdefault(t, {})
            old = r.get(ev[0])
            if old is None or old[0] < ev[1]:
                r[ev[0]] = (ev[1], ev[2])
        return ev

    def emit(self):
        nc = self.nc
        waits = []
        for i, c in enumerate(self.dma_cnt):
            if c > 0 and self.seen["sp"].get(("dma", i), 0) < 16 * c:
                waits.append((("dma", i), 16 * c))
        self.ops["sp"].append((None, waits, None))
        with contextlib.ExitStack() as st:
            sems = {}
            for e in ENGS[:4]:
                sems[e] = st.enter_context(nc.semaphore("sem_" + e))
            for i in range(N_DMA_SEMS):
                sems[("dma", i)] = st.enter_context(nc.semaphore("sem_dma%d" % i))
            block = st.enter_context(nc.Block())

            def run(engobj, name):
                for fn, waits, inc in self.ops[name]:
                    for sk, v in waits:
                        engobj.wait_ge(sems[sk], v)
                    if fn is None:
                        continue
                    ins = fn(engobj)
                    if inc is not None:
                        ins.then_inc(sems[inc], 16 if isinstance(inc, tuple) else 1)

            @block.tensor
            def _(e):
                run(e, "pe")

            @block.scalar
            def _(e):
                run(e, "act")

            @block.vector
            def _(e):
                run(e, "dve")

            @block.gpsimd
            def _(e):
                run(e, "pool")

            @block.sync
            def _(e):
                run(e, "sp")


VEC_ROWS = {"gains": (0, 14), "caw": (14, 6), "cbw": (20, 8), "cbb": (28, 2), "bga": (30, 2),
            "bgx": (32, 2), "lam": (34, 2)}
NVEC = 36
NW = 10
WSLOT = 2048
NTMP = 24
TMPW = 528


class Builder:
    def __init__(self):
        nc = bass.Bass("TRN2", target_bir_lowering=False)
        self.nc = nc
        self.S = Sched(nc)
        S = self.S

        def din(name, shape):
            return nc.dram_tensor(name, list(shape), F32, kind="ExternalInput").ap()

        def dout(name, shape):
            return nc.dram_tensor(name, list(shape), F32, kind="ExternalOutput").ap()

        self.d = {}
        d = self.d
        d["xp"] = din("xp", [SEQ, D])
        d["xs"] = din("xs", [TS, D])
        d["sta"] = din("sta", [DEPTH, NSEQ_S * 2, D])
        d["stb"] = din("stb", [DEPTH, NSEQ_S * 3, D])
        d["sth"] = din("sth", [DEPTH, NSEQ_S, D])
        d["ck"] = din("ck", [DEPTH, NSEQ_S, NMEM, D])
        d["cv"] = din("cv", [DEPTH, NSEQ_S, NMEM, D])
        d["memp"] = din("memp", [NMEM, D])
        d["gains"] = din("gains", [14, D])
        d["caw"] = din("caw", [6, D])
        d["cbw"] = din("cbw", [8, D])
        d["cbb"] = din("cbb", [2, D])
        d["bga"] = din("bga", [2, D])
        d["bgx"] = din("bgx", [2, D])
        d["lam"] = din("lam", [2, D])
        d["ident"] = din("ident", [128, 128])
        d["w_in"] = din("w_in", [DEPTH, D, 6 * D])
        d["w_conv_out"] = din("w_conv_out", [DEPTH, D, D])
        d["w_gate_a"] = din("w_gate_a", [DEPTH, 16, 64, 64])
        d["w_gate_x"] = din("w_gate_x", [DEPTH, 16, 64, 64])
        d["w_rnn_out"] = din("w_rnn_out", [DEPTH, D, D])
        d["w_mix_out"] = din("w_mix_out", [DEPTH, D, D])
        d["w_kv"] = din("w_kv", [DEPTH, D, 2 * D])
        d["w_q"] = din("w_q", [DEPTH, D, D])
        d["w_o"] = din("w_o", [DEPTH, D, D])
        d["w_ffn_in"] = din("w_ffn_in", [DEPTH, D, 2 * DFF])
        d["w_ffn_out"] = din("w_ffn_out", [DEPTH, DFF, D])
        d["yp"] = dout("yp", [SEQ, D])
        d["ys"] = dout("ys", [TS, D])
        d["pca"] = dout("pca", [DEPTH, 2, D])
        d["pcb"] = dout("pcb", [DEPTH, 3, D])
        d["ph"] = dout("ph", [DEPTH, 1, D])
        d["pk"] = dout("pk", [DEPTH, NMEM, D])
        d["pv"] = dout("pv", [DEPTH, NMEM, D])
        d["sca"] = dout("sca", [DEPTH, NSEQ_S * 2, D])
        d["scb"] = dout("scb", [DEPTH, NSEQ_S * 3, D])
        d["sh"] = dout("sh", [DEPTH, NSEQ_S, D])

        def sb(name, shape, dt=F32):
            return nc.alloc_sbuf_tensor("sb_" + name, list(shape), dt).ap()

        self.ps = nc.alloc_psum_tensor("ps", [128, 8, 512], F32).ap()
        self.bank_rr = 0
        self.pinned = set()
        self.ident = sb("ident", [128, 128])
        self.ones = sb("ones", [128, 128], BF16)
        self.cst = sb("cst", [128, 8])
        self.vec = sb("vec", [128, NCH, NVEC])
        self.der = sb("der", [128, NCH, 12])
        self.dtmp = sb("dtmp", [128, NCH, 8])
        self.bd = sb("bd", [128, DEPTH * 2, NCH, 128], BF16)
        self.KT = sb("KT", [128, DEPTH, NCH, NMEM], BF16)
        self.V = sb("V", [128, DEPTH, 2, D], BF16)
        self.x = sb("x", [128, NCH, TP])
        self.xn = sb("xn", [128, NCH, TP], BF16)
        self.R = sb("R", [128, 24, TP], BF16)
        self.m = sb("m", [128, NCH, TP])
        self.tmpf = sb("tmpf", [128, NTMP, TMPW])
        self.tmp_rr = 0
        self.kcnt = {}
        self.tmp_pinned = set()
        self.aux = "dve"
        self.tmpb_ = sb("tmpb", [128, 4, TP], BF16)
        self.tmpb_rr = 0
        self.wring = sb("wring", [128, NW, WSLOT], BF16)
        self.stage = sb("stage", [128, 2, 2048])
        self.stage_rr = 0
        self.kts = self.KT
        self.vs = self.V
        self.kv_rr = 0
        self.carA = sb("carA", [128, DEPTH, NCH, 2])
        self.carB = sb("carB", [128, DEPTH, NCH, 3])
        self.carH = sb("carH", [128, DEPTH, NCH, 1])
        self.stA = sb("stA", [128, NCH, NSEQ_S * 2])
        self.stB = sb("stB", [128, NCH, NSEQ_S * 3])
        self.stH = sb("stH", [128, NCH, NSEQ_S])
        self.oA = sb("oA", [128, NCH, NSEQ_S * 2])
        self.oB = sb("oB", [128, NCH, NSEQ_S * 3])
        self.oH = sb("oH", [128, NCH, NSEQ_S])
        self.wplan = []
        self.w_next_load = 0
        self.w_cons = 0
        self.w_released = 0

    def bank(self, pin=False):
        while self.bank_rr in self.pinned:
            self.bank_rr = (self.bank_rr + 1) % 8
        i = self.bank_rr
        self.bank_rr = (i + 1) % 8
        if pin:
            self.pinned.add(i)
        return self.ps[:, i, :], "ps%d" % i

    def unpin(self, tok):
        self.pinned.discard(int(tok[2:]))

    def tmp(self, pin=False):
        while self.tmp_rr in self.tmp_pinned:
            self.tmp_rr = (self.tmp_rr + 1) % NTMP
        i = self.tmp_rr
        self.tmp_rr = (i + 1) % NTMP
        if pin:
            self.tmp_pinned.add(i)
        return self.tmpf[:, i, :], "tf%d" % i

    def tmp_unpin(self, tok):
        self.tmp_pinned.discard(int(tok[2:]))

    KINDS = {"hcs": (0, 1), "G": (1, 2), "ya": (3, 1), "U": (4, 2), "uc": (6, 3), "tr": (9, 1), "ti": (10, 2),
             "a": (12, 2), "a2": (14, 2), "iu": (16, 2), "hs": (18, 2), "t0": (20, 1)}

    def ktmp(self, kind):
        base, depth = self.KINDS[kind]
        k = self.kcnt.get(kind, 0)
        self.kcnt[kind] = k + 1
        i = base + k % depth
        return self.tmpf[:, i, :], "tf%d" % i

    def tmpb(self):
        i = self.tmpb_rr
        self.tmpb_rr = (i + 1) % 4
        return self.tmpb_[:, i, :], "tb%d" % i

    def stg(self):
        i = self.stage_rr
        self.stage_rr = (i + 1) % 2
        return self.stage[:, i, :], "stg%d" % i

    def vcol(self, name, row, c):
        base = VEC_ROWS[name][0] + row
        return self.vec[:, c, base:base + 1]

    def plan_weights(self):
        d = self.d
        plan = []

        def blk(w, l, c0, n):
            return (w[l, :, c0:c0 + n].rearrange("(c p) n -> p c n", p=128), int(w.shape[1]) // 128, n)

        def layer_blocks(l):
            out = []
            for cp in range(4):
                for sec in (0, 1, 2, 3):
                    out.append(blk(d["w_in"], l, sec * D + cp * 256, 256))
            for cp in range(4):
                out.append(blk(d["w_conv_out"], l, cp * 256, 256))
                out.append(blk(d["w_rnn_out"], l, cp * 256, 256))
                out.append(blk(d["w_in"], l, 4 * D + cp * 256, 256))
                out.append(blk(d["w_in"], l, 5 * D + cp * 256, 256))
            for cp in range(4):
                out.append(blk(d["w_mix_out"], l, cp * 256, 256))
            for cp in range(4):
                out.append(blk(d["w_q"], l, cp * 256, 256))
            for cp in range(4):
                out.append(blk(d["w_o"], l, cp * 256, 256))
            for jp in range(11):
                out.append(blk(d["w_ffn_in"], l, jp * 256, 256))
                out.append(blk(d["w_ffn_in"], l, DFF + jp * 256, 256))
            for c in range(8):
                for hf in range(2):
                    out.append((d["w_ffn_out"][l, hf * 1408:(hf + 1) * 1408, c * 128:(c + 1) * 128]
                                .rearrange("(c p) n -> p c n", p=128), 11, 128))
            return out

        for l in range(DEPTH):
            for b in range(8):
                plan.append(blk(d["w_kv"], l, b * 256, 256) + (None, True, False))
        for t in range(NPT + 1):
            i = 0
            for l in range(DEPTH):
                for b_ in layer_blocks(l):
                    plan.append(b_ + (i, t == 0, t == NPT))
                    i += 1
        self.n_scr = i
        self.wscr = self.nc.dram_tensor("wscr", [self.n_scr, 128, WSLOT], BF16, kind="Internal").ap()
        self.wplan = plan

    def _w_prefetch(self):
        while self.w_next_load < len(self.wplan) and self.w_next_load < self.w_released + NW:
            j = self.w_next_load
            src, K, N, scr, first, samp = self.wplan[j]
            slot = j % NW
            if first:
                dst = self.wring[:, slot, 0:K * N].rearrange("p (c n) -> p c n", c=K)
                self.S.add("pool", lambda e, dst=dst, src=src: e.dma_start(out=dst, in_=src),
                           writes=["w%d" % slot], dma=True)
                if scr is not None:
                    self.S.add("sp", lambda e, slot=slot, scr=scr, n=K * N: e.dma_start(
                        out=self.wscr[scr, :, 0:n], in_=self.wring[:, slot, 0:n]),
                        reads=["w%d" % slot], writes=["scr%d" % scr], dma=True)
            else:
                q = "pool" if (samp and j % 2 == 1) else "sp"
                self.S.add(q, lambda e, slot=slot, scr=scr, n=K * N: e.dma_start(
                    out=self.wring[:, slot, 0:n], in_=self.wscr[scr, :, 0:n]),
                    reads=["scr%d" % scr], writes=["w%d" % slot], dma=True)
            self.w_next_load += 1

    def wget(self):
        j = self.w_cons
        self.w_cons += 1
        assert j < self.w_released + NW, "weight ring too small"
        self._w_prefetch()
        assert self.w_next_load > j
        src, K, N, scr, first, samp = self.wplan[j]
        slot = j % NW
        ap = self.wring[:, slot, 0:K * N].rearrange("p (c n) -> p c n", c=K)
        return ap, "w%d" % slot

    def wrel(self, n=1):
        self.w_released += n
        self._w_prefetch()

    def load_tm_to_fm(self, src_rows, R, dst_fn, dst_tokens, evac="act", scale_fn=None):
        S = self.S
        st, stt = self.stg()
        S.add("sp", lambda e: e.dma_start(out=st[0:R, 0:D], in_=src_rows), writes=[stt], dma=True)
        self.tm_to_fm(st, stt, R, dst_fn, dst_tokens, evac, scale_fn)

    def tm_to_fm(self, st, stt, R, dst_fn, dst_tokens, evac="act", scale_fn=None):
        S = self.S
        stts = list(stt) if isinstance(stt, (list, tuple)) else [stt]
        for half in range(2):
            bk, bt = self.bank()

            def tr(e, half=half, bk=bk):
                ins = None
                for j in range(4):
                    c = half * 4 + j
                    ins = e.transpose(bk[:, j * R:(j + 1) * R], st[0:R, c * 128:(c + 1) * 128], self.ident[0:R, 0:R])
                return ins
            S.add("pe", tr, reads=stts + ["ident"], writes=[bt])
            if scale_fn is None:
                dst = dst_fn(half * 4, 4)
                src = bk[:, 0:4 * R].rearrange("p (j r) -> p j r", j=4)
                if evac == "act":
                    S.add("act", lambda e, dst=dst, src=src: e.copy(out=dst, in_=src), reads=[bt], writes=dst_tokens)
                else:
                    S.add("dve", lambda e, dst=dst, src=src: e.tensor_copy(out=dst, in_=src), reads=[bt], writes=dst_tokens)
            else:
                for j in range(4):
                    c = half * 4 + j
                    dst = dst_fn(c, 1)
                    S.add("act", lambda e, dst=dst, j=j, bk=bk, c=c: e.activation(
                        out=dst, in_=bk[:, j * R:(j + 1) * R].rearrange("p (j r) -> p j r", j=1), func=AF.Copy,
                        scale=scale_fn(c)), reads=[bt, "vec"], writes=dst_tokens)

    def store_fm_to_tm(self, src_fn, src_tokens, R, dst_rows):
        S = self.S
        st, stt = self.stg()
        for half in range(2):
            bk, bt = self.bank()

            def tr(e, half=half, bk=bk):
                ins = None
                for j in range(4):
                    c = half * 4 + j
                    ins = e.transpose(bk[0:R, j * 128:(j + 1) * 128], src_fn(c), self.ident)
                return ins
            S.add("pe", tr, reads=list(src_tokens) + ["ident"], writes=[bt])
            S.add("act", lambda e, half=half, bk=bk: e.copy(out=st[0:R, half * 512:(half + 1) * 512], in_=bk[0:R, :]),
                  reads=[bt], writes=[stt])
        S.add("sp", lambda e: e.dma_start(out=dst_rows, in_=st[0:R, 0:D]), reads=[stt], dma=True)

    def prologue(self, tile0):
        S = self.S
        d = self.d
        nc = self.nc
        S.add("sp", lambda e: e.dma_start(out=self.ident, in_=d["ident"]), writes=["ident"], dma=True)
        S.add("dve", lambda e: e.memset(self.ones, 1.0), writes=["ones"])
        S.add("dve", lambda e: e.memset(self.cst[:, 0:1], EPS), writes=["cst"])
        S.add("dve", lambda e: e.memset(self.cst[:, 1:2], 1.0), writes=["cst"])
        S.add("dve", lambda e: e.memset(self.cst[:, 2:3], 0.0), writes=["cst"])
        S.add("dve", lambda e: e.memset(self.cst[:, 3:4], -0.5), writes=["cst"])
        S.add("dve", lambda e: e.memset(self.cst[:, 4:5], 0.5), writes=["cst"])
        S.add("dve", lambda e: e.memset(self.cst[:, 5:6], -1.0), writes=["cst"])
        S.add("pool", lambda e: e.memset(self.bd, 0.0), writes=["bd"])
        st, stt = self.stg()
        for name, (r0, n) in VEC_ROWS.items():
            S.add("sp", lambda e, name=name, r0=r0, n=n: e.dma_start(out=st[r0:r0 + n, 0:D], in_=d[name]),
                  writes=[stt], dma=True)
        self.tm_to_fm(st, stt, NVEC, lambda c0, n: self.vec[:, c0:c0 + n, :], ["vec"], evac="dve")
        for l in range(DEPTH):
            for g, wname in enumerate(("w_gate_a", "w_gate_x")):
                for hh in range(2):
                    src = d[wname][l].rearrange("(c h) k j -> h k c j", h=2)[hh]
                    dst = self.bd[hh * 64:(hh + 1) * 64, l * 2 + g, :, hh * 64:(hh + 1) * 64]
                    S.add("pool", lambda e, dst=dst, src=src: e.dma_start(out=dst, in_=src),
                          reads=[], writes=["bd"], dma=True)
        for l in range(DEPTH):
            lam = self.vec[:, :, VEC_ROWS["lam"][0] + l]
            bga = self.vec[:, :, VEC_ROWS["bga"][0] + l]
            bgx = self.vec[:, :, VEC_ROWS["bgx"][0] + l]
            t_abs = self.dtmp[:, :, 0]
            t_e = self.dtmp[:, :, 1]
            t_l = self.dtmp[:, :, 2]
            t_r = self.dtmp[:, :, 3]
            S.add("dve", lambda e, bga=bga, l=l: e.tensor_scalar_mul(out=self.der[:, :, 4 * l + 0], in0=bga, scalar1=-1.0),
                  reads=["vec"], writes=["der"])
            S.add("dve", lambda e, bgx=bgx, l=l: e.tensor_scalar_mul(out=self.der[:, :, 4 * l + 1], in0=bgx, scalar1=-1.0),
                  reads=["vec"], writes=["der"])
            S.add("act", lambda e, lam=lam: e.activation(out=t_abs, in_=lam, func=AF.Abs), reads=["vec"], writes=["dt0"])
            S.add("act", lambda e: e.activation(out=t_e, in_=t_abs, func=AF.Exp, scale=-1.0), reads=["dt0"], writes=["dt1"])
            S.add("act", lambda e: e.activation(out=t_l, in_=t_e, func=AF.Ln, bias=self.cst[:, 1:2]),
                  reads=["dt1", "cst"], writes=["dt2"])
            S.add("dve", lambda e, lam=lam: e.tensor_scalar(out=t_r, in0=lam, scalar1=-1.0, scalar2=0.0,
                                                            op0=ALU.mult, op1=ALU.max), reads=["vec"], writes=["dt3"])
            S.add("dve", lambda e: e.tensor_tensor(out=t_r, in0=t_r, in1=t_l, op=ALU.add), reads=["dt3", "dt2"], writes=["dt3"])
            S.add("dve", lambda e, l=l: e.tensor_scalar_mul(out=self.der[:, :, 4 * l + 2], in0=t_r, scalar1=-8.0),
                  reads=["dt3"], writes=["der"])
            S.add("dve", lambda e, l=l: e.tensor_scalar_mul(out=self.der[:, :, 4 * l + 3], in0=t_r, scalar1=-16.0),
                  reads=["dt3"], writes=["der"])
        S.add("pool", lambda e: e.memset(self.carA, 0.0), writes=["carA%d" % c_ for c_ in range(NCH)])
        S.add("pool", lambda e: e.memset(self.carB, 0.0), writes=["carB%d" % c_ for c_ in range(NCH)])
        S.add("pool", lambda e: e.memset(self.carH, 0.0), writes=["carH%d" % c_ for c_ in range(NCH)])
        self.load_x_direct(tile0)
        tile0["staged"] = []
        self.mem_kv()

    def mem_kv(self):
        S = self.S
        d = self.d
        memt = self.stage.rearrange("p a f -> p (a f)")[:, 0:2 * D].rearrange("p (a f) -> p a f", a=2)
        S.add("sp", lambda e: e.dma_start(out=memt, in_=d["memp"].rearrange("(a p) f -> p a f", p=128)),
              writes=["stg0"], dma=True)
        for a in range(2):
            for hf in range(2):
                t, tt = self.tmp()
                S.add("act", lambda e, a=a, hf=hf, t=t: e.activation(
                    out=t[:, 0:512], in_=memt[:, a, hf * 512:(hf + 1) * 512], func=AF.Square),
                    reads=["stg0"], writes=[tt])
                S.add("dve", lambda e, a=a, hf=hf, t=t: e.reduce_sum(
                    out=self.dtmp[:, 0, 4 + 2 * a + hf:5 + 2 * a + hf], in_=t[:, 0:512], axis=mybir.AxisListType.X),
                    reads=[tt], writes=["ssq%d%d" % (a, hf)])
        rs = self.dtmp[:, 1, 0:2]
        S.add("dve", lambda e: e.tensor_tensor(out=self.dtmp[:, 1, 2:4].rearrange("p (a o) -> p a o", o=1),
                                               in0=self.dtmp[:, 0, 4:8].rearrange("p (a h) -> p a h", h=2)[:, :, 0:1],
                                               in1=self.dtmp[:, 0, 4:8].rearrange("p (a h) -> p a h", h=2)[:, :, 1:2],
                                               op=ALU.add),
              reads=["ssq00", "ssq01", "ssq10", "ssq11"], writes=["ssum"])
        S.add("act", lambda e: e.activation(out=self.dtmp[:, 1, 4:6], in_=self.dtmp[:, 1, 2:4], func=AF.Sqrt,
                                            scale=1.0 / D, bias=self.cst[:, 0:1]), reads=["ssum", "cst"], writes=["srt"])
        S.add("dve", lambda e: e.reciprocal(out=rs, in_=self.dtmp[:, 1, 4:6]), reads=["srt"], writes=["mrs"])
        for a in range(2):
            S.add("dve", lambda e, a=a: e.tensor_scalar(out=memt[:, a, :], in0=memt[:, a, :], scalar1=rs[:, a:a + 1],
                                                        scalar2=None, op0=ALU.mult), reads=["mrs", "stg0"], writes=["stg0"])
        mT0 = self.m.rearrange("p c t -> p (c t)")[:, 0:NCH * NMEM].rearrange("p (c t) -> p c t", c=NCH)
        for a in range(2):
            for half in range(2):
                bk, bt = self.bank()

                def tr(e, a=a, half=half, bk=bk):
                    ins = None
                    for j in range(4):
                        c = half * 4 + j
                        ins = e.transpose(bk[:, j * 128:(j + 1) * 128], memt[:, a, c * 128:(c + 1) * 128], self.ident)
                    return ins
                S.add("pe", tr, reads=["stg0", "ident"], writes=[bt])
                S.add("act", lambda e, a=a, half=half, bk=bk: e.copy(
                    out=mT0[:, half * 4:half * 4 + 4, a * 128:(a + 1) * 128],
                    in_=bk.rearrange("p (j r) -> p j r", j=4)), reads=[bt], writes=["m%d" % c_ for c_ in range(NCH)])
        mTl = self.xn.rearrange("p c t -> p (c t)")[:, 0:NCH * NMEM].rearrange("p (c t) -> p c t", c=NCH)
        for l in range(DEPTH):
            for c in range(NCH):
                S.add("dve", lambda e, c=c, l=l: e.tensor_scalar(out=mTl[:, c, :], in0=mT0[:, c, :],
                                                                 scalar1=self.vcol("gains", 7 * l + 6, c), scalar2=None,
                                                                 op0=ALU.mult), reads=["m%d" % c_ for c_ in range(NCH)] + ["vec"], writes=["xn%d" % c_ for c_ in range(NCH)])
            for b in range(8):
                w, wt = self.wget()
                for a in range(2):
                    bk, bt = self.bank()

                    def mm(e, a=a, bk=bk, w=w):
                        ins = None
                        for k in range(NCH):
                            ins = e.matmul(bk[:, 0:256], mTl[:, k, a * 128:(a + 1) * 128], w[:, k, :],
                                           start=(k == 0), stop=(k == NCH - 1))
                        return ins
                    S.add("pe", mm, reads=["xn%d" % c_ for c_ in range(NCH)] + [wt], writes=[bt])
                    t, tt = self.tmp()
                    S.add("act", lambda e, t=t, bk=bk: e.copy(out=t[:, 0:256], in_=bk[:, 0:256]), reads=[bt], writes=[tt])
                    if b < 4:
                        dst = d["pk"][l, a * 128:(a + 1) * 128, b * 256:(b + 1) * 256]
                    else:
                        dst = d["pv"][l, a * 128:(a + 1) * 128, (b - 4) * 256:(b - 3) * 256]
                        S.add("dve", lambda e, t=t, a=a, b=b, l=l: e.tensor_copy(
                            out=self.V[:, l, a, (b - 4) * 256:(b - 3) * 256], in_=t[:, 0:256]), reads=[tt], writes=["V%d" % l])
                    S.add("sp", lambda e, dst=dst, t=t: e.dma_start(out=dst, in_=t[:, 0:256]), reads=[tt], dma=True)
                if b < 4:
                    for j in range(2):
                        bk, bt = self.bank()

                        def mm2(e, j=j, bk=bk, w=w):
                            ins = None
                            for k in range(NCH):
                                ins = e.matmul(bk[:, 0:256], w[:, k, j * 128:(j + 1) * 128], mTl[:, k, :],
                                               start=(k == 0), stop=(k == NCH - 1))
                            return ins
                        S.add("pe", mm2, reads=["xn%d" % c_ for c_ in range(NCH)] + [wt], writes=[bt])
                        S.add("act", lambda e, j=j, b=b, l=l, bk=bk: e.copy(out=self.KT[:, l, 2 * b + j, :], in_=bk[:, 0:256]),
                              reads=[bt], writes=["KT%d" % l])
                self.wrel()

    def rstd_from(self, srcs, T, scale=1.0):
        S = self.S
        bk, bt = self.bank()
        for c, (src, stoks) in enumerate(srcs):
            sq, sqt = self.tmpb()
            S.add("act", lambda e, sq=sq, src=src: e.activation(out=sq[:, 0:T], in_=src, func=AF.Square, scale=scale),
                  reads=stoks, writes=[sqt])
            S.add("pe", lambda e, sq=sq, c=c, bk=bk: e.matmul(bk[:, 0:T], self.ones, sq[:, 0:T], start=(c == 0),
                                                             stop=(c == NCH - 1)), reads=[sqt, "ones"], writes=[bt])
        ms, mst = self.tmp()
        S.add("act", lambda e: e.activation(out=ms[:, 0:T], in_=bk[:, 0:T], func=AF.Ln, scale=1.0 / D,
                                            bias=self.cst[:, 0:1]), reads=[bt, "cst"], writes=[mst])
        rs, rst = self.tmp()
        S.add("act", lambda e: e.activation(out=rs[:, 0:T], in_=ms[:, 0:T], func=AF.Exp, scale=-0.5),
              reads=[mst], writes=[rst])
        return rs, rst

    def norm_begin(self, T):
        bk, bt = self.bank(pin=True)
        return dict(bk=bk, bt=bt, T=T, n=0)

    def norm_add(self, acc, src, stoks, lag=True):
        S = self.S
        T, bk, bt, c = acc["T"], acc["bk"], acc["bt"], acc["n"]
        acc["n"] = c + 1
        sq, sqt = self.tmpb()
        S.add("act", lambda e: e.activation(out=sq[:, 0:T], in_=src, func=AF.Square), reads=stoks, writes=[sqt])

        def emit_pe():
            S.add("pe", lambda e: e.matmul(bk[:, 0:T], self.ones, sq[:, 0:T], start=(c == 0), stop=(c == NCH - 1)),
                  reads=[sqt, "ones"], writes=[bt])
        prev = acc.get("pend")
        if prev is not None:
            prev()
        if lag:
            acc["pend"] = emit_pe
        else:
            acc["pend"] = None
            emit_pe()

    def norm_finish(self, acc, pin=False):
        S = self.S
        T, bk, bt = acc["T"], acc["bk"], acc["bt"]
        assert acc["n"] == NCH
        if acc.get("pend") is not None:
            acc["pend"]()
            acc["pend"] = None
        ms, mst = self.tmp()
        S.add("act", lambda e: e.activation(out=ms[:, 0:T], in_=bk[:, 0:T], func=AF.Ln, scale=1.0 / D,
                                            bias=self.cst[:, 0:1]), reads=[bt, "cst"], writes=[mst])
        rs, rst = self.tmp(pin=pin)
        S.add("act", lambda e: e.activation(out=rs[:, 0:T], in_=ms[:, 0:T], func=AF.Exp, scale=-0.5),
              reads=[mst], writes=[rst])
        self.unpin(bt)
        return rs, rst

    def pre_norm(self, T, gidx):
        S = self.S
        rs, rst = self.rstd_from([(self.x[:, c, 0:T], ["x%d" % c]) for c in range(NCH)], T)
        for c in range(NCH):
            S.add("dve", lambda e, c=c: e.scalar_tensor_tensor(out=self.xn[:, c, 0:T], in0=self.x[:, c, 0:T],
                                                               scalar=self.vcol("gains", gidx, c), in1=rs[:, 0:T],
                                                               op0=ALU.mult, op1=ALU.mult),
                  reads=["x%d" % c, rst, "vec"], writes=["xn%d" % c])

    def post_norm_residual(self, T, gidx, acc):
        S = self.S
        rs, rst = self.norm_finish(acc)
        for c in range(NCH):
            t, tt = self.tmp()
            S.add("dve", lambda e, c=c, t=t: e.scalar_tensor_tensor(out=t[:, 0:T], in0=self.m[:, c, 0:T],
                                                                    scalar=self.vcol("gains", gidx, c), in1=rs[:, 0:T],
                                                                    op0=ALU.mult, op1=ALU.mult),
                  reads=["m%d" % c, rst, "vec"], writes=[tt])
            S.add(self.aux if c % 4 == 1 else "dve", lambda e, c=c, t=t: e.tensor_tensor(
                out=self.x[:, c, 0:T], in0=self.x[:, c, 0:T], in1=t[:, 0:T], op=ALU.add),
                reads=[tt, "x%d" % c], writes=["x%d" % c])

    def proj_group(self, w, wt, col0, rhs_fn, rhs_tokens, T, nk=NCH):
        bk, bt = self.bank()

        def mm(e):
            ins = None
            for k in range(nk):
                ins = e.matmul(bk[:, 0:T], w[:, k, col0:col0 + 128], rhs_fn(k), start=(k == 0), stop=(k == nk - 1))
            return ins
        self.S.add("pe", mm, reads=list(rhs_tokens) + [wt], writes=[bt])
        return bk, bt

    def load_x_dma(self, tile, use_R=False):
        S = self.S
        T = tile["T"]
        staged = []
        Rf = self.R.bitcast(F32)
        for tb in range(T // 128):
            src = tile["xsrc"][tb * 128:(tb + 1) * 128, :]
            assert use_R
            st = Rf[:, 4 * tb:4 * tb + 4, :].rearrange("p a f -> p (a f)")
            toks = ["R%d" % u for u in range(4 * tb, 4 * tb + 4)]
            S.add("sp", lambda e, st=st, src=src: e.dma_start(out=st[:, 0:D], in_=src), writes=toks, dma=True)
            staged.append((st, toks))
        return staged

    def load_x_direct(self, tile):
        T = tile["T"]
        for tb in range(T // 128):
            src = tile["xsrc"][tb * 128:(tb + 1) * 128, :]
            self.load_tm_to_fm(src, 128, lambda c0, n, tb=tb: self.x[:, c0:c0 + n, tb * 128:(tb + 1) * 128],
                               ["x%d" % c for c in range(NCH)], evac="act" if tb % 2 == 0 else "dve")

    def load_x_tr(self, tile, staged):
        for tb, (st, toks) in enumerate(staged):
            self.tm_to_fm(st, toks, 128, lambda c0, n, tb=tb: self.x[:, c0:c0 + n, tb * 128:(tb + 1) * 128],
                          ["x%d" % c for c in range(NCH)], evac="act" if tb % 2 == 0 else "dve")

    def store_y(self, tile):
        T = tile["T"]
        for tb in range(T // 128):
            self.store_fm_to_tm(lambda c, tb=tb: self.x[:, c, tb * 128:(tb + 1) * 128], ["x%d" % c for c in range(NCH)],
                                128, tile["ydst"][tb * 128:(tb + 1) * 128, :])

    def mix(self, tile, l):
        S = self.S
        T, nseq, L = tile["T"], tile["nseq"], tile["L"]
        sample = tile["kind"] == "s"
        first = tile.get("first", False)
        xn_toks = ["xn%d" % c for c in range(NCH)]
        rhs_xn = lambda k: self.xn[:, k, 0:T]
        self.pre_norm(T, 7 * l + 0)
        if sample:
            self.load_tm_to_fm(self.d["sta"][l], NSEQ_S * 2, lambda c0, n: self.stA[:, c0:c0 + n, :], ["stA"], evac="dve")
            self.load_tm_to_fm(self.d["stb"][l], NSEQ_S * 3, lambda c0, n: self.stB[:, c0:c0 + n, :], ["stB"], evac="dve")
            self.load_tm_to_fm(self.d["sth"][l], NSEQ_S, lambda c0, n: self.stH[:, c0:c0 + n, :], ["stH"], evac="dve")
        WA, WB = 2, 3
        aux = self.aux
        ctx = {}

        def stage1(c, col, wb_, wbt, wc_, wct, wh_, wht, wu_, wut):
            bu, but = self.proj_group(wu_, wut, col, rhs_xn, xn_toks, T)
            bhc, bhct = self.proj_group(wc_, wct, col, rhs_xn, xn_toks, T)
            bhh, bhht = self.proj_group(wh_, wht, col, rhs_xn, xn_toks, T)
            bhb, bhbt = self.proj_group(wb_, wbt, col, rhs_xn, xn_toks, T)
            U, Ut = self.ktmp("U")
            U3 = U[:, 0:nseq * (WB + L)].rearrange("p (s w) -> p s w", s=nseq)
            if sample:
                S.add(aux, lambda e: e.tensor_copy(out=U3[:, :, 0:WB], in_=self.stB[:, c, :].rearrange("p (s k) -> p s k", k=WB)),
                      reads=["stB"], writes=[Ut])
            else:
                S.add(aux, lambda e: e.tensor_copy(out=U3[:, 0, 0:WB], in_=self.carB[:, l, c, :]), reads=["carB%d" % c], writes=[Ut])
            bu3 = bu[:, 0:T].rearrange("p (s t) -> p s t", s=nseq)
            S.add("act", lambda e: e.copy(out=U3[:, :, WB:WB + L], in_=bu3), reads=[but, Ut], writes=[Ut])
            uc, uct = self.ktmp("uc")
            uc3 = uc[:, 0:T].rearrange("p (s t) -> p s t", s=nseq)
            S.add("act", lambda e: e.activation(out=uc3, in_=bu3, func=AF.Identity, scale=self.vcol("cbw", 4 * l + 3, c),
                                                bias=self.vcol("cbb", l, c)), reads=[but, "vec"], writes=[uct])
            if sample:
                S.add(aux, lambda e: e.tensor_copy(out=self.oB[:, c, :].rearrange("p (s k) -> p s k", k=WB), in_=U3[:, :, L:L + WB]),
                      reads=[Ut], writes=["oB"])
            else:
                S.add(aux, lambda e: e.tensor_copy(out=self.carB[:, l, c, :], in_=U3[:, 0, L:L + WB]), reads=[Ut], writes=["carB%d" % c])
            hcs, hcst = self.ktmp("hcs")
            S.add("act", lambda e: e.copy(out=hcs[:, 0:T], in_=bhc[:, 0:T]), reads=[bhct], writes=[hcst])
            G, Gt = self.ktmp("G")
            G3 = G[:, 0:nseq * (WA + L)].rearrange("p (s w) -> p s w", s=nseq)
            if sample:
                S.add(aux, lambda e: e.tensor_copy(out=G3[:, :, 0:WA], in_=self.stA[:, c, :].rearrange("p (s k) -> p s k", k=WA)),
                      reads=["stA"], writes=[Gt])
            else:
                S.add(aux, lambda e: e.tensor_copy(out=G3[:, 0, 0:WA], in_=self.carA[:, l, c, :]), reads=["carA%d" % c], writes=[Gt])
            S.add("dve", lambda e: e.tensor_tensor(out=G3[:, :, WA:WA + L], in0=bhh[:, 0:T].rearrange("p (s t) -> p s t", s=nseq),
                                                   in1=hcs[:, 0:T].rearrange("p (s t) -> p s t", s=nseq), op=ALU.mult),
                  reads=[bhht, hcst, Gt], writes=[Gt])
            if sample:
                S.add(aux, lambda e: e.tensor_copy(out=self.oA[:, c, :].rearrange("p (s k) -> p s k", k=WA), in_=G3[:, :, L:L + WA]),
                      reads=[Gt], writes=["oA"])
            else:
                S.add(aux, lambda e: e.tensor_copy(out=self.carA[:, l, c, :], in_=G3[:, 0, L:L + WA]), reads=[Gt], writes=["carA%d" % c])
            ya, yat = self.ktmp("ya")
            ya3 = ya[:, 0:T].rearrange("p (s t) -> p s t", s=nseq)
            S.add("act", lambda e: e.activation(out=ya3, in_=G3[:, :, 2:2 + L], func=AF.Copy, scale=self.vcol("caw", 3 * l + 2, c)),
                  reads=[Gt, "vec"], writes=[yat])
            for k in (1, 0):
                S.add("dve", lambda e, k=k: e.scalar_tensor_tensor(out=ya3, in0=G3[:, :, k:k + L], scalar=self.vcol("caw", 3 * l + k, c),
                                                                   in1=ya3, op0=ALU.mult, op1=ALU.add), reads=[Gt, yat, "vec"], writes=[yat])
            S.add("dve", lambda e: e.tensor_tensor(out=self.R[:, c, 0:T], in0=bhb[:, 0:T], in1=ya[:, 0:T], op=ALU.mult),
                  reads=[bhbt, yat], writes=["R%d" % c])
            for k in (2, 1, 0):
                S.add("dve", lambda e, k=k: e.scalar_tensor_tensor(out=uc3, in0=U3[:, :, k:k + L], scalar=self.vcol("cbw", 4 * l + k, c),
                                                                   in1=uc3, op0=ALU.mult, op1=ALU.add), reads=[Ut, uct, "vec"], writes=[uct])
            ucb, ucbt = self.tmpb()
            S.add("dve", lambda e: e.tensor_copy(out=ucb[:, 0:T], in_=uc[:, 0:T]), reads=[uct], writes=[ucbt])
            ctx[c] = dict(uc=uc, uct=uct, ucb=ucb, ucbt=ucbt)

        def stage2(c):
            uc, uct, ucb, ucbt = ctx[c]["uc"], ctx[c]["uct"], ctx[c]["ucb"], ctx[c]["ucbt"]
            bga_, bgat = self.bank()
            S.add("pe", lambda e: e.matmul(bga_[:, 0:T], self.bd[:, l * 2 + 0, c, :], ucb[:, 0:T], start=True, stop=True),
                  reads=[ucbt, "bd"], writes=[bgat])
            bgx_, bgxt = self.bank()
            S.add("pe", lambda e: e.matmul(bgx_[:, 0:T], self.bd[:, l * 2 + 1, c, :], ucb[:, 0:T], start=True, stop=True),
                  reads=[ucbt, "bd"], writes=[bgxt])

            def sigm(dst, bk, bias_ap, rd, wr):
                S.add("act", lambda e: e.activation(out=dst[:, 0:T], in_=bk[:, 0:T], func=AF.Exp, scale=-1.0, bias=bias_ap),
                      reads=rd + ["der"], writes=[wr])
                S.add("act", lambda e: e.activation(out=dst[:, 0:T], in_=dst[:, 0:T], func=AF.Ln, bias=self.cst[:, 1:2]),
                      reads=[wr, "cst"], writes=[wr])
                S.add("act", lambda e: e.activation(out=dst[:, 0:T], in_=dst[:, 0:T], func=AF.Exp, scale=-1.0), reads=[wr], writes=[wr])
            tr_, trt = self.ktmp("tr")
            sigm(tr_, bga_, self.der[:, c, 4 * l + 0:4 * l + 1], [bgat], trt)
            ti_, tit = self.ktmp("ti")
            sigm(ti_, bgx_, self.der[:, c, 4 * l + 1:4 * l + 2], [bgxt], tit)
            a_, at = self.ktmp("a")
            S.add("act", lambda e: e.activation(out=a_[:, 0:T], in_=tr_[:, 0:T], func=AF.Exp, scale=self.der[:, c, 4 * l + 2:4 * l + 3]),
                  reads=[trt, "der"], writes=[at])
            a2_, a2t = self.ktmp("a2")
            S.add("act", lambda e: e.activation(out=a2_[:, 0:T], in_=tr_[:, 0:T], func=AF.Exp, scale=self.der[:, c, 4 * l + 3:4 * l + 4]),
                  reads=[trt, "der"], writes=[a2t])
            S.add("act", lambda e: e.activation(out=a2_[:, 0:T], in_=a2_[:, 0:T], func=AF.Ln, scale=-1.0, bias=self.cst[:, 1:2]),
                  reads=[a2t, "cst"], writes=[a2t])
            S.add("act", lambda e: e.activation(out=a2_[:, 0:T], in_=a2_[:, 0:T], func=AF.Exp, scale=0.5), reads=[a2t], writes=[a2t])
            ctx[c].update(ti=ti_, tit=tit, a=a_, at=at, a2=a2_, a2t=a2t)

        def stage3(c):
            k = ctx.pop(c)
            uc, uct, ti_, tit, a_, at, a2_, a2t = k["uc"], k["uct"], k["ti"], k["tit"], k["a"], k["at"], k["a2"], k["a2t"]
            iu, iut = self.ktmp("iu")
            S.add("dve", lambda e: e.tensor_tensor(out=iu[:, 0:T], in0=ti_[:, 0:T], in1=uc[:, 0:T], op=ALU.mult),
                  reads=[tit, uct], writes=[iut])
            S.add("dve", lambda e: e.tensor_tensor(out=iu[:, 0:T], in0=iu[:, 0:T], in1=a2_[:, 0:T], op=ALU.mult),
                  reads=[iut, a2t], writes=[iut])
            hs, hst = self.ktmp("hs")
            if sample:
                a3 = a_[:, 0:T].rearrange("p (s t) -> p s t", s=nseq)
                b3 = iu[:, 0:T].rearrange("p (s t) -> p s t", s=nseq)
                t0, t0t = self.ktmp("t0")
                S.add("dve", lambda e: e.tensor_tensor(out=t0[:, 0:nseq], in0=a3[:, :, 0], in1=self.stH[:, c, :], op=ALU.mult),
                      reads=[at, "stH"], writes=[t0t])
                S.add("dve", lambda e: e.tensor_tensor(out=b3[:, :, 0], in0=b3[:, :, 0], in1=t0[:, 0:nseq], op=ALU.add),
                      reads=[t0t, iut], writes=[iut])
                S.add("dve", lambda e: e.memset(a3[:, :, 0], 0.0), reads=[t0t], writes=[at])
                init, init_toks = 0.0, []
            else:
                init, init_toks = self.carH[:, l, c, :], ["carH%d" % c]
            S.add("dve", lambda e: e.tensor_tensor_scan(out=hs[:, 0:T], data0=a_[:, 0:T], data1=iu[:, 0:T], initial=init,
                                                        op0=ALU.mult, op1=ALU.add), reads=[at, iut] + init_toks, writes=[hst])
            if sample:
                S.add(aux, lambda e: e.tensor_copy(out=self.oH[:, c, :], in_=hs[:, 0:T].rearrange("p (s t) -> p s t", s=nseq)[:, :, L - 1]),
                      reads=[hst], writes=["oH"])
            else:
                S.add(aux, lambda e: e.tensor_copy(out=self.carH[:, l, c, :], in_=hs[:, T - 1:T]), reads=[hst], writes=["carH%d" % c])
            S.add("act", lambda e: e.copy(out=self.R[:, 8 + c, 0:T], in_=hs[:, 0:T]), reads=[hst], writes=["R%d" % (8 + c)])

        for cp in range(4):
            wb_, wbt = self.wget()
            wc_, wct = self.wget()
            wh_, wht = self.wget()
            wu_, wut = self.wget()
            for j in range(2):
                c = cp * 2 + j
                stage1(c, j * 128, wb_, wbt, wc_, wct, wh_, wht, wu_, wut)
                if c >= 1:
                    stage2(c - 1)
                if c >= 2:
                    stage3(c - 2)
            self.wrel(4)
        stage2(NCH - 1)
        stage3(NCH - 2)
        stage3(NCH - 1)
        ba_toks = ["R%d" % c for c in range(NCH)]
        hs_toks = ["R%d" % (8 + c) for c in range(NCH)]
        for cp in range(4):
            wco, wcot = self.wget()
            wro, wrot = self.wget()
            wgc, wgct = self.wget()
            wgr, wgrt = self.wget()
            def sigm0(dst, bk, rd, wr):
                S.add("act", lambda e: e.activation(out=dst[:, 0:T], in_=bk[:, 0:T], func=AF.Exp, scale=-1.0),
                      reads=rd, writes=[wr])
                S.add("act", lambda e: e.activation(out=dst[:, 0:T], in_=dst[:, 0:T], func=AF.Ln, bias=self.cst[:, 1:2]),
                      reads=[wr, "cst"], writes=[wr])
                S.add("act", lambda e: e.activation(out=dst[:, 0:T], in_=dst[:, 0:T], func=AF.Exp, scale=-1.0),
                      reads=[wr], writes=[wr])
            part = {}
            for j in range(2):
                col = j * 128
                bgc, bgct = self.proj_group(wgc, wgct, col, rhs_xn, xn_toks, T)
                bgr, bgrt = self.proj_group(wgr, wgrt, col, rhs_xn, xn_toks, T)
                byc, byct = self.proj_group(wco, wcot, col, lambda k: self.R[:, k, 0:T], ba_toks, T)
                tc_, tct = self.tmp()
                sigm0(tc_, bgc, [bgct], tct)
                tg_, tgt = self.tmp()
                sigm0(tg_, bgr, [bgrt], tgt)
                S.add("dve", lambda e, tc_=tc_, bk=byc: e.tensor_tensor(
                    out=tc_[:, 0:T], in0=bk[:, 0:T], in1=tc_[:, 0:T], op=ALU.mult), reads=[tct, byct], writes=[tct])
                part[j] = (tc_, tct, tg_, tgt)
            for j in range(2):
                c = cp * 2 + j
                tc_, tct, tg_, tgt = part[j]
                byr, byrt = self.proj_group(wro, wrot, j * 128, lambda k: self.R[:, 8 + k, 0:T], hs_toks, T)
                S.add("dve", lambda e, tg_=tg_, bk=byr: e.tensor_tensor(
                    out=tg_[:, 0:T], in0=bk[:, 0:T], in1=tg_[:, 0:T], op=ALU.mult), reads=[tgt, byrt], writes=[tgt])
                S.add("dve", lambda e, tc_=tc_, tg_=tg_, c=c: e.tensor_tensor(
                    out=self.R[:, 16 + c, 0:T], in0=tc_[:, 0:T], in1=tg_[:, 0:T], op=ALU.add),
                    reads=[tct, tgt], writes=["R%d" % (16 + c)])
            self.wrel(4)
        z_toks = ["R%d" % (16 + c) for c in range(NCH)]
        acc = self.norm_begin(T)
        for cp in range(4):
            w, wt = self.wget()
            for j in range(2):
                c = cp * 2 + j
                bk, bt = self.proj_group(w, wt, j * 128, lambda k: self.R[:, 16 + k, 0:T], z_toks, T)
                S.add("act", lambda e, bk=bk, c=c: e.copy(out=self.m[:, c, 0:T], in_=bk[:, 0:T]),
                      reads=[bt], writes=["m%d" % c])
                self.norm_add(acc, bk[:, 0:T], [bt])
            self.wrel()
        self.post_norm_residual(T, 7 * l + 1, acc)
        if sample:
            self.store_fm_to_tm(lambda c: self.oA[:, c, :], ["oA"], NSEQ_S * 2, self.d["sca"][l])
            self.store_fm_to_tm(lambda c: self.oB[:, c, :], ["oB"], NSEQ_S * 3, self.d["scb"][l])
            self.store_fm_to_tm(lambda c: self.oH[:, c, :], ["oH"], NSEQ_S, self.d["sh"][l])
        elif tile.get("last", False):
            self.store_fm_to_tm(lambda c: self.carA[:, l, c, :], ["carA%d" % c_ for c_ in range(NCH)], 2, self.d["pca"][l])
            self.store_fm_to_tm(lambda c: self.carB[:, l, c, :], ["carB%d" % c_ for c_ in range(NCH)], 3, self.d["pcb"][l])
            self.store_fm_to_tm(lambda c: self.carH[:, l, c, :], ["carH%d" % c_ for c_ in range(NCH)], 1, self.d["ph"][l])

    def attn(self, tile, l):
        S = self.S
        T = tile["T"]
        sample = tile["kind"] == "s"
        xn_toks = ["xn%d" % c for c in range(NCH)]
        rhs_xn = lambda k: self.xn[:, k, 0:T]
        acc = self.norm_begin(T)
        for c in range(NCH):
            self.norm_add(acc, self.x[:, c, 0:T], ["x%d" % c])
            S.add("act", lambda e, c=c: e.activation(out=self.xn[:, c, 0:T], in_=self.x[:, c, 0:T], func=AF.Copy,
                                                     scale=self.vcol("gains", 7 * l + 2, c)),
                  reads=["x%d" % c, "vec"], writes=["xn%d" % c])
        rs, rst = self.norm_finish(acc)
        for cp in range(4):
            w, wt = self.wget()
            for j in range(2):
                c = cp * 2 + j
                bk, bt = self.proj_group(w, wt, j * 128, rhs_xn, xn_toks, T)
                S.add("dve", lambda e, bk=bk, c=c: e.scalar_tensor_tensor(
                    out=self.R[:, c, 0:T], in0=bk[:, 0:T], scalar=1.0 / 16.0, in1=rs[:, 0:T], op0=ALU.mult, op1=ALU.mult),
                    reads=[bt, rst], writes=["R%d" % c])
            self.wrel()
        if not sample:
            def head_a(h):
                pts = []
                for mc in range(2):
                    bk, bt = self.bank()

                    def mm(e, bk=bk, mc=mc):
                        ins = None
                        for dc in range(2):
                            ins = e.matmul(bk[:, 0:T], self.KT[:, l, 2 * h + dc, mc * 128:(mc + 1) * 128],
                                           self.R[:, 2 * h + dc, 0:T], start=(dc == 0), stop=(dc == 1))
                        return ins
                    S.add("pe", mm, reads=["KT%d" % l, "R%d" % (2 * h), "R%d" % (2 * h + 1)], writes=[bt])
                    ru = 8 + 2 * h + mc
                    S.add("act", lambda e, bk=bk, ru=ru: e.activation(out=self.R[:, ru, 0:T], in_=bk[:, 0:T], func=AF.Exp),
                          reads=[bt], writes=["R%d" % ru])
                    pts.append(ru)
                return pts

            def head_b(h, pts):
                bs, bst = self.bank()

                def mms(e):
                    ins = None
                    for mc in range(2):
                        ins = e.matmul(bs[:, 0:T], self.ones, self.R[:, pts[mc], 0:T], start=(mc == 0), stop=(mc == 1))
                    return ins
                S.add("pe", mms, reads=["R%d" % p for p in pts] + ["ones"], writes=[bst])
                ri, rit = self.tmp()
                S.add("act", lambda e: e.activation(out=ri[:, 0:T], in_=bs[:, 0:T], func=AF.Ln), reads=[bst], writes=[rit])
                S.add("act", lambda e: e.activation(out=ri[:, 0:T], in_=ri[:, 0:T], func=AF.Exp, scale=-1.0),
                      reads=[rit], writes=[rit])
                for dc in range(2):
                    bo, bot = self.bank()

                    def mmo(e, bo=bo, dc=dc):
                        ins = None
                        for mc in range(2):
                            ins = e.matmul(bo[:, 0:T], self.V[:, l, mc, h * 256 + dc * 128:h * 256 + (dc + 1) * 128],
                                           self.R[:, pts[mc], 0:T], start=(mc == 0), stop=(mc == 1))
                        return ins
                    S.add("pe", mmo, reads=["R%d" % p for p in pts] + ["V%d" % l], writes=[bot])
                    ou = 16 + 2 * h + dc
                    S.add("dve", lambda e, bo=bo, ou=ou: e.tensor_tensor(out=self.R[:, ou, 0:T], in0=bo[:, 0:T],
                                                                         in1=ri[:, 0:T], op=ALU.mult),
                          reads=[bot, rit], writes=["R%d" % ou])
            prev = None
            for h in range(4):
                pts = head_a(h)
                if prev is not None:
                    head_b(*prev)
                prev = (h, pts)
            head_b(*prev)
        else:
            self.attn_sample(l)
        o_toks = ["R%d" % (16 + c) for c in range(NCH)]
        acc = self.norm_begin(T)
        for cp in range(4):
            w, wt = self.wget()
            for j in range(2):
                c = cp * 2 + j
                bk, bt = self.proj_group(w, wt, j * 128, lambda k: self.R[:, 16 + k, 0:T], o_toks, T)
                S.add("act", lambda e, bk=bk, c=c: e.copy(out=self.m[:, c, 0:T], in_=bk[:, 0:T]), reads=[bt], writes=["m%d" % c])
                self.norm_add(acc, bk[:, 0:T], [bt])
            self.wrel()
        self.post_norm_residual(T, 7 * l + 3, acc)

    def attn_sample(self, l):
        S = self.S
        d = self.d
        T = TS
        for grp in range(2):
            bS, bSt = self.bank(pin=True)
            bO, bOt = self.bank(pin=True)
            ptu = 8 + grp
            PT = self.R[:, ptu, :]
            kvs = []
            for sl in range(8):
                s = grp * 8 + sl
                i = self.kv_rr
                self.kv_rr = (i + 1) % 2
                kt = self.kts[:, i]
                v = self.vs[:, i]
                st, stt = self.stg()
                kst = st.rearrange("p (a f) -> p a f", a=2)
                S.add("sp", lambda e, kst=kst, s=s: e.dma_start(out=kst, in_=d["ck"][l, s].rearrange("(a p) f -> p a f", p=128)),
                      writes=[stt], dma=True)
                S.add("pool", lambda e, v=v, s=s: e.dma_start(out=v, in_=d["cv"][l, s].rearrange("(a p) f -> p a f", p=128)),
                      writes=["V%d" % i], dma=True)
                for dp in range(4):
                    bk, bt = self.bank()

                    def tr(e, bk=bk, dp=dp, kst=kst):
                        ins = None
                        for dj in range(2):
                            dc = dp * 2 + dj
                            for a in range(2):
                                ins = e.transpose(bk[:, dj * 256 + a * 128:dj * 256 + (a + 1) * 128],
                                                  kst[:, a, dc * 128:(dc + 1) * 128], self.ident)
                        return ins
                    S.add("pe", tr, reads=[stt, "ident"], writes=[bt])
                    eng = "act" if dp % 2 == 0 else "dve"
                    if eng == "act":
                        S.add("act", lambda e, bk=bk, dp=dp, kt=kt: e.copy(out=kt[:, 2 * dp:2 * dp + 2, :],
                                                                          in_=bk.rearrange("p (j m) -> p j m", j=2)),
                              reads=[bt], writes=["KT%d" % i])
                    else:
                        S.add("dve", lambda e, bk=bk, dp=dp, kt=kt: e.tensor_copy(out=kt[:, 2 * dp:2 * dp + 2, :],
                                                                                 in_=bk.rearrange("p (j m) -> p j m", j=2)),
                              reads=[bt], writes=["KT%d" % i])
                def mm(e, kt=kt, sl=sl, s=s, bS=bS):
                    ins = None
                    for h in range(4):
                        for mc in range(2):
                            col = mc * 256 + sl * 32 + h * 8
                            for dc in range(2):
                                ins = e.matmul(bS[:, col:col + 8], kt[:, 2 * h + dc, mc * 128:(mc + 1) * 128],
                                               self.R[:, 2 * h + dc, s * 8:(s + 1) * 8], start=(dc == 0), stop=(dc == 1))
                    return ins
                S.add("pe", mm, reads=["KT%d" % i] + ["R%d" % c for c in range(NCH)], writes=[bSt])
                kvs.append((v, "V%d" % i, sl, s))
                S.add("act", lambda e, sl=sl, bS=bS, PT=PT: e.activation(
                    out=PT.rearrange("p (a c) -> p a c", a=2)[:, :, sl * 32:(sl + 1) * 32],
                    in_=bS.rearrange("p (a c) -> p a c", a=2)[:, :, sl * 32:(sl + 1) * 32], func=AF.Exp),
                    reads=[bSt], writes=["R%d" % ptu])
                def mmo(e, v=v, sl=sl, bO=bO, PT=PT):
                    ins = None
                    for h in range(4):
                        for dc in range(2):
                            c = 2 * h + dc
                            for mc in range(2):
                                ins = e.matmul(bO[:, c * 64 + sl * 8:c * 64 + sl * 8 + 8],
                                               v[:, mc, h * 256 + dc * 128:h * 256 + (dc + 1) * 128],
                                               PT[:, mc * 256 + sl * 32 + h * 8:mc * 256 + sl * 32 + h * 8 + 8],
                                               start=(mc == 0), stop=(mc == 1))
                    return ins
                S.add("pe", mmo, reads=["V%d" % i, "R%d" % ptu], writes=[bOt])
            bs, bst = self.bank()

            def mms(e, bs=bs, PT=PT):
                ins = None
                for mc in range(2):
                    ins = e.matmul(bs[:, 0:256], self.ones, PT[:, mc * 256:(mc + 1) * 256], start=(mc == 0), stop=(mc == 1))
                return ins
            S.add("pe", mms, reads=["R%d" % ptu, "ones"], writes=[bst])
            ri, rit = self.tmp()
            S.add("act", lambda e, ri=ri, bs=bs: e.activation(out=ri[:, 0:256], in_=bs[:, 0:256], func=AF.Ln), reads=[bst], writes=[rit])
            S.add("act", lambda e, ri=ri: e.activation(out=ri[:, 0:256], in_=ri[:, 0:256], func=AF.Exp, scale=-1.0),
                  reads=[rit], writes=[rit])
            for h in range(4):
                for dc in range(2):
                    c = 2 * h + dc
                    S.add("dve", lambda e, c=c, h=h, ri=ri, bO=bO, grp=grp: e.tensor_tensor(
                        out=self.R[:, 16 + c, grp * 64:(grp + 1) * 64].rearrange("p (s t) -> p s t", s=8),
                        in0=bO[:, c * 64:(c + 1) * 64].rearrange("p (s t) -> p s t", s=8),
                        in1=ri[:, 0:256].rearrange("p (s h t) -> p s h t", s=8, h=4)[:, :, h, :], op=ALU.mult),
                        reads=[bOt, rit], writes=["R%d" % (16 + c)])
            self.unpin(bSt)
            self.unpin(bOt)

    def ffn(self, tile, l):
        S = self.S
        T = tile["T"]
        xn_toks = ["xn%d" % c for c in range(NCH)]
        rhs_xn = lambda k: self.xn[:, k, 0:T]
        acc0 = self.norm_begin(T)
        for c in range(NCH):
            self.norm_add(acc0, self.x[:, c, 0:T], ["x%d" % c])
            S.add("act", lambda e, c=c: e.activation(out=self.xn[:, c, 0:T], in_=self.x[:, c, 0:T], func=AF.Copy,
                                                     scale=self.vcol("gains", 7 * l + 4, c)),
                  reads=["x%d" % c, "vec"], writes=["xn%d" % c])
        rs, rst = self.norm_finish(acc0, pin=True)
        for jp in range(11):
            wg, wgt = self.wget()
            wu, wut = self.wget()
            for jj in range(2):
                j = jp * 2 + jj
                bg, bgt = self.proj_group(wg, wgt, jj * 128, rhs_xn, xn_toks, T)
                bu, but = self.proj_group(wu, wut, jj * 128, rhs_xn, xn_toks, T)
                sg, sgt = self.tmp()
                S.add("dve", lambda e, sg=sg, bg=bg: e.tensor_tensor(out=sg[:, 0:T], in0=bg[:, 0:T], in1=rs[:, 0:T], op=ALU.mult),
                      reads=[bgt, rst], writes=[sgt])
                S.add("act", lambda e, sg=sg: e.activation(out=sg[:, 0:T], in_=sg[:, 0:T], func=AF.Silu),
                      reads=[sgt], writes=[sgt])
                tu, tut = self.tmp()
                S.add("dve", lambda e, tu=tu, bu=bu: e.tensor_tensor(out=tu[:, 0:T], in0=bu[:, 0:T], in1=rs[:, 0:T], op=ALU.mult),
                      reads=[but, rst], writes=[tut])
                S.add("dve", lambda e, sg=sg, tu=tu, j=j: e.tensor_tensor(out=self.R[:, j, 0:T], in0=tu[:, 0:T], in1=sg[:, 0:T],
                                                                         op=ALU.mult), reads=[tut, sgt], writes=["R%d" % j])
            self.wrel(2)
        self.tmp_unpin(rst)
        h_toks = ["R%d" % j for j in range(NJ)]
        acc = self.norm_begin(T)
        for c in range(NCH):
            w0, wt0 = self.wget()
            w1, wt1 = self.wget()
            bk, bt = self.bank()

            def mm(e, bk=bk, w0=w0, w1=w1):
                ins = None
                for k in range(NJ):
                    w = w0 if k < 11 else w1
                    ins = e.matmul(bk[:, 0:T], w[:, k % 11, :], self.R[:, k, 0:T], start=(k == 0), stop=(k == NJ - 1))
                return ins
            S.add("pe", mm, reads=h_toks + [wt0, wt1], writes=[bt])
            S.add("act", lambda e, bk=bk, c=c: e.copy(out=self.m[:, c, 0:T], in_=bk[:, 0:T]), reads=[bt], writes=["m%d" % c])
            self.norm_add(acc, bk[:, 0:T], [bt])
            self.wrel(2)
        if l == DEPTH - 1 and tile.get("next") is not None:
            tile["next"]["staged"] = self.load_x_dma(tile["next"], use_R=True)
        self.post_norm_residual(T, 7 * l + 5, acc)

    def build(self):
        self.plan_weights()
        tiles = []
        for i in range(NPT):
            tiles.append(dict(kind="p", T=TP, nseq=1, L=TP, first=(i == 0), last=(i == NPT - 1),
                              xsrc=self.d["xp"][i * TP:(i + 1) * TP, :], ydst=self.d["yp"][i * TP:(i + 1) * TP, :]))
        tiles.append(dict(kind="s", T=TS, nseq=NSEQ_S, L=L_S, xsrc=self.d["xs"], ydst=self.d["ys"]))
        for ti in range(len(tiles) - 1):
            tiles[ti]["next"] = tiles[ti + 1]
        self.prologue(tiles[0])
        for ti, tile in enumerate(tiles):
            self.aux = "dve" if (ti == 0 or ti == NPT) else "pool"
            if ti > 0:
                self.load_x_tr(tile, tile["staged"])
            for l in range(DEPTH):
                self.mix(tile, l)
                self.attn(tile, l)
                self.ffn(tile, l)
            self.store_y(tile)
        assert self.w_cons == len(self.wplan), (self.w_cons, len(self.wplan))
        self.S.emit()
        return self.nc


_CACHE = {}


def _get_program():
    if "nc" not in _CACHE:
        _CACHE["nc"] = Builder().build()
    return _CACHE["nc"]


def kernel(x_prompt, x_sample, state_conv_a, state_conv_b, state_rglru, cache_mem_k, cache_mem_v, mem_prompt,
           norm_gains, w_in, conv_a_w, w_conv_out, conv_b_w, conv_b_b, w_gate_a, b_gate_a, w_gate_x, b_gate_x,
           lru_lambda, w_rnn_out, w_mix_out, w_kv_x, w_q_x, w_o_x, w_ffn_in, w_ffn_out):
    f = lambda a: np.ascontiguousarray(np.asarray(a, dtype=np.float32))
    x_prompt, x_sample = f(x_prompt), f(x_sample)
    state_conv_a, state_conv_b, state_rglru = f(state_conv_a), f(state_conv_b), f(state_rglru)
    cache_mem_k, cache_mem_v, mem_prompt = f(cache_mem_k), f(cache_mem_v), f(mem_prompt)
    n = 8
    shared = {
        "gains": f(norm_gains).reshape(14, D), "caw": f(conv_a_w).reshape(6, D), "cbw": f(conv_b_w).reshape(8, D),
        "cbb": f(conv_b_b), "bga": f(b_gate_a), "bgx": f(b_gate_x), "lam": f(lru_lambda),
        "ident": np.eye(128, dtype=np.float32),
        "w_in": f(w_in), "w_conv_out": f(w_conv_out), "w_gate_a": f(w_gate_a), "w_gate_x": f(w_gate_x),
        "w_rnn_out": f(w_rnn_out), "w_mix_out": f(w_mix_out), "w_kv": f(w_kv_x), "w_q": f(w_q_x), "w_o": f(w_o_x),
        "w_ffn_in": f(w_ffn_in), "w_ffn_out": f(w_ffn_out),
    }
    in_maps = []
    for b in range(n):
        sl = slice(b * NSEQ_S, (b + 1) * NSEQ_S)
        m = dict(shared)
        m["xp"] = x_prompt[b]
        m["xs"] = x_sample[sl].reshape(TS, D)
        m["sta"] = np.ascontiguousarray(state_conv_a[:, sl]).reshape(DEPTH, NSEQ_S * 2, D)
        m["stb"] = np.ascontiguousarray(state_conv_b[:, sl]).reshape(DEPTH, NSEQ_S * 3, D)
        m["sth"] = np.ascontiguousarray(state_rglru[:, sl]).reshape(DEPTH, NSEQ_S, D)
        m["ck"] = np.ascontiguousarray(cache_mem_k[:, sl]).reshape(DEPTH, NSEQ_S, NMEM, D)
        m["cv"] = np.ascontiguousarray(cache_mem_v[:, sl]).reshape(DEPTH, NSEQ_S, NMEM, D)
        m["memp"] = mem_prompt[b]
        in_maps.append(m)
    nc = _get_program()
    res = run_bass_kernel_spmd(nc, in_maps, core_ids=list(range(n)))
    r = res.results
    y_prompt = np.stack([r[b]["yp"] for b in range(n)], axis=0)
    y_sample = np.concatenate([r[b]["ys"].reshape(NSEQ_S, L_S, D) for b in range(n)], axis=0)
    p_conv_a = np.stack([r[b]["pca"] for b in range(n)], axis=1)
    p_conv_b = np.stack([r[b]["pcb"] for b in range(n)], axis=1)
    p_rglru = np.stack([r[b]["ph"].reshape(DEPTH, D) for b in range(n)], axis=1)
    p_mem_k = np.stack([r[b]["pk"].reshape(DEPTH, NMEM, 4, 256) for b in range(n)], axis=1)
    p_mem_v = np.stack([r[b]["pv"].reshape(DEPTH, NMEM, 4, 256) for b in range(n)], axis=1)
    s_conv_a = np.concatenate([r[b]["sca"].reshape(DEPTH, NSEQ_S, 2, D) for b in range(n)], axis=1)
    s_conv_b = np.concatenate([r[b]["scb"].reshape(DEPTH, NSEQ_S, 3, D) for b in range(n)], axis=1)
    s_rglru = np.concatenate([r[b]["sh"].reshape(DEPTH, NSEQ_S, D) for b in range(n)], axis=1)
    return (y_prompt.astype(np.float32), y_sample.astype(np.float32), p_conv_a.astype(np.float32),
            p_conv_b.astype(np.float32), p_rglru.astype(np.float32), p_mem_k.astype(np.float32),
            p_mem_v.astype(np.float32), s_conv_a.astype(np.float32), s_conv_b.astype(np.float32),
            s_rglru.astype(np.float32))
```

```python
import contextlib
import numpy as np
import concourse.bass as bass
import concourse.mybir as mybir
from concourse.bass_utils import run_bass_kernel_spmd

F32 = mybir.dt.float32
BF16 = mybir.dt.bfloat16
ALU = mybir.AluOpType
AF = mybir.ActivationFunctionType

ENGS = ("pe", "act", "dve", "pool", "sp")
N_DMA_SEMS = 12

D = 1024
NCH = 8
DFF = 2816
NJ = 22
DEPTH = 2
SEQ = 2048
TP = 512
NPT = SEQ // TP
NSEQ_S = 16
L_S = 8
TS = NSEQ_S * L_S
NMEM = 256
EPS = 1e-6


class Sched:
    def __init__(self, nc):
        self.nc = nc
        self.ops = {e: [] for e in ENGS}
        self.cnt = {e: 0 for e in ENGS}
        self.seen = {e: {} for e in ENGS}
        self.last_w = {}
        self.readers = {}
        self.dma_cnt = [0] * N_DMA_SEMS
        self.dma_rr = {"pool": 0, "sp": 0}

    def _need(self, deps, eng, ev, raw):
        semkey, val, src = ev
        if src == eng:
            if eng == "pe":
                return
        if deps.get(semkey, 0) < val:
            deps[semkey] = val

    def add(self, eng, fn, reads=(), writes=(), dma=False):
        deps = {}
        deng = None if dma else eng
        for t in reads:
            for ev in self.last_w.get(t, ()):
                self._need(deps, deng, ev, True)
        for t in writes:
            for ev in self.last_w.get(t, ()):
                self._need(deps, deng, ev, False)
            for sk, (v, src) in self.readers.get(t, {}).items():
                self._need(deps, deng, (sk, v, src), False)
        if dma:
            half = N_DMA_SEMS // 2
            k = self.dma_rr[eng]
            self.dma_rr[eng] = (k + 1) % half
            i = k + (0 if eng == "pool" else half)
            c = self.dma_cnt[i] + 1
            self.dma_cnt[i] = c
            if c > 1:
                sk = ("dma", i)
                if deps.get(sk, 0) < 16 * (c - 1):
                    deps[sk] = 16 * (c - 1)
            ev = (("dma", i), 16 * c, None)
        else:
            self.cnt[eng] += 1
            ev = (eng, self.cnt[eng], eng)
        waits = []
        seen = self.seen[eng]
        for sk, v in deps.items():
            if seen.get(sk, 0) < v:
                seen[sk] = v
                waits.append((sk, v))
        self.ops[eng].append((fn, waits, ev[0]))
        for t in writes:
            prev = self.last_w.get(t)
            if dma and prev and all(p[2] is None for p in prev):
                self.last_w[t] = (prev + [ev])[-8:]
            else:
                self.last_w[t] = [ev]
            self.readers[t] = {}
        for t in reads:
            if t in writes:
                continue
            r = self.readers.setdefault(t, {})
            old = r.get(ev[0])
            if old is None or old[0] < ev[1]:
                r[ev[0]] = (ev[1], ev[2])
        return ev

    def emit(self):
        nc = self.nc
        waits = []
        for i, c in enumerate(self.dma_cnt):
            if c > 0 and self.seen["sp"].get(("dma", i), 0) < 16 * c:
                waits.append((("dma", i), 16 * c))
        self.ops["sp"].append((None, waits, None))
        with contextlib.ExitStack() as st:
            sems = {}
            for e in ENGS[:4]:
                sems[e] = st.enter_context(nc.semaphore("sem_" + e))
            for i in range(N_DMA_SEMS):
                sems[("dma", i)] = st.enter_context(nc.semaphore("sem_dma%d" % i))
            block = st.enter_context(nc.Block())

            def run(engobj, name):
                for fn, waits, inc in self.ops[name]:
                    for sk, v in waits:
                        engobj.wait_ge(sems[sk], v)
                    if fn is None:
                        continue
                    ins = fn(engobj)
                    if inc is not None:
                        ins.then_inc(sems[inc], 16 if isinstance(inc, tuple) else 1)

            @block.tensor
            def _(e):
                run(e, "pe")

            @block.scalar
            def _(e):
                run(e, "act")

            @block.vector
            def _(e):
                run(e, "dve")

            @block.gpsimd
            def _(e):
                run(e, "pool")

            @block.sync
            def _(e):
                run(e, "sp")


VEC_ROWS = {"gains": (0, 14), "caw": (14, 6), "cbw": (20, 8), "cbb": (28, 2), "bga": (30, 2),
            "bgx": (32, 2), "lam": (34, 2)}
NVEC = 36
NW = 10
SCR_SPLIT = 2
WSLOT = 2048
NTMP = 24
TMPW = 528


class Builder:
    def __init__(self):
        nc = bass.Bass("TRN2", target_bir_lowering=False)
        self.nc = nc
        self.S = Sched(nc)
        S = self.S

        def din(name, shape):
            return nc.dram_tensor(name, list(shape), F32, kind="ExternalInput").ap()

        def dout(name, shape):
            return nc.dram_tensor(name, list(shape), F32, kind="ExternalOutput").ap()

        self.d = {}
        d = self.d
        d["xp"] = din("xp", [SEQ, D])
        d["xs"] = din("xs", [TS, D])
        d["sta"] = din("sta", [DEPTH, NSEQ_S * 2, D])
        d["stb"] = din("stb", [DEPTH, NSEQ_S * 3, D])
        d["sth"] = din("sth", [DEPTH, NSEQ_S, D])
        d["ck"] = din("ck", [DEPTH, NSEQ_S, NMEM, D])
        d["cv"] = din("cv", [DEPTH, NSEQ_S, NMEM, D])
        d["memp"] = din("memp", [NMEM, D])
        d["gains"] = din("gains", [14, D])
        d["caw"] = din("caw", [6, D])
        d["cbw"] = din("cbw", [8, D])
        d["cbb"] = din("cbb", [2, D])
        d["bga"] = din("bga", [2, D])
        d["bgx"] = din("bgx", [2, D])
        d["lam"] = din("lam", [2, D])
        d["ident"] = din("ident", [128, 128])
        d["w_in"] = din("w_in", [DEPTH, D, 6 * D])
        d["w_conv_out"] = din("w_conv_out", [DEPTH, D, D])
        d["w_gate_a"] = din("w_gate_a", [DEPTH, 16, 64, 64])
        d["w_gate_x"] = din("w_gate_x", [DEPTH, 16, 64, 64])
        d["w_rnn_out"] = din("w_rnn_out", [DEPTH, D, D])
        d["w_mix_out"] = din("w_mix_out", [DEPTH, D, D])
        d["w_kv"] = din("w_kv", [DEPTH, D, 2 * D])
        d["w_q"] = din("w_q", [DEPTH, D, D])
        d["w_o"] = din("w_o", [DEPTH, D, D])
        d["w_ffn_in"] = din("w_ffn_in", [DEPTH, D, 2 * DFF])
        d["w_ffn_out"] = din("w_ffn_out", [DEPTH, DFF, D])
        d["yp"] = dout("yp", [SEQ, D])
        d["ys"] = dout("ys", [TS, D])
        d["pca"] = dout("pca", [DEPTH, 2, D])
        d["pcb"] = dout("pcb", [DEPTH, 3, D])
        d["ph"] = dout("ph", [DEPTH, 1, D])
        d["pk"] = dout("pk", [DEPTH, NMEM, D])
        d["pv"] = dout("pv", [DEPTH, NMEM, D])
        d["sca"] = dout("sca", [DEPTH, NSEQ_S * 2, D])
        d["scb"] = dout("scb", [DEPTH, NSEQ_S * 3, D])
        d["sh"] = dout("sh", [DEPTH, NSEQ_S, D])

        def sb(name, shape, dt=F32):
            return nc.alloc_sbuf_tensor("sb_" + name, list(shape), dt).ap()

        self.ps = nc.alloc_psum_tensor("ps", [128, 8, 512], F32).ap()
        self.bank_rr = 0
        self.pinned = set()
        self.ident = sb("ident", [128, 128])
        self.ones = sb("ones", [128, 128], BF16)
        self.cst = sb("cst", [128, 8])
        self.vec = sb("vec", [128, NCH, NVEC])
        self.der = sb("der", [128, NCH, 12])
        self.dtmp = sb("dtmp", [128, NCH, 8])
        self.bd = sb("bd", [128, DEPTH * 2, NCH, 128], BF16)
        self.KT = sb("KT", [128, DEPTH, NCH, NMEM], BF16)
        self.V = sb("V", [128, DEPTH, 2, D], BF16)
        self.x = sb("x", [128, NCH, TP])
        self.xn = sb("xn", [128, NCH, TP], BF16)
        self.R = sb("R", [128, 24, TP], BF16)
        self.m = sb("m", [128, NCH, TP])
        self.tmpf = sb("tmpf", [128, NTMP, TMPW])
        self.tmp_rr = 0
        self.kcnt = {}
        self.tmp_pinned = set()
        self.aux = "dve"
        self.tmpb_ = sb("tmpb", [128, 4, TP], BF16)
        self.tmpb_rr = 0
        self.wring = sb("wring", [128, NW, WSLOT], BF16)
        self.stage = sb("stage", [128, 2, 2048])
        self.stage_rr = 0
        self.kts = self.KT
        self.vs = self.V
        self.kv_rr = 0
        self.carA = sb("carA", [128, DEPTH, NCH, 2])
        self.carB = sb("carB", [128, DEPTH, NCH, 3])
        self.carH = sb("carH", [128, DEPTH, NCH, 1])
        self.stA = sb("stA", [128, NCH, NSEQ_S * 2])
        self.stB = sb("stB", [128, NCH, NSEQ_S * 3])
        self.stH = sb("stH", [128, NCH, NSEQ_S])
        self.oA = sb("oA", [128, NCH, NSEQ_S * 2])
        self.oB = sb("oB", [128, NCH, NSEQ_S * 3])
        self.oH = sb("oH", [128, NCH, NSEQ_S])
        self.wplan = []
        self.w_next_load = 0
        self.w_cons = 0
        self.w_released = 0

    def bank(self, pin=False):
        while self.bank_rr in self.pinned:
            self.bank_rr = (self.bank_rr + 1) % 8
        i = self.bank_rr
        self.bank_rr = (i + 1) % 8
        if pin:
            self.pinned.add(i)
        return self.ps[:, i, :], "ps%d" % i

    def unpin(self, tok):
        self.pinned.discard(int(tok[2:]))

    def tmp(self, pin=False):
        while self.tmp_rr in self.tmp_pinned:
            self.tmp_rr = (self.tmp_rr + 1) % NTMP
        i = self.tmp_rr
        self.tmp_rr = (i + 1) % NTMP
        if pin:
            self.tmp_pinned.add(i)
        return self.tmpf[:, i, :], "tf%d" % i

    def tmp_unpin(self, tok):
        self.tmp_pinned.discard(int(tok[2:]))

    KINDS = {"hcs": (0, 1), "G": (1, 2), "ya": (3, 1), "U": (4, 2), "uc": (6, 3), "tr": (9, 1), "ti": (10, 2),
             "a": (12, 2), "a2": (14, 2), "iu": (16, 2), "hs": (18, 2), "t0": (20, 1)}

    def ktmp(self, kind):
        base, depth = self.KINDS[kind]
        k = self.kcnt.get(kind, 0)
        self.kcnt[kind] = k + 1
        i = base + k % depth
        return self.tmpf[:, i, :], "tf%d" % i

    def tmpb(self):
        i = self.tmpb_rr
        self.tmpb_rr = (i + 1) % 4
        return self.tmpb_[:, i, :], "tb%d" % i

    def stg(self):
        i = self.stage_rr
        self.stage_rr = (i + 1) % 2
        return self.stage[:, i, :], "stg%d" % i

    def vcol(self, name, row, c):
        base = VEC_ROWS[name][0] + row
        return self.vec[:, c, base:base + 1]

    def plan_weights(self):
        d = self.d
        plan = []

        def blk(w, l, c0, n):
            return (w[l, :, c0:c0 + n].rearrange("(c p) n -> p c n", p=128), int(w.shape[1]) // 128, n)

        def layer_blocks(l):
            out = []
            for cp in range(4):
                for sec in (0, 1, 2, 3):
                    out.append(blk(d["w_in"], l, sec * D + cp * 256, 256))
            for cp in range(4):
                out.append(blk(d["w_conv_out"], l, cp * 256, 256))
                out.append(blk(d["w_rnn_out"], l, cp * 256, 256))
                out.append(blk(d["w_in"], l, 4 * D + cp * 256, 256))
                out.append(blk(d["w_in"], l, 5 * D + cp * 256, 256))
            for cp in range(4):
                out.append(blk(d["w_mix_out"], l, cp * 256, 256))
            for cp in range(4):
                out.append(blk(d["w_q"], l, cp * 256, 256))
            for cp in range(4):
                out.append(blk(d["w_o"], l, cp * 256, 256))
            for jp in range(11):
                out.append(blk(d["w_ffn_in"], l, jp * 256, 256))
                out.append(blk(d["w_ffn_in"], l, DFF + jp * 256, 256))
            for c in range(8):
                for hf in range(2):
                    out.append((d["w_ffn_out"][l, hf * 1408:(hf + 1) * 1408, c * 128:(c + 1) * 128]
                                .rearrange("(c p) n -> p c n", p=128), 11, 128))
            return out

        for l in range(DEPTH):
            for b in range(8):
                plan.append(blk(d["w_kv"], l, b * 256, 256) + (None, True, False))
        for t in range(NPT + 1):
            i = 0
            for l in range(DEPTH):
                for b_ in layer_blocks(l):
                    if t == 0:
                        plan.append(b_ + (i, True, i % SCR_SPLIT == 0))
                    elif t == 1 and i % SCR_SPLIT != 0:
                        plan.append(b_ + (i, True, True))
                    else:
                        plan.append(b_ + (i, False, False))
                    i += 1
        self.n_scr = i
        self.wscr = self.nc.dram_tensor("wscr", [self.n_scr, 128, WSLOT], BF16, kind="Internal").ap()
        self.wplan = plan

    def _w_prefetch(self):
        while self.w_next_load < len(self.wplan) and self.w_next_load < self.w_released + NW:
            j = self.w_next_load
            src, K, N, scr, first, wr = self.wplan[j]
            slot = j % NW
            if first:
                dst = self.wring[:, slot, 0:K * N].rearrange("p (c n) -> p c n", c=K)
                self.S.add("pool", lambda e, dst=dst, src=src: e.dma_start(out=dst, in_=src),
                           writes=["w%d" % slot], dma=True)
                if wr:
                    self.S.add("sp", lambda e, slot=slot, scr=scr, n=K * N: e.dma_start(
                        out=self.wscr[scr, :, 0:n], in_=self.wring[:, slot, 0:n]),
                        reads=["w%d" % slot], writes=["scr%d" % scr], dma=True)
            else:
                self.S.add("sp", lambda e, slot=slot, scr=scr, n=K * N: e.dma_start(
                    out=self.wring[:, slot, 0:n], in_=self.wscr[scr, :, 0:n]),
                    reads=["scr%d" % scr], writes=["w%d" % slot], dma=True)
            self.w_next_load += 1

    def wget(self):
        j = self.w_cons
        self.w_cons += 1
        assert j < self.w_released + NW, "weight ring too small"
        self._w_prefetch()
        assert self.w_next_load > j
        src, K, N, scr, first, wr = self.wplan[j]
        slot = j % NW
        ap = self.wring[:, slot, 0:K * N].rearrange("p (c n) -> p c n", c=K)
        return ap, "w%d" % slot

    def wrel(self, n=1):
        self.w_released += n
        self._w_prefetch()

    def load_tm_to_fm(self, src_rows, R, dst_fn, dst_tokens, evac="act", scale_fn=None):
        S = self.S
        st, stt = self.stg()
        S.add("sp", lambda e: e.dma_start(out=st[0:R, 0:D], in_=src_rows), writes=[stt], dma=True)
        self.tm_to_fm(st, stt, R, dst_fn, dst_tokens, evac, scale_fn)

    def tm_to_fm(self, st, stt, R, dst_fn, dst_tokens, evac="act", scale_fn=None):
        S = self.S
        stts = list(stt) if isinstance(stt, (list, tuple)) else [stt]
        for half in range(2):
            bk, bt = self.bank()

            def tr(e, half=half, bk=bk):
                ins = None
                for j in range(4):
                    c = half * 4 + j
                    ins = e.transpose(bk[:, j * R:(j + 1) * R], st[0:R, c * 128:(c + 1) * 128], self.ident[0:R, 0:R])
                return ins
            S.add("pe", tr, reads=stts + ["ident"], writes=[bt])
            if scale_fn is None:
                dst = dst_fn(half * 4, 4)
                src = bk[:, 0:4 * R].rearrange("p (j r) -> p j r", j=4)
                if evac == "act":
                    S.add("act", lambda e, dst=dst, src=src: e.copy(out=dst, in_=src), reads=[bt], writes=dst_tokens)
                else:
                    S.add("dve", lambda e, dst=dst, src=src: e.tensor_copy(out=dst, in_=src), reads=[bt], writes=dst_tokens)
            else:
                for j in range(4):
                    c = half * 4 + j
                    dst = dst_fn(c, 1)
                    S.add("act", lambda e, dst=dst, j=j, bk=bk, c=c: e.activation(
                        out=dst, in_=bk[:, j * R:(j + 1) * R].rearrange("p (j r) -> p j r", j=1), func=AF.Copy,
                        scale=scale_fn(c)), reads=[bt, "vec"], writes=dst_tokens)

    def store_fm_to_tm(self, src_fn, src_tokens, R, dst_rows):
        S = self.S
        st, stt = self.stg()
        for half in range(2):
            bk, bt = self.bank()

            def tr(e, half=half, bk=bk):
                ins = None
                for j in range(4):
                    c = half * 4 + j
                    ins = e.transpose(bk[0:R, j * 128:(j + 1) * 128], src_fn(c), self.ident)
                return ins
            S.add("pe", tr, reads=list(src_tokens) + ["ident"], writes=[bt])
            S.add("act", lambda e, half=half, bk=bk: e.copy(out=st[0:R, half * 512:(half + 1) * 512], in_=bk[0:R, :]),
                  reads=[bt], writes=[stt])
        S.add("sp", lambda e: e.dma_start(out=dst_rows, in_=st[0:R, 0:D]), reads=[stt], dma=True)

    def prologue(self, tile0):
        S = self.S
        d = self.d
        nc = self.nc
        S.add("sp", lambda e: e.dma_start(out=self.ident, in_=d["ident"]), writes=["ident"], dma=True)
        S.add("dve", lambda e: e.memset(self.ones, 1.0), writes=["ones"])
        S.add("dve", lambda e: e.memset(self.cst[:, 0:1], EPS), writes=["cst"])
        S.add("dve", lambda e: e.memset(self.cst[:, 1:2], 1.0), writes=["cst"])
        S.add("dve", lambda e: e.memset(self.cst[:, 2:3], 0.0), writes=["cst"])
        S.add("dve", lambda e: e.memset(self.cst[:, 3:4], -0.5), writes=["cst"])
        S.add("dve", lambda e: e.memset(self.cst[:, 4:5], 0.5), writes=["cst"])
        S.add("dve", lambda e: e.memset(self.cst[:, 5:6], -1.0), writes=["cst"])
        S.add("pool", lambda e: e.memset(self.bd, 0.0), writes=["bd"])
        st, stt = self.stg()
        for name, (r0, n) in VEC_ROWS.items():
            S.add("sp", lambda e, name=name, r0=r0, n=n: e.dma_start(out=st[r0:r0 + n, 0:D], in_=d[name]),
                  writes=[stt], dma=True)
        self.tm_to_fm(st, stt, NVEC, lambda c0, n: self.vec[:, c0:c0 + n, :], ["vec"], evac="dve")
        for l in range(DEPTH):
            for g, wname in enumerate(("w_gate_a", "w_gate_x")):
                for hh in range(2):
                    src = d[wname][l].rearrange("(c h) k j -> h k c j", h=2)[hh]
                    dst = self.bd[hh * 64:(hh + 1) * 64, l * 2 + g, :, hh * 64:(hh + 1) * 64]
                    S.add("pool", lambda e, dst=dst, src=src: e.dma_start(out=dst, in_=src),
                          reads=[], writes=["bd"], dma=True)
        for l in range(DEPTH):
            lam = self.vec[:, :, VEC_ROWS["lam"][0] + l]
            bga = self.vec[:, :, VEC_ROWS["bga"][0] + l]
            bgx = self.vec[:, :, VEC_ROWS["bgx"][0] + l]
            t_abs = self.dtmp[:, :, 0]
            t_e = self.dtmp[:, :, 1]
            t_l = self.dtmp[:, :, 2]
            t_r = self.dtmp[:, :, 3]
            S.add("dve", lambda e, bga=bga, l=l: e.tensor_scalar_mul(out=self.der[:, :, 4 * l + 0], in0=bga, scalar1=-1.0),
                  reads=["vec"], writes=["der"])
            S.add("dve", lambda e, bgx=bgx, l=l: e.tensor_scalar_mul(out=self.der[:, :, 4 * l + 1], in0=bgx, scalar1=-1.0),
                  reads=["vec"], writes=["der"])
            S.add("act", lambda e, lam=lam: e.activation(out=t_abs, in_=lam, func=AF.Abs), reads=["vec"], writes=["dt0"])
            S.add("act", lambda e: e.activation(out=t_e, in_=t_abs, func=AF.Exp, scale=-1.0), reads=["dt0"], writes=["dt1"])
            S.add("act", lambda e: e.activation(out=t_l, in_=t_e, func=AF.Ln, bias=self.cst[:, 1:2]),
                  reads=["dt1", "cst"], writes=["dt2"])
            S.add("dve", lambda e, lam=lam: e.tensor_scalar(out=t_r, in0=lam, scalar1=-1.0, scalar2=0.0,
                                                            op0=ALU.mult, op1=ALU.max), reads=["vec"], writes=["dt3"])
            S.add("dve", lambda e: e.tensor_tensor(out=t_r, in0=t_r, in1=t_l, op=ALU.add), reads=["dt3", "dt2"], writes=["dt3"])
            S.add("dve", lambda e, l=l: e.tensor_scalar_mul(out=self.der[:, :, 4 * l + 2], in0=t_r, scalar1=-8.0),
                  reads=["dt3"], writes=["der"])
            S.add("dve", lambda e, l=l: e.tensor_scalar_mul(out=self.der[:, :, 4 * l + 3], in0=t_r, scalar1=-16.0),
                  reads=["dt3"], writes=["der"])
        S.add("pool", lambda e: e.memset(self.carA, 0.0), writes=["carA%d" % c_ for c_ in range(NCH)])
        S.add("pool", lambda e: e.memset(self.carB, 0.0), writes=["carB%d" % c_ for c_ in range(NCH)])
        S.add("pool", lambda e: e.memset(self.carH, 0.0), writes=["carH%d" % c_ for c_ in range(NCH)])
        self.load_x_direct(tile0)
        tile0["staged"] = []
        self.mem_kv()

    def mem_kv(self):
        S = self.S
        d = self.d
        memt = self.stage.rearrange("p a f -> p (a f)")[:, 0:2 * D].rearrange("p (a f) -> p a f", a=2)
        S.add("sp", lambda e: e.dma_start(out=memt, in_=d["memp"].rearrange("(a p) f -> p a f", p=128)),
              writes=["stg0"], dma=True)
        for a in range(2):
            for hf in range(2):
                t, tt = self.tmp()
                S.add("act", lambda e, a=a, hf=hf, t=t: e.activation(
                    out=t[:, 0:512], in_=memt[:, a, hf * 512:(hf + 1) * 512], func=AF.Square),
                    reads=["stg0"], writes=[tt])
                S.add("dve", lambda e, a=a, hf=hf, t=t: e.reduce_sum(
                    out=self.dtmp[:, 0, 4 + 2 * a + hf:5 + 2 * a + hf], in_=t[:, 0:512], axis=mybir.AxisListType.X),
                    reads=[tt], writes=["ssq%d%d" % (a, hf)])
        rs = self.dtmp[:, 1, 0:2]
        S.add("dve", lambda e: e.tensor_tensor(out=self.dtmp[:, 1, 2:4].rearrange("p (a o) -> p a o", o=1),
                                               in0=self.dtmp[:, 0, 4:8].rearrange("p (a h) -> p a h", h=2)[:, :, 0:1],
                                               in1=self.dtmp[:, 0, 4:8].rearrange("p (a h) -> p a h", h=2)[:, :, 1:2],
                                               op=ALU.add),
              reads=["ssq00", "ssq01", "ssq10", "ssq11"], writes=["ssum"])
        S.add("act", lambda e: e.activation(out=self.dtmp[:, 1, 4:6], in_=self.dtmp[:, 1, 2:4], func=AF.Sqrt,
                                            scale=1.0 / D, bias=self.cst[:, 0:1]), reads=["ssum", "cst"], writes=["srt"])
        S.add("dve", lambda e: e.reciprocal(out=rs, in_=self.dtmp[:, 1, 4:6]), reads=["srt"], writes=["mrs"])
        for a in range(2):
            S.add("dve", lambda e, a=a: e.tensor_scalar(out=memt[:, a, :], in0=memt[:, a, :], scalar1=rs[:, a:a + 1],
                                                        scalar2=None, op0=ALU.mult), reads=["mrs", "stg0"], writes=["stg0"])
        mT0 = self.m.rearrange("p c t -> p (c t)")[:, 0:NCH * NMEM].rearrange("p (c t) -> p c t", c=NCH)
        for a in range(2):
            for half in range(2):
                bk, bt = self.bank()

                def tr(e, a=a, half=half, bk=bk):
                    ins = None
                    for j in range(4):
                        c = half * 4 + j
                        ins = e.transpose(bk[:, j * 128:(j + 1) * 128], memt[:, a, c * 128:(c + 1) * 128], self.ident)
                    return ins
                S.add("pe", tr, reads=["stg0", "ident"], writes=[bt])
                S.add("act", lambda e, a=a, half=half, bk=bk: e.copy(
                    out=mT0[:, half * 4:half * 4 + 4, a * 128:(a + 1) * 128],
                    in_=bk.rearrange("p (j r) -> p j r", j=4)), reads=[bt], writes=["m%d" % c_ for c_ in range(NCH)])
        mTl = self.xn.rearrange("p c t -> p (c t)")[:, 0:NCH * NMEM].rearrange("p (c t) -> p c t", c=NCH)
        for l in range(DEPTH):
            for c in range(NCH):
                S.add("dve", lambda e, c=c, l=l: e.tensor_scalar(out=mTl[:, c, :], in0=mT0[:, c, :],
                                                                 scalar1=self.vcol("gains", 7 * l + 6, c), scalar2=None,
                                                                 op0=ALU.mult), reads=["m%d" % c_ for c_ in range(NCH)] + ["vec"], writes=["xn%d" % c_ for c_ in range(NCH)])
            for b in range(8):
                w, wt = self.wget()
                for a in range(2):
                    bk, bt = self.bank()

                    def mm(e, a=a, bk=bk, w=w):
                        ins = None
                        for k in range(NCH):
                            ins = e.matmul(bk[:, 0:256], mTl[:, k, a * 128:(a + 1) * 128], w[:, k, :],
                                           start=(k == 0), stop=(k == NCH - 1))
                        return ins
                    S.add("pe", mm, reads=["xn%d" % c_ for c_ in range(NCH)] + [wt], writes=[bt])
                    t, tt = self.tmp()
                    S.add("act", lambda e, t=t, bk=bk: e.copy(out=t[:, 0:256], in_=bk[:, 0:256]), reads=[bt], writes=[tt])
                    if b < 4:
                        dst = d["pk"][l, a * 128:(a + 1) * 128, b * 256:(b + 1) * 256]
                    else:
                        dst = d["pv"][l, a * 128:(a + 1) * 128, (b - 4) * 256:(b - 3) * 256]
                        S.add("dve", lambda e, t=t, a=a, b=b, l=l: e.tensor_copy(
                            out=self.V[:, l, a, (b - 4) * 256:(b - 3) * 256], in_=t[:, 0:256]), reads=[tt], writes=["V%d" % l])
                    S.add("sp", lambda e, dst=dst, t=t: e.dma_start(out=dst, in_=t[:, 0:256]), reads=[tt], dma=True)
                if b < 4:
                    for j in range(2):
                        bk, bt = self.bank()

                        def mm2(e, j=j, bk=bk, w=w):
                            ins = None
                            for k in range(NCH):
                                ins = e.matmul(bk[:, 0:256], w[:, k, j * 128:(j + 1) * 128], mTl[:, k, :],
                                               start=(k == 0), stop=(k == NCH - 1))
                            return ins
                        S.add("pe", mm2, reads=["xn%d" % c_ for c_ in range(NCH)] + [wt], writes=[bt])
                        S.add("act", lambda e, j=j, b=b, l=l, bk=bk: e.copy(out=self.KT[:, l, 2 * b + j, :], in_=bk[:, 0:256]),
                              reads=[bt], writes=["KT%d" % l])
                self.wrel()

    def rstd_from(self, srcs, T, scale=1.0):
        S = self.S
        bk, bt = self.bank()
        for c, (src, stoks) in enumerate(srcs):
            sq, sqt = self.tmpb()
            S.add("act", lambda e, sq=sq, src=src: e.activation(out=sq[:, 0:T], in_=src, func=AF.Square, scale=scale),
                  reads=stoks, writes=[sqt])
            S.add("pe", lambda e, sq=sq, c=c, bk=bk: e.matmul(bk[:, 0:T], self.ones, sq[:, 0:T], start=(c == 0),
                                                             stop=(c == NCH - 1)), reads=[sqt, "ones"], writes=[bt])
        ms, mst = self.tmp()
        S.add("act", lambda e: e.activation(out=ms[:, 0:T], in_=bk[:, 0:T], func=AF.Ln, scale=1.0 / D,
                                            bias=self.cst[:, 0:1]), reads=[bt, "cst"], writes=[mst])
        rs, rst = self.tmp()
        S.add("act", lambda e: e.activation(out=rs[:, 0:T], in_=ms[:, 0:T], func=AF.Exp, scale=-0.5),
              reads=[mst], writes=[rst])
        return rs, rst

    def norm_begin(self, T):
        bk, bt = self.bank(pin=True)
        return dict(bk=bk, bt=bt, T=T, n=0)

    def norm_add(self, acc, src, stoks, lag=True):
        S = self.S
        T, bk, bt, c = acc["T"], acc["bk"], acc["bt"], acc["n"]
        acc["n"] = c + 1
        sq, sqt = self.tmpb()
        S.add("act", lambda e: e.activation(out=sq[:, 0:T], in_=src, func=AF.Square), reads=stoks, writes=[sqt])

        def emit_pe():
            S.add("pe", lambda e: e.matmul(bk[:, 0:T], self.ones, sq[:, 0:T], start=(c == 0), stop=(c == NCH - 1)),
                  reads=[sqt, "ones"], writes=[bt])
        prev = acc.get("pend")
        if prev is not None:
            prev()
        if lag:
            acc["pend"] = emit_pe
        else:
            acc["pend"] = None
            emit_pe()

    def norm_finish(self, acc, pin=False):
        S = self.S
        T, bk, bt = acc["T"], acc["bk"], acc["bt"]
        assert acc["n"] == NCH
        if acc.get("pend") is not None:
            acc["pend"]()
            acc["pend"] = None
        ms, mst = self.tmp()
        S.add("act", lambda e: e.activation(out=ms[:, 0:T], in_=bk[:, 0:T], func=AF.Ln, scale=1.0 / D,
                                            bias=self.cst[:, 0:1]), reads=[bt, "cst"], writes=[mst])
        rs, rst = self.tmp(pin=pin)
        S.add("act", lambda e: e.activation(out=rs[:, 0:T], in_=ms[:, 0:T], func=AF.Exp, scale=-0.5),
              reads=[mst], writes=[rst])
        self.unpin(bt)
        return rs, rst

    def pre_norm(self, T, gidx):
        S = self.S
        rs, rst = self.rstd_from([(self.x[:, c, 0:T], ["x%d" % c]) for c in range(NCH)], T)
        for c in range(NCH):
            S.add("dve", lambda e, c=c: e.scalar_tensor_tensor(out=self.xn[:, c, 0:T], in0=self.x[:, c, 0:T],
                                                               scalar=self.vcol("gains", gidx, c), in1=rs[:, 0:T],
                                                               op0=ALU.mult, op1=ALU.mult),
                  reads=["x%d" % c, rst, "vec"], writes=["xn%d" % c])

    def post_norm_residual(self, T, gidx, acc):
        S = self.S
        rs, rst = self.norm_finish(acc)
        for c in range(NCH):
            t, tt = self.tmp()
            S.add("dve", lambda e, c=c, t=t: e.scalar_tensor_tensor(out=t[:, 0:T], in0=self.m[:, c, 0:T],
                                                                    scalar=self.vcol("gains", gidx, c), in1=rs[:, 0:T],
                                                                    op0=ALU.mult, op1=ALU.mult),
                  reads=["m%d" % c, rst, "vec"], writes=[tt])
            S.add(self.aux if c % 4 == 1 else "dve", lambda e, c=c, t=t: e.tensor_tensor(
                out=self.x[:, c, 0:T], in0=self.x[:, c, 0:T], in1=t[:, 0:T], op=ALU.add),
                reads=[tt, "x%d" % c], writes=["x%d" % c])

    def proj_group(self, w, wt, col0, rhs_fn, rhs_tokens, T, nk=NCH):
        bk, bt = self.bank()

        def mm(e):
            ins = None
            for k in range(nk):
                ins = e.matmul(bk[:, 0:T], w[:, k, col0:col0 + 128], rhs_fn(k), start=(k == 0), stop=(k == nk - 1))
            return ins
        self.S.add("pe", mm, reads=list(rhs_tokens) + [wt], writes=[bt])
        return bk, bt

    def load_x_dma(self, tile, use_R=False):
        S = self.S
        T = tile["T"]
        staged = []
        Rf = self.R.bitcast(F32)
        for tb in range(T // 128):
            src = tile["xsrc"][tb * 128:(tb + 1) * 128, :]
            assert use_R
            st = Rf[:, 4 * tb:4 * tb + 4, :].rearrange("p a f -> p (a f)")
            toks = ["R%d" % u for u in range(4 * tb, 4 * tb + 4)]
            S.add("sp", lambda e, st=st, src=src: e.dma_start(out=st[:, 0:D], in_=src), writes=toks, dma=True)
            staged.append((st, toks))
        return staged

    def load_x_direct(self, tile):
        T = tile["T"]
        for tb in range(T // 128):
            src = tile["xsrc"][tb * 128:(tb + 1) * 128, :]
            self.load_tm_to_fm(src, 128, lambda c0, n, tb=tb: self.x[:, c0:c0 + n, tb * 128:(tb + 1) * 128],
                               ["x%d" % c for c in range(NCH)], evac="act" if tb % 2 == 0 else "dve")

    def load_x_tr(self, tile, staged):
        for tb, (st, toks) in enumerate(staged):
            self.tm_to_fm(st, toks, 128, lambda c0, n, tb=tb: self.x[:, c0:c0 + n, tb * 128:(tb + 1) * 128],
                          ["x%d" % c for c in range(NCH)], evac="act" if tb % 2 == 0 else "dve")

    def store_y(self, tile):
        T = tile["T"]
        for tb in range(T // 128):
            self.store_fm_to_tm(lambda c, tb=tb: self.x[:, c, tb * 128:(tb + 1) * 128], ["x%d" % c for c in range(NCH)],
                                128, tile["ydst"][tb * 128:(tb + 1) * 128, :])

    def mix(self, tile, l):
        S = self.S
        T, nseq, L = tile["T"], tile["nseq"], tile["L"]
        sample = tile["kind"] == "s"
        first = tile.get("first", False)
        xn_toks = ["xn%d" % c for c in range(NCH)]
        rhs_xn = lambda k: self.xn[:, k, 0:T]
        self.pre_norm(T, 7 * l + 0)
        if sample:
            self.load_tm_to_fm(self.d["sta"][l], NSEQ_S * 2, lambda c0, n: self.stA[:, c0:c0 + n, :], ["stA"], evac="dve")
            self.load_tm_to_fm(self.d["stb"][l], NSEQ_S * 3, lambda c0, n: self.stB[:, c0:c0 + n, :], ["stB"], evac="dve")
            self.load_tm_to_fm(self.d["sth"][l], NSEQ_S, lambda c0, n: self.stH[:, c0:c0 + n, :], ["stH"], evac="dve")
        WA, WB = 2, 3
        aux = self.aux
        ctx = {}

        def stage1(c, col, wb_, wbt, wc_, wct, wh_, wht, wu_, wut):
            bu, but = self.proj_group(wu_, wut, col, rhs_xn, xn_toks, T)
            bhc, bhct = self.proj_group(wc_, wct, col, rhs_xn, xn_toks, T)
            bhh, bhht = self.proj_group(wh_, wht, col, rhs_xn, xn_toks, T)
            bhb, bhbt = self.proj_group(wb_, wbt, col, rhs_xn, xn_toks, T)
            U, Ut = self.ktmp("U")
            U3 = U[:, 0:nseq * (WB + L)].rearrange("p (s w) -> p s w", s=nseq)
            if sample:
                S.add(aux, lambda e: e.tensor_copy(out=U3[:, :, 0:WB], in_=self.stB[:, c, :].rearrange("p (s k) -> p s k", k=WB)),
                      reads=["stB"], writes=[Ut])
            else:
                S.add(aux, lambda e: e.tensor_copy(out=U3[:, 0, 0:WB], in_=self.carB[:, l, c, :]), reads=["carB%d" % c], writes=[Ut])
            bu3 = bu[:, 0:T].rearrange("p (s t) -> p s t", s=nseq)
            S.add("act", lambda e: e.copy(out=U3[:, :, WB:WB + L], in_=bu3), reads=[but, Ut], writes=[Ut])
            uc, uct = self.ktmp("uc")
            uc3 = uc[:, 0:T].rearrange("p (s t) -> p s t", s=nseq)
            S.add("act", lambda e: e.activation(out=uc3, in_=bu3, func=AF.Identity, scale=self.vcol("cbw", 4 * l + 3, c),
                                                bias=self.vcol("cbb", l, c)), reads=[but, "vec"], writes=[uct])
            if sample:
                S.add(aux, lambda e: e.tensor_copy(out=self.oB[:, c, :].rearrange("p (s k) -> p s k", k=WB), in_=U3[:, :, L:L + WB]),
                      reads=[Ut], writes=["oB"])
            else:
                S.add(aux, lambda e: e.tensor_copy(out=self.carB[:, l, c, :], in_=U3[:, 0, L:L + WB]), reads=[Ut], writes=["carB%d" % c])
            hcs, hcst = self.ktmp("hcs")
            S.add("act", lambda e: e.copy(out=hcs[:, 0:T], in_=bhc[:, 0:T]), reads=[bhct], writes=[hcst])
            G, Gt = self.ktmp("G")
            G3 = G[:, 0:nseq * (WA + L)].rearrange("p (s w) -> p s w", s=nseq)
            if sample:
                S.add(aux, lambda e: e.tensor_copy(out=G3[:, :, 0:WA], in_=self.stA[:, c, :].rearrange("p (s k) -> p s k", k=WA)),
                      reads=["stA"], writes=[Gt])
            else:
                S.add(aux, lambda e: e.tensor_copy(out=G3[:, 0, 0:WA], in_=self.carA[:, l, c, :]), reads=["carA%d" % c], writes=[Gt])
            S.add("dve", lambda e: e.tensor_tensor(out=G3[:, :, WA:WA + L], in0=bhh[:, 0:T].rearrange("p (s t) -> p s t", s=nseq),
                                                   in1=hcs[:, 0:T].rearrange("p (s t) -> p s t", s=nseq), op=ALU.mult),
                  reads=[bhht, hcst, Gt], writes=[Gt])
            if sample:
                S.add(aux, lambda e: e.tensor_copy(out=self.oA[:, c, :].rearrange("p (s k) -> p s k", k=WA), in_=G3[:, :, L:L + WA]),
                      reads=[Gt], writes=["oA"])
            else:
                S.add(aux, lambda e: e.tensor_copy(out=self.carA[:, l, c, :], in_=G3[:, 0, L:L + WA]), reads=[Gt], writes=["carA%d" % c])
            ya, yat = self.ktmp("ya")
            ya3 = ya[:, 0:T].rearrange("p (s t) -> p s t", s=nseq)
            S.add("act", lambda e: e.activation(out=ya3, in_=G3[:, :, 2:2 + L], func=AF.Copy, scale=self.vcol("caw", 3 * l + 2, c)),
                  reads=[Gt, "vec"], writes=[yat])
            for k in (1, 0):
                S.add("dve", lambda e, k=k: e.scalar_tensor_tensor(out=ya3, in0=G3[:, :, k:k + L], scalar=self.vcol("caw", 3 * l + k, c),
                                                                   in1=ya3, op0=ALU.mult, op1=ALU.add), reads=[Gt, yat, "vec"], writes=[yat])
            S.add("dve", lambda e: e.tensor_tensor(out=self.R[:, c, 0:T], in0=bhb[:, 0:T], in1=ya[:, 0:T], op=ALU.mult),
                  reads=[bhbt, yat], writes=["R%d" % c])
            for k in (2, 1, 0):
                S.add("dve", lambda e, k=k: e.scalar_tensor_tensor(out=uc3, in0=U3[:, :, k:k + L], scalar=self.vcol("cbw", 4 * l + k, c),
                                                                   in1=uc3, op0=ALU.mult, op1=ALU.add), reads=[Ut, uct, "vec"], writes=[uct])
            ucb, ucbt = self.tmpb()
            S.add("dve", lambda e: e.tensor_copy(out=ucb[:, 0:T], in_=uc[:, 0:T]), reads=[uct], writes=[ucbt])
            ctx[c] = dict(uc=uc, uct=uct, ucb=ucb, ucbt=ucbt)

        def stage2(c):
            uc, uct, ucb, ucbt = ctx[c]["uc"], ctx[c]["uct"], ctx[c]["ucb"], ctx[c]["ucbt"]
            bga_, bgat = self.bank()
            S.add("pe", lambda e: e.matmul(bga_[:, 0:T], self.bd[:, l * 2 + 0, c, :], ucb[:, 0:T], start=True, stop=True),
                  reads=[ucbt, "bd"], writes=[bgat])
            bgx_, bgxt = self.bank()
            S.add("pe", lambda e: e.matmul(bgx_[:, 0:T], self.bd[:, l * 2 + 1, c, :], ucb[:, 0:T], start=True, stop=True),
                  reads=[ucbt, "bd"], writes=[bgxt])

            def sigm(dst, bk, bias_ap, rd, wr):
                S.add("act", lambda e: e.activation(out=dst[:, 0:T], in_=bk[:, 0:T], func=AF.Exp, scale=-1.0, bias=bias_ap),
                      reads=rd + ["der"], writes=[wr])
                S.add("act", lambda e: e.activation(out=dst[:, 0:T], in_=dst[:, 0:T], func=AF.Ln, bias=self.cst[:, 1:2]),
                      reads=[wr, "cst"], writes=[wr])
                S.add("act", lambda e: e.activation(out=dst[:, 0:T], in_=dst[:, 0:T], func=AF.Exp, scale=-1.0), reads=[wr], writes=[wr])
            tr_, trt = self.ktmp("tr")
            sigm(tr_, bga_, self.der[:, c, 4 * l + 0:4 * l + 1], [bgat], trt)
            ti_, tit = self.ktmp("ti")
            sigm(ti_, bgx_, self.der[:, c, 4 * l + 1:4 * l + 2], [bgxt], tit)
            a_, at = self.ktmp("a")
            S.add("act", lambda e: e.activation(out=a_[:, 0:T], in_=tr_[:, 0:T], func=AF.Exp, scale=self.der[:, c, 4 * l + 2:4 * l + 3]),
                  reads=[trt, "der"], writes=[at])
            a2_, a2t = self.ktmp("a2")
            S.add("act", lambda e: e.activation(out=a2_[:, 0:T], in_=tr_[:, 0:T], func=AF.Exp, scale=self.der[:, c, 4 * l + 3:4 * l + 4]),
                  reads=[trt, "der"], writes=[a2t])
            S.add("act", lambda e: e.activation(out=a2_[:, 0:T], in_=a2_[:, 0:T], func=AF.Ln, scale=-1.0, bias=self.cst[:, 1:2]),
                  reads=[a2t, "cst"], writes=[a2t])
            S.add("act", lambda e: e.activation(out=a2_[:, 0:T], in_=a2_[:, 0:T], func=AF.Exp, scale=0.5), reads=[a2t], writes=[a2t])
            ctx[c].update(ti=ti_, tit=tit, a=a_, at=at, a2=a2_, a2t=a2t)

        def stage3(c):
            k = ctx.pop(c)
            uc, uct, ti_, tit, a_, at, a2_, a2t = k["uc"], k["uct"], k["ti"], k["tit"], k["a"], k["at"], k["a2"], k["a2t"]
            iu, iut = self.ktmp("iu")
            S.add("dve", lambda e: e.tensor_tensor(out=iu[:, 0:T], in0=ti_[:, 0:T], in1=uc[:, 0:T], op=ALU.mult),
                  reads=[tit, uct], writes=[iut])
            S.add("dve", lambda e: e.tensor_tensor(out=iu[:, 0:T], in0=iu[:, 0:T], in1=a2_[:, 0:T], op=ALU.mult),
                  reads=[iut, a2t], writes=[iut])
            hs, hst = self.ktmp("hs")
            if sample:
                a3 = a_[:, 0:T].rearrange("p (s t) -> p s t", s=nseq)
                b3 = iu[:, 0:T].rearrange("p (s t) -> p s t", s=nseq)
                t0, t0t = self.ktmp("t0")
                S.add("dve", lambda e: e.tensor_tensor(out=t0[:, 0:nseq], in0=a3[:, :, 0], in1=self.stH[:, c, :], op=ALU.mult),
                      reads=[at, "stH"], writes=[t0t])
                S.add("dve", lambda e: e.tensor_tensor(out=b3[:, :, 0], in0=b3[:, :, 0], in1=t0[:, 0:nseq], op=ALU.add),
                      reads=[t0t, iut], writes=[iut])
                S.add("dve", lambda e: e.memset(a3[:, :, 0], 0.0), reads=[t0t], writes=[at])
                init, init_toks = 0.0, []
            else:
                init, init_toks = self.carH[:, l, c, :], ["carH%d" % c]
            S.add("dve", lambda e: e.tensor_tensor_scan(out=hs[:, 0:T], data0=a_[:, 0:T], data1=iu[:, 0:T], initial=init,
                                                        op0=ALU.mult, op1=ALU.add), reads=[at, iut] + init_toks, writes=[hst])
            if sample:
                S.add(aux, lambda e: e.tensor_copy(out=self.oH[:, c, :], in_=hs[:, 0:T].rearrange("p (s t) -> p s t", s=nseq)[:, :, L - 1]),
                      reads=[hst], writes=["oH"])
            else:
                S.add(aux, lambda e: e.tensor_copy(out=self.carH[:, l, c, :], in_=hs[:, T - 1:T]), reads=[hst], writes=["carH%d" % c])
            S.add("act", lambda e: e.copy(out=self.R[:, 8 + c, 0:T], in_=hs[:, 0:T]), reads=[hst], writes=["R%d" % (8 + c)])

        for cp in range(4):
            wb_, wbt = self.wget()
            wc_, wct = self.wget()
            wh_, wht = self.wget()
            wu_, wut = self.wget()
            for j in range(2):
                c = cp * 2 + j
                stage1(c, j * 128, wb_, wbt, wc_, wct, wh_, wht, wu_, wut)
                if c >= 1:
                    stage2(c - 1)
                if c >= 2:
                    stage3(c - 2)
            self.wrel(4)
        stage2(NCH - 1)
        stage3(NCH - 2)
        stage3(NCH - 1)
        ba_toks = ["R%d" % c for c in range(NCH)]
        hs_toks = ["R%d" % (8 + c) for c in range(NCH)]
        for cp in range(4):
            wco, wcot = self.wget()
            wro, wrot = self.wget()
            wgc, wgct = self.wget()
            wgr, wgrt = self.wget()
            def sigm0(dst, bk, rd, wr):
                S.add("act", lambda e: e.activation(out=dst[:, 0:T], in_=bk[:, 0:T], func=AF.Exp, scale=-1.0),
                      reads=rd, writes=[wr])
                S.add("act", lambda e: e.activation(out=dst[:, 0:T], in_=dst[:, 0:T], func=AF.Ln, bias=self.cst[:, 1:2]),
                      reads=[wr, "cst"], writes=[wr])
                S.add("act", lambda e: e.activation(out=dst[:, 0:T], in_=dst[:, 0:T], func=AF.Exp, scale=-1.0),
                      reads=[wr], writes=[wr])
            part = {}
            for j in range(2):
                col = j * 128
                bgc, bgct = self.proj_group(wgc, wgct, col, rhs_xn, xn_toks, T)
                bgr, bgrt = self.proj_group(wgr, wgrt, col, rhs_xn, xn_toks, T)
                byc, byct = self.proj_group(wco, wcot, col, lambda k: self.R[:, k, 0:T], ba_toks, T)
                tc_, tct = self.tmp()
                sigm0(tc_, bgc, [bgct], tct)
                tg_, tgt = self.tmp()
                sigm0(tg_, bgr, [bgrt], tgt)
                S.add("dve", lambda e, tc_=tc_, bk=byc: e.tensor_tensor(
                    out=tc_[:, 0:T], in0=bk[:, 0:T], in1=tc_[:, 0:T], op=ALU.mult), reads=[tct, byct], writes=[tct])
                part[j] = (tc_, tct, tg_, tgt)
            for j in range(2):
                c = cp * 2 + j
                tc_, tct, tg_, tgt = part[j]
                byr, byrt = self.proj_group(wro, wrot, j * 128, lambda k: self.R[:, 8 + k, 0:T], hs_toks, T)
                S.add("dve", lambda e, tg_=tg_, bk=byr: e.tensor_tensor(
                    out=tg_[:, 0:T], in0=bk[:, 0:T], in1=tg_[:, 0:T], op=ALU.mult), reads=[tgt, byrt], writes=[tgt])
                S.add("dve", lambda e, tc_=tc_, tg_=tg_, c=c: e.tensor_tensor(
                    out=self.R[:, 16 + c, 0:T], in0=tc_[:, 0:T], in1=tg_[:, 0:T], op=ALU.add),
                    reads=[tct, tgt], writes=["R%d" % (16 + c)])
            self.wrel(4)
        z_toks = ["R%d" % (16 + c) for c in range(NCH)]
        acc = self.norm_begin(T)
        for cp in range(4):
            w, wt = self.wget()
            for j in range(2):
                c = cp * 2 + j
                bk, bt = self.proj_group(w, wt, j * 128, lambda k: self.R[:, 16 + k, 0:T], z_toks, T)
                S.add("act", lambda e, bk=bk, c=c: e.copy(out=self.m[:, c, 0:T], in_=bk[:, 0:T]),
                      reads=[bt], writes=["m%d" % c])
                self.norm_add(acc, bk[:, 0:T], [bt])
            self.wrel()
        self.post_norm_residual(T, 7 * l + 1, acc)
        if sample:
            self.store_fm_to_tm(lambda c: self.oA[:, c, :], ["oA"], NSEQ_S * 2, self.d["sca"][l])
            self.store_fm_to_tm(lambda c: self.oB[:, c, :], ["oB"], NSEQ_S * 3, self.d["scb"][l])
            self.store_fm_to_tm(lambda c: self.oH[:, c, :], ["oH"], NSEQ_S, self.d["sh"][l])
        elif tile.get("last", False):
            self.store_fm_to_tm(lambda c: self.carA[:, l, c, :], ["carA%d" % c_ for c_ in range(NCH)], 2, self.d["pca"][l])
            self.store_fm_to_tm(lambda c: self.carB[:, l, c, :], ["carB%d" % c_ for c_ in range(NCH)], 3, self.d["pcb"][l])
            self.store_fm_to_tm(lambda c: self.carH[:, l, c, :], ["carH%d" % c_ for c_ in range(NCH)], 1, self.d["ph"][l])

    def attn(self, tile, l):
        S = self.S
        T = tile["T"]
        sample = tile["kind"] == "s"
        xn_toks = ["xn%d" % c for c in range(NCH)]
        rhs_xn = lambda k: self.xn[:, k, 0:T]
        acc = self.norm_begin(T)
        for c in range(NCH):
            self.norm_add(acc, self.x[:, c, 0:T], ["x%d" % c])
            S.add("act", lambda e, c=c: e.activation(out=self.xn[:, c, 0:T], in_=self.x[:, c, 0:T], func=AF.Copy,
                                                     scale=self.vcol("gains", 7 * l + 2, c)),
                  reads=["x%d" % c, "vec"], writes=["xn%d" % c])
        rs, rst = self.norm_finish(acc)
        for cp in range(4):
            w, wt = self.wget()
            for j in range(2):
                c = cp * 2 + j
                bk, bt = self.proj_group(w, wt, j * 128, rhs_xn, xn_toks, T)
                S.add("dve", lambda e, bk=bk, c=c: e.scalar_tensor_tensor(
                    out=self.R[:, c, 0:T], in0=bk[:, 0:T], scalar=1.0 / 16.0, in1=rs[:, 0:T], op0=ALU.mult, op1=ALU.mult),
                    reads=[bt, rst], writes=["R%d" % c])
            self.wrel()
        if not sample:
            def head_a(h):
                pts = []
                for mc in range(2):
                    bk, bt = self.bank()

                    def mm(e, bk=bk, mc=mc):
                        ins = None
                        for dc in range(2):
                            ins = e.matmul(bk[:, 0:T], self.KT[:, l, 2 * h + dc, mc * 128:(mc + 1) * 128],
                                           self.R[:, 2 * h + dc, 0:T], start=(dc == 0), stop=(dc == 1))
                        return ins
                    S.add("pe", mm, reads=["KT%d" % l, "R%d" % (2 * h), "R%d" % (2 * h + 1)], writes=[bt])
                    ru = 8 + 2 * h + mc
                    S.add("act", lambda e, bk=bk, ru=ru: e.activation(out=self.R[:, ru, 0:T], in_=bk[:, 0:T], func=AF.Exp),
                          reads=[bt], writes=["R%d" % ru])
                    pts.append(ru)
                return pts

            def head_b(h, pts):
                bs, bst = self.bank()

                def mms(e):
                    ins = None
                    for mc in range(2):
                        ins = e.matmul(bs[:, 0:T], self.ones, self.R[:, pts[mc], 0:T], start=(mc == 0), stop=(mc == 1))
                    return ins
                S.add("pe", mms, reads=["R%d" % p for p in pts] + ["ones"], writes=[bst])
                ri, rit = self.tmp()
                S.add("act", lambda e: e.activation(out=ri[:, 0:T], in_=bs[:, 0:T], func=AF.Ln), reads=[bst], writes=[rit])
                S.add("act", lambda e: e.activation(out=ri[:, 0:T], in_=ri[:, 0:T], func=AF.Exp, scale=-1.0),
                      reads=[rit], writes=[rit])
                for dc in range(2):
                    bo, bot = self.bank()

                    def mmo(e, bo=bo, dc=dc):
                        ins = None
                        for mc in range(2):
                            ins = e.matmul(bo[:, 0:T], self.V[:, l, mc, h * 256 + dc * 128:h * 256 + (dc + 1) * 128],
                                           self.R[:, pts[mc], 0:T], start=(mc == 0), stop=(mc == 1))
                        return ins
                    S.add("pe", mmo, reads=["R%d" % p for p in pts] + ["V%d" % l], writes=[bot])
                    ou = 16 + 2 * h + dc
                    S.add("dve", lambda e, bo=bo, ou=ou: e.tensor_tensor(out=self.R[:, ou, 0:T], in0=bo[:, 0:T],
                                                                         in1=ri[:, 0:T], op=ALU.mult),
                          reads=[bot, rit], writes=["R%d" % ou])
            prev = None
            for h in range(4):
                pts = head_a(h)
                if prev is not None:
                    head_b(*prev)
                prev = (h, pts)
            head_b(*prev)
        else:
            self.attn_sample(l)
        o_toks = ["R%d" % (16 + c) for c in range(NCH)]
        acc = self.norm_begin(T)
        for cp in range(4):
            w, wt = self.wget()
            for j in range(2):
                c = cp * 2 + j
                bk, bt = self.proj_group(w, wt, j * 128, lambda k: self.R[:, 16 + k, 0:T], o_toks, T)
                S.add("act", lambda e, bk=bk, c=c: e.copy(out=self.m[:, c, 0:T], in_=bk[:, 0:T]), reads=[bt], writes=["m%d" % c])
                self.norm_add(acc, bk[:, 0:T], [bt])
            self.wrel()
        self.post_norm_residual(T, 7 * l + 3, acc)

    def attn_sample(self, l):
        S = self.S
        d = self.d
        T = TS
        for grp in range(2):
            bS, bSt = self.bank(pin=True)
            bO, bOt = self.bank(pin=True)
            ptu = 8 + grp
            PT = self.R[:, ptu, :]
            kvs = []
            for sl in range(8):
                s = grp * 8 + sl
                i = self.kv_rr
                self.kv_rr = (i + 1) % 2
                kt = self.kts[:, i]
                v = self.vs[:, i]
                st, stt = self.stg()
                kst = st.rearrange("p (a f) -> p a f", a=2)
                S.add("sp", lambda e, kst=kst, s=s: e.dma_start(out=kst, in_=d["ck"][l, s].rearrange("(a p) f -> p a f", p=128)),
                      writes=[stt], dma=True)
                S.add("pool", lambda e, v=v, s=s: e.dma_start(out=v, in_=d["cv"][l, s].rearrange("(a p) f -> p a f", p=128)),
                      writes=["V%d" % i], dma=True)
                for dp in range(4):
                    bk, bt = self.bank()

                    def tr(e, bk=bk, dp=dp, kst=kst):
                        ins = None
                        for dj in range(2):
                            dc = dp * 2 + dj
                            for a in range(2):
                                ins = e.transpose(bk[:, dj * 256 + a * 128:dj * 256 + (a + 1) * 128],
                                                  kst[:, a, dc * 128:(dc + 1) * 128], self.ident)
                        return ins
                    S.add("pe", tr, reads=[stt, "ident"], writes=[bt])
                    eng = "act" if dp % 2 == 0 else "dve"
                    if eng == "act":
                        S.add("act", lambda e, bk=bk, dp=dp, kt=kt: e.copy(out=kt[:, 2 * dp:2 * dp + 2, :],
                                                                          in_=bk.rearrange("p (j m) -> p j m", j=2)),
                              reads=[bt], writes=["KT%d" % i])
                    else:
                        S.add("dve", lambda e, bk=bk, dp=dp, kt=kt: e.tensor_copy(out=kt[:, 2 * dp:2 * dp + 2, :],
                                                                                 in_=bk.rearrange("p (j m) -> p j m", j=2)),
                              reads=[bt], writes=["KT%d" % i])
                def mm(e, kt=kt, sl=sl, s=s, bS=bS):
                    ins = None
                    for h in range(4):
                        for mc in range(2):
                            col = mc * 256 + sl * 32 + h * 8
                            for dc in range(2):
                                ins = e.matmul(bS[:, col:col + 8], kt[:, 2 * h + dc, mc * 128:(mc + 1) * 128],
                                               self.R[:, 2 * h + dc, s * 8:(s + 1) * 8], start=(dc == 0), stop=(dc == 1))
                    return ins
                S.add("pe", mm, reads=["KT%d" % i] + ["R%d" % c for c in range(NCH)], writes=[bSt])
                kvs.append((v, "V%d" % i, sl, s))
                S.add("act", lambda e, sl=sl, bS=bS, PT=PT: e.activation(
                    out=PT.rearrange("p (a c) -> p a c", a=2)[:, :, sl * 32:(sl + 1) * 32],
                    in_=bS.rearrange("p (a c) -> p a c", a=2)[:, :, sl * 32:(sl + 1) * 32], func=AF.Exp),
                    reads=[bSt], writes=["R%d" % ptu])
                def mmo(e, v=v, sl=sl, bO=bO, PT=PT):
                    ins = None
                    for h in range(4):
                        for dc in range(2):
                            c = 2 * h + dc
                            for mc in range(2):
                                ins = e.matmul(bO[:, c * 64 + sl * 8:c * 64 + sl * 8 + 8],
                                               v[:, mc, h * 256 + dc * 128:h * 256 + (dc + 1) * 128],
                                               PT[:, mc * 256 + sl * 32 + h * 8:mc * 256 + sl * 32 + h * 8 + 8],
                                               start=(mc == 0), stop=(mc == 1))
                    return ins
                S.add("pe", mmo, reads=["V%d" % i, "R%d" % ptu], writes=[bOt])
            bs, bst = self.bank()

            def mms(e, bs=bs, PT=PT):
                ins = None
                for mc in range(2):
                    ins = e.matmul(bs[:, 0:256], self.ones, PT[:, mc * 256:(mc + 1) * 256], start=(mc == 0), stop=(mc == 1))
                return ins
            S.add("pe", mms, reads=["R%d" % ptu, "ones"], writes=[bst])
            ri, rit = self.tmp()
            S.add("act", lambda e, ri=ri, bs=bs: e.activation(out=ri[:, 0:256], in_=bs[:, 0:256], func=AF.Ln), reads=[bst], writes=[rit])
            S.add("act", lambda e, ri=ri: e.activation(out=ri[:, 0:256], in_=ri[:, 0:256], func=AF.Exp, scale=-1.0),
                  reads=[rit], writes=[rit])
            for h in range(4):
                for dc in range(2):
                    c = 2 * h + dc
                    S.add("dve", lambda e, c=c, h=h, ri=ri, bO=bO, grp=grp: e.tensor_tensor(
                        out=self.R[:, 16 + c, grp * 64:(grp + 1) * 64].rearrange("p (s t) -> p s t", s=8),
                        in0=bO[:, c * 64:(c + 1) * 64].rearrange("p (s t) -> p s t", s=8),
                        in1=ri[:, 0:256].rearrange("p (s h t) -> p s h t", s=8, h=4)[:, :, h, :], op=ALU.mult),
                        reads=[bOt, rit], writes=["R%d" % (16 + c)])
            self.unpin(bSt)
            self.unpin(bOt)

    def ffn(self, tile, l):
        S = self.S
        T = tile["T"]
        xn_toks = ["xn%d" % c for c in range(NCH)]
        rhs_xn = lambda k: self.xn[:, k, 0:T]
        acc0 = self.norm_begin(T)
        for c in range(NCH):
            self.norm_add(acc0, self.x[:, c, 0:T], ["x%d" % c])
            S.add("act", lambda e, c=c: e.activation(out=self.xn[:, c, 0:T], in_=self.x[:, c, 0:T], func=AF.Copy,
                                                     scale=self.vcol("gains", 7 * l + 4, c)),
                  reads=["x%d" % c, "vec"], writes=["xn%d" % c])
        rs, rst = self.norm_finish(acc0, pin=True)
        for jp in range(11):
            wg, wgt = self.wget()
            wu, wut = self.wget()
            for jj in range(2):
                j = jp * 2 + jj
                bg, bgt = self.proj_group(wg, wgt, jj * 128, rhs_xn, xn_toks, T)
                bu, but = self.proj_group(wu, wut, jj * 128, rhs_xn, xn_toks, T)
                sg, sgt = self.tmp()
                S.add("dve", lambda e, sg=sg, bg=bg: e.tensor_tensor(out=sg[:, 0:T], in0=bg[:, 0:T], in1=rs[:, 0:T], op=ALU.mult),
                      reads=[bgt, rst], writes=[sgt])
                S.add("act", lambda e, sg=sg: e.activation(out=sg[:, 0:T], in_=sg[:, 0:T], func=AF.Silu),
                      reads=[sgt], writes=[sgt])
                tu, tut = self.tmp()
                S.add("dve", lambda e, tu=tu, bu=bu: e.tensor_tensor(out=tu[:, 0:T], in0=bu[:, 0:T], in1=rs[:, 0:T], op=ALU.mult),
                      reads=[but, rst], writes=[tut])
                S.add("dve", lambda e, sg=sg, tu=tu, j=j: e.tensor_tensor(out=self.R[:, j, 0:T], in0=tu[:, 0:T], in1=sg[:, 0:T],
                                                                         op=ALU.mult), reads=[tut, sgt], writes=["R%d" % j])
            self.wrel(2)
        self.tmp_unpin(rst)
        h_toks = ["R%d" % j for j in range(NJ)]
        acc = self.norm_begin(T)
        for c in range(NCH):
            w0, wt0 = self.wget()
            w1, wt1 = self.wget()
            bk, bt = self.bank()

            def mm(e, bk=bk, w0=w0, w1=w1):
                ins = None
                for k in range(NJ):
                    w = w0 if k < 11 else w1
                    ins = e.matmul(bk[:, 0:T], w[:, k % 11, :], self.R[:, k, 0:T], start=(k == 0), stop=(k == NJ - 1))
                return ins
            S.add("pe", mm, reads=h_toks + [wt0, wt1], writes=[bt])
            S.add("act", lambda e, bk=bk, c=c: e.copy(out=self.m[:, c, 0:T], in_=bk[:, 0:T]), reads=[bt], writes=["m%d" % c])
            self.norm_add(acc, bk[:, 0:T], [bt])
            self.wrel(2)
        if l == DEPTH - 1 and tile.get("next") is not None:
            tile["next"]["staged"] = self.load_x_dma(tile["next"], use_R=True)
        self.post_norm_residual(T, 7 * l + 5, acc)

    def build(self):
        self.plan_weights()
        tiles = []
        for i in range(NPT):
            tiles.append(dict(kind="p", T=TP, nseq=1, L=TP, first=(i == 0), last=(i == NPT - 1),
                              xsrc=self.d["xp"][i * TP:(i + 1) * TP, :], ydst=self.d["yp"][i * TP:(i + 1) * TP, :]))
        tiles.append(dict(kind="s", T=TS, nseq=NSEQ_S, L=L_S, xsrc=self.d["xs"], ydst=self.d["ys"]))
        for ti in range(len(tiles) - 1):
            tiles[ti]["next"] = tiles[ti + 1]
        self.prologue(tiles[0])
        for ti, tile in enumerate(tiles):
            self.aux = "dve" if ti <= 1 else "pool"
            if ti > 0:
                self.load_x_tr(tile, tile["staged"])
            for l in range(DEPTH):
                self.mix(tile, l)
                self.attn(tile, l)
                self.ffn(tile, l)
            self.store_y(tile)
        assert self.w_cons == len(self.wplan), (self.w_cons, len(self.wplan))
        self.S.emit()
        return self.nc


_CACHE = {}


def _get_program():
    if "nc" not in _CACHE:
        _CACHE["nc"] = Builder().build()
    return _CACHE["nc"]


def kernel(x_prompt, x_sample, state_conv_a, state_conv_b, state_rglru, cache_mem_k, cache_mem_v, mem_prompt,
           norm_gains, w_in, conv_a_w, w_conv_out, conv_b_w, conv_b_b, w_gate_a, b_gate_a, w_gate_x, b_gate_x,
           lru_lambda, w_rnn_out, w_mix_out, w_kv_x, w_q_x, w_o_x, w_ffn_in, w_ffn_out):
    f = lambda a: np.ascontiguousarray(np.asarray(a, dtype=np.float32))
    x_prompt, x_sample = f(x_prompt), f(x_sample)
    state_conv_a, state_conv_b, state_rglru = f(state_conv_a), f(state_conv_b), f(state_rglru)
    cache_mem_k, cache_mem_v, mem_prompt = f(cache_mem_k), f(cache_mem_v), f(mem_prompt)
    n = 8
    shared = {
        "gains": f(norm_gains).reshape(14, D), "caw": f(conv_a_w).reshape(6, D), "cbw": f(conv_b_w).reshape(8, D),
        "cbb": f(conv_b_b), "bga": f(b_gate_a), "bgx": f(b_gate_x), "lam": f(lru_lambda),
        "ident": np.eye(128, dtype=np.float32),
        "w_in": f(w_in), "w_conv_out": f(w_conv_out), "w_gate_a": f(w_gate_a), "w_gate_x": f(w_gate_x),
        "w_rnn_out": f(w_rnn_out), "w_mix_out": f(w_mix_out), "w_kv": f(w_kv_x), "w_q": f(w_q_x), "w_o": f(w_o_x),
        "w_ffn_in": f(w_ffn_in), "w_ffn_out": f(w_ffn_out),
    }
    in_maps = []
    for b in range(n):
        sl = slice(b * NSEQ_S, (b + 1) * NSEQ_S)
        m = dict(shared)
        m["xp"] = x_prompt[b]
        m["xs"] = x_sample[sl].reshape(TS, D)
        m["sta"] = np.ascontiguousarray(state_conv_a[:, sl]).reshape(DEPTH, NSEQ_S * 2, D)
        m["stb"] = np.ascontiguousarray(state_conv_b[:, sl]).reshape(DEPTH, NSEQ_S * 3, D)
        m["sth"] = np.ascontiguousarray(state_rglru[:, sl]).reshape(DEPTH, NSEQ_S, D)
        m["ck"] = np.ascontiguousarray(cache_mem_k[:, sl]).reshape(DEPTH, NSEQ_S, NMEM, D)
        m["cv"] = np.ascontiguousarray(cache_mem_v[:, sl]).reshape(DEPTH, NSEQ_S, NMEM, D)
        m["memp"] = mem_prompt[b]
        in_maps.append(m)
    nc = _get_program()
    res = run_bass_kernel_spmd(nc, in_maps, core_ids=list(range(n)))
    r = res.results
    y_prompt = np.stack([r[b]["yp"] for b in range(n)], axis=0)
    y_sample = np.concatenate([r[b]["ys"].reshape(NSEQ_S, L_S, D) for b in range(n)], axis=0)
    p_conv_a = np.stack([r[b]["pca"] for b in range(n)], axis=1)
    p_conv_b = np.stack([r[b]["pcb"] for b in range(n)], axis=1)
    p_rglru = np.stack([r[b]["ph"].reshape(DEPTH, D) for b in range(n)], axis=1)
    p_mem_k = np.stack([r[b]["pk"].reshape(DEPTH, NMEM, 4, 256) for b in range(n)], axis=1)
    p_mem_v = np.stack([r[b]["pv"].reshape(DEPTH, NMEM, 4, 256) for b in range(n)], axis=1)
    s_conv_a = np.concatenate([r[b]["sca"].reshape(DEPTH, NSEQ_S, 2, D) for b in range(n)], axis=1)
    s_conv_b = np.concatenate([r[b]["scb"].reshape(DEPTH, NSEQ_S, 3, D) for b in range(n)], axis=1)
    s_rglru = np.concatenate([r[b]["sh"].reshape(DEPTH, NSEQ_S, D) for b in range(n)], axis=1)
    return (y_prompt.astype(np.float32), y_sample.astype(np.float32), p_conv_a.astype(np.float32),
            p_conv_b.astype(np.float32), p_rglru.astype(np.float32), p_mem_k.astype(np.float32),
            p_mem_v.astype(np.float32), s_conv_a.astype(np.float32), s_conv_b.astype(np.float32),
            s_rglru.astype(np.float32))
```

```python
import contextlib
import numpy as np
import concourse.bass as bass
import concourse.mybir as mybir
from concourse.bass_utils import run_bass_kernel_spmd

F32 = mybir.dt.float32
BF16 = mybir.dt.bfloat16
ALU = mybir.AluOpType
AF = mybir.ActivationFunctionType

ENGS = ("pe", "act", "dve", "pool", "sp")
N_DMA_SEMS = 12

D = 1024
NCH = 8
DFF = 2816
NJ = 22
DEPTH = 2
SEQ = 2048
TP = 512
NPT = SEQ // TP
NSEQ_S = 16
L_S = 8
TS = NSEQ_S * L_S
NMEM = 256
EPS = 1e-6


class Sched:
    def __init__(self, nc):
        self.nc = nc
        self.ops = {e: [] for e in ENGS}
        self.cnt = {e: 0 for e in ENGS}
        self.seen = {e: {} for e in ENGS}
        self.last_w = {}
        self.readers = {}
        self.dma_cnt = [0] * N_DMA_SEMS
        self.dma_rr = {"pool": 0, "sp": 0}

    def _need(self, deps, eng, ev, raw):
        semkey, val, src = ev
        if src == eng:
            if eng == "pe":
                return
        if deps.get(semkey, 0) < val:
            deps[semkey] = val

    def add(self, eng, fn, reads=(), writes=(), dma=False):
        deps = {}
        deng = None if dma else eng
        for t in reads:
            for ev in self.last_w.get(t, ()):
                self._need(deps, deng, ev, True)
        for t in writes:
            for ev in self.last_w.get(t, ()):
                self._need(deps, deng, ev, False)
            for sk, (v, src) in self.readers.get(t, {}).items():
                self._need(deps, deng, (sk, v, src), False)
        if dma:
            half = N_DMA_SEMS // 2
            k = self.dma_rr[eng]
            self.dma_rr[eng] = (k + 1) % half
            i = k + (0 if eng == "pool" else half)
            c = self.dma_cnt[i] + 1
            self.dma_cnt[i] = c
            if c > 1:
                sk = ("dma", i)
                if deps.get(sk, 0) < 16 * (c - 1):
                    deps[sk] = 16 * (c - 1)
            ev = (("dma", i), 16 * c, None)
        else:
            self.cnt[eng] += 1
            ev = (eng, self.cnt[eng], eng)
        waits = []
        seen = self.seen[eng]
        for sk, v in deps.items():
            if seen.get(sk, 0) < v:
                seen[sk] = v
                waits.append((sk, v))
        self.ops[eng].append((fn, waits, ev[0]))
        for t in writes:
            prev = self.last_w.get(t)
            if dma and prev and all(p[2] is None for p in prev):
                self.last_w[t] = (prev + [ev])[-8:]
            else:
                self.last_w[t] = [ev]
            self.readers[t] = {}
        for t in reads:
            if t in writes:
                continue
            r = self.readers.setdefault(t, {})
            old = r.get(ev[0])
            if old is None or old[0] < ev[1]:
                r[ev[0]] = (ev[1], ev[2])
        return ev

    def emit(self):
        nc = self.nc
        waits = []
        for i, c in enumerate(self.dma_cnt):
            if c > 0 and self.seen["sp"].get(("dma", i), 0) < 16 * c:
                waits.append((("dma", i), 16 * c))
        self.ops["sp"].append((None, waits, None))
        with contextlib.ExitStack() as st:
            sems = {}
            for e in ENGS[:4]:
                sems[e] = st.enter_context(nc.semaphore("sem_" + e))
            for i in range(N_DMA_SEMS):
                sems[("dma", i)] = st.enter_context(nc.semaphore("sem_dma%d" % i))
            block = st.enter_context(nc.Block())

            def run(engobj, name):
                for fn, waits, inc in self.ops[name]:
                    for sk, v in waits:
                        engobj.wait_ge(sems[sk], v)
                    if fn is None:
                        continue
                    ins = fn(engobj)
                    if inc is not None:
                        ins.then_inc(sems[inc], 16 if isinstance(inc, tuple) else 1)

            @block.tensor
            def _(e):
                run(e, "pe")

            @block.scalar
            def _(e):
                run(e, "act")

            @block.vector
            def _(e):
                run(e, "dve")

            @block.gpsimd
            def _(e):
                run(e, "pool")

            @block.sync
            def _(e):
                run(e, "sp")


VEC_ROWS = {"gains": (0, 14), "caw": (14, 6), "cbw": (20, 8), "cbb": (28, 2), "bga": (30, 2),
            "bgx": (32, 2), "lam": (34, 2)}
NVEC = 36
NW = 10
WSLOT = 2048
NTMP = 24
TMPW = 528


class Builder:
    def __init__(self):
        nc = bass.Bass("TRN2", target_bir_lowering=False)
        self.nc = nc
        self.S = Sched(nc)
        S = self.S

        def din(name, shape):
            return nc.dram_tensor(name, list(shape), F32, kind="ExternalInput").ap()

        def dout(name, shape):
            return nc.dram_tensor(name, list(shape), F32, kind="ExternalOutput").ap()

        self.d = {}
        d = self.d
        d["xp"] = din("xp", [SEQ, D])
        d["xs"] = din("xs", [TS, D])
        d["sta"] = din("sta", [DEPTH, NSEQ_S * 2, D])
        d["stb"] = din("stb", [DEPTH, NSEQ_S * 3, D])
        d["sth"] = din("sth", [DEPTH, NSEQ_S, D])
        d["ck"] = din("ck", [DEPTH, NSEQ_S, NMEM, D])
        d["cv"] = din("cv", [DEPTH, NSEQ_S, NMEM, D])
        d["memp"] = din("memp", [NMEM, D])
        d["gains"] = din("gains", [14, D])
        d["caw"] = din("caw", [6, D])
        d["cbw"] = din("cbw", [8, D])
        d["cbb"] = din("cbb", [2, D])
        d["bga"] = din("bga", [2, D])
        d["bgx"] = din("bgx", [2, D])
        d["lam"] = din("lam", [2, D])
        d["ident"] = din("ident", [128, 128])
        d["w_in"] = din("w_in", [DEPTH, D, 6 * D])
        d["w_conv_out"] = din("w_conv_out", [DEPTH, D, D])
        d["w_gate_a"] = din("w_gate_a", [DEPTH, 16, 64, 64])
        d["w_gate_x"] = din("w_gate_x", [DEPTH, 16, 64, 64])
        d["w_rnn_out"] = din("w_rnn_out", [DEPTH, D, D])
        d["w_mix_out"] = din("w_mix_out", [DEPTH, D, D])
        d["w_kv"] = din("w_kv", [DEPTH, D, 2 * D])
        d["w_q"] = din("w_q", [DEPTH, D, D])
        d["w_o"] = din("w_o", [DEPTH, D, D])
        d["w_ffn_in"] = din("w_ffn_in", [DEPTH, D, 2 * DFF])
        d["w_ffn_out"] = din("w_ffn_out", [DEPTH, DFF, D])
        d["yp"] = dout("yp", [SEQ, D])
        d["ys"] = dout("ys", [TS, D])
        d["pca"] = dout("pca", [DEPTH, 2, D])
        d["pcb"] = dout("pcb", [DEPTH, 3, D])
        d["ph"] = dout("ph", [DEPTH, 1, D])
        d["pk"] = dout("pk", [DEPTH, NMEM, D])
        d["pv"] = dout("pv", [DEPTH, NMEM, D])
        d["sca"] = dout("sca", [DEPTH, NSEQ_S * 2, D])
        d["scb"] = dout("scb", [DEPTH, NSEQ_S * 3, D])
        d["sh"] = dout("sh", [DEPTH, NSEQ_S, D])

        def sb(name, shape, dt=F32):
            return nc.alloc_sbuf_tensor("sb_" + name, list(shape), dt).ap()

        self.ps = nc.alloc_psum_tensor("ps", [128, 8, 512], F32).ap()
        self.bank_rr = 0
        self.pinned = set()
        self.ident = sb("ident", [128, 128])
        self.ones = sb("ones", [128, 128], BF16)
        self.cst = sb("cst", [128, 8])
        self.vec = sb("vec", [128, NCH, NVEC])
        self.der = sb("der", [128, NCH, 12])
        self.dtmp = sb("dtmp", [128, NCH, 8])
        self.bd = sb("bd", [128, DEPTH * 2, NCH, 128], BF16)
        self.KT = sb("KT", [128, DEPTH, NCH, NMEM], BF16)
        self.V = sb("V", [128, DEPTH, 2, D], BF16)
        self.x = sb("x", [128, NCH, TP])
        self.xn = sb("xn", [128, NCH, TP], BF16)
        self.R = sb("R", [128, 24, TP], BF16)
        self.m = sb("m", [128, NCH, TP])
        self.tmpf = sb("tmpf", [128, NTMP, TMPW])
        self.tmp_rr = 0
        self.kcnt = {}
        self.tmp_pinned = set()
        self.aux = "dve"
        self.tmpb_ = sb("tmpb", [128, 4, TP], BF16)
        self.tmpb_rr = 0
        self.wring = sb("wring", [128, NW, WSLOT], BF16)
        self.stage = sb("stage", [128, 2, 2048])
        self.stage_rr = 0
        self.kts = self.KT
        self.vs = self.V
        self.kv_rr = 0
        self.carA = sb("carA", [128, DEPTH, NCH, 2])
        self.carB = sb("carB", [128, DEPTH, NCH, 3])
        self.carH = sb("carH", [128, DEPTH, NCH, 1])
        self.stA = sb("stA", [128, NCH, NSEQ_S * 2])
        self.stB = sb("stB", [128, NCH, NSEQ_S * 3])
        self.stH = sb("stH", [128, NCH, NSEQ_S])
        self.oA = sb("oA", [128, NCH, NSEQ_S * 2])
        self.oB = sb("oB", [128, NCH, NSEQ_S * 3])
        self.oH = sb("oH", [128, NCH, NSEQ_S])
        self.wplan = []
        self.w_next_load = 0
        self.w_cons = 0
        self.w_released = 0

    def bank(self, pin=False):
        while self.bank_rr in self.pinned:
            self.bank_rr = (self.bank_rr + 1) % 8
        i = self.bank_rr
        self.bank_rr = (i + 1) % 8
        if pin:
            self.pinned.add(i)
        return self.ps[:, i, :], "ps%d" % i

    def unpin(self, tok):
        self.pinned.discard(int(tok[2:]))

    def tmp(self, pin=False):
        while self.tmp_rr in self.tmp_pinned:
            self.tmp_rr = (self.tmp_rr + 1) % NTMP
        i = self.tmp_rr
        self.tmp_rr = (i + 1) % NTMP
        if pin:
            self.tmp_pinned.add(i)
        return self.tmpf[:, i, :], "tf%d" % i

    def tmp_unpin(self, tok):
        self.tmp_pinned.discard(int(tok[2:]))

    KINDS = {"hcs": (0, 1), "G": (1, 2), "ya": (3, 1), "U": (4, 2), "uc": (6, 3), "tr": (9, 1), "ti": (10, 2),
             "a": (12, 2), "a2": (14, 2), "iu": (16, 2), "hs": (18, 2), "t0": (20, 1)}

    def ktmp(self, kind):
        base, depth = self.KINDS[kind]
        k = self.kcnt.get(kind, 0)
        self.kcnt[kind] = k + 1
        i = base + k % depth
        return self.tmpf[:, i, :], "tf%d" % i

    def tmpb(self):
        i = self.tmpb_rr
        self.tmpb_rr = (i + 1) % 4
        return self.tmpb_[:, i, :], "tb%d" % i

    def stg(self):
        i = self.stage_rr
        self.stage_rr = (i + 1) % 2
        return self.stage[:, i, :], "stg%d" % i

    def vcol(self, name, row, c):
        base = VEC_ROWS[name][0] + row
        return self.vec[:, c, base:base + 1]

    def plan_weights(self):
        d = self.d
        plan = []

        def blk(w, l, c0, n):
            return (w[l, :, c0:c0 + n].rearrange("(c p) n -> p c n", p=128), int(w.shape[1]) // 128, n)

        def layer_blocks(l):
            out = []
            for cp in range(4):
                for sec in (0, 1, 2, 3):
                    out.append(blk(d["w_in"], l, sec * D + cp * 256, 256))
            for cp in range(4):
                out.append(blk(d["w_conv_out"], l, cp * 256, 256))
                out.append(blk(d["w_rnn_out"], l, cp * 256, 256))
                out.append(blk(d["w_in"], l, 4 * D + cp * 256, 256))
                out.append(blk(d["w_in"], l, 5 * D + cp * 256, 256))
            for cp in range(4):
                out.append(blk(d["w_mix_out"], l, cp * 256, 256))
            for cp in range(4):
                out.append(blk(d["w_q"], l, cp * 256, 256))
            for cp in range(4):
                out.append(blk(d["w_o"], l, cp * 256, 256))
            for jp in range(11):
                out.append(blk(d["w_ffn_in"], l, jp * 256, 256))
                out.append(blk(d["w_ffn_in"], l, DFF + jp * 256, 256))
            for c in range(8):
                for hf in range(2):
                    out.append((d["w_ffn_out"][l, hf * 1408:(hf + 1) * 1408, c * 128:(c + 1) * 128]
                                .rearrange("(c p) n -> p c n", p=128), 11, 128))
            return out

        for l in range(DEPTH):
            for b in range(8):
                plan.append(blk(d["w_kv"], l, b * 256, 256) + (None, True))
        for t in range(NPT + 1):
            i = 0
            for l in range(DEPTH):
                for b_ in layer_blocks(l):
                    plan.append(b_ + (i, t == 0))
                    i += 1
        self.n_scr = i
        self.wscr = self.nc.dram_tensor("wscr", [self.n_scr, 128, WSLOT], BF16, kind="Internal").ap()
        self.wplan = plan

    def _w_prefetch(self):
        while self.w_next_load < len(self.wplan) and self.w_next_load < self.w_released + NW:
            j = self.w_next_load
            src, K, N, scr, first = self.wplan[j]
            slot = j % NW
            if first:
                dst = self.wring[:, slot, 0:K * N].rearrange("p (c n) -> p c n", c=K)
                self.S.add("pool", lambda e, dst=dst, src=src: e.dma_start(out=dst, in_=src),
                           writes=["w%d" % slot], dma=True)
                if scr is not None:
                    self.S.add("sp", lambda e, slot=slot, scr=scr, n=K * N: e.dma_start(
                        out=self.wscr[scr, :, 0:n], in_=self.wring[:, slot, 0:n]),
                        reads=["w%d" % slot], writes=["scr%d" % scr], dma=True)
            else:
                self.S.add("sp", lambda e, slot=slot, scr=scr, n=K * N: e.dma_start(
                    out=self.wring[:, slot, 0:n], in_=self.wscr[scr, :, 0:n]),
                    reads=["scr%d" % scr], writes=["w%d" % slot], dma=True)
            self.w_next_load += 1

    def wget(self):
        j = self.w_cons
        self.w_cons += 1
        assert j < self.w_released + NW, "weight ring too small"
        self._w_prefetch()
        assert self.w_next_load > j
        src, K, N, scr, first = self.wplan[j]
        slot = j % NW
        ap = self.wring[:, slot, 0:K * N].rearrange("p (c n) -> p c n", c=K)
        return ap, "w%d" % slot

    def wrel(self, n=1):
        self.w_released += n
        self._w_prefetch()

    def load_tm_to_fm(self, src_rows, R, dst_fn, dst_tokens, evac="act", scale_fn=None):
        S = self.S
        st, stt = self.stg()
        S.add("sp", lambda e: e.dma_start(out=st[0:R, 0:D], in_=src_rows), writes=[stt], dma=True)
        self.tm_to_fm(st, stt, R, dst_fn, dst_tokens, evac, scale_fn)

    def tm_to_fm(self, st, stt, R, dst_fn, dst_tokens, evac="act", scale_fn=None):
        S = self.S
        stts = list(stt) if isinstance(stt, (list, tuple)) else [stt]
        for half in range(2):
            bk, bt = self.bank()

            def tr(e, half=half, bk=bk):
                ins = None
                for j in range(4):
                    c = half * 4 + j
                    ins = e.transpose(bk[:, j * R:(j + 1) * R], st[0:R, c * 128:(c + 1) * 128], self.ident[0:R, 0:R])
                return ins
            S.add("pe", tr, reads=stts + ["ident"], writes=[bt])
            if scale_fn is None:
                dst = dst_fn(half * 4, 4)
                src = bk[:, 0:4 * R].rearrange("p (j r) -> p j r", j=4)
                if evac == "act":
                    S.add("act", lambda e, dst=dst, src=src: e.copy(out=dst, in_=src), reads=[bt], writes=dst_tokens)
                else:
                    S.add("dve", lambda e, dst=dst, src=src: e.tensor_copy(out=dst, in_=src), reads=[bt], writes=dst_tokens)
            else:
                for j in range(4):
                    c = half * 4 + j
                    dst = dst_fn(c, 1)
                    S.add("act", lambda e, dst=dst, j=j, bk=bk, c=c: e.activation(
                        out=dst, in_=bk[:, j * R:(j + 1) * R].rearrange("p (j r) -> p j r", j=1), func=AF.Copy,
                        scale=scale_fn(c)), reads=[bt, "vec"], writes=dst_tokens)

    def store_fm_to_tm(self, src_fn, src_tokens, R, dst_rows):
        S = self.S
        st, stt = self.stg()
        for half in range(2):
            bk, bt = self.bank()

            def tr(e, half=half, bk=bk):
                ins = None
                for j in range(4):
                    c = half * 4 + j
                    ins = e.transpose(bk[0:R, j * 128:(j + 1) * 128], src_fn(c), self.ident)
                return ins
            S.add("pe", tr, reads=list(src_tokens) + ["ident"], writes=[bt])
            S.add("act", lambda e, half=half, bk=bk: e.copy(out=st[0:R, half * 512:(half + 1) * 512], in_=bk[0:R, :]),
                  reads=[bt], writes=[stt])
        S.add("sp", lambda e: e.dma_start(out=dst_rows, in_=st[0:R, 0:D]), reads=[stt], dma=True)

    def prologue(self, tile0):
        S = self.S
        d = self.d
        nc = self.nc
        S.add("sp", lambda e: e.dma_start(out=self.ident, in_=d["ident"]), writes=["ident"], dma=True)
        S.add("dve", lambda e: e.memset(self.ones, 1.0), writes=["ones"])
        S.add("dve", lambda e: e.memset(self.cst[:, 0:1], EPS), writes=["cst"])
        S.add("dve", lambda e: e.memset(self.cst[:, 1:2], 1.0), writes=["cst"])
        S.add("dve", lambda e: e.memset(self.cst[:, 2:3], 0.0), writes=["cst"])
        S.add("dve", lambda e: e.memset(self.cst[:, 3:4], -0.5), writes=["cst"])
        S.add("dve", lambda e: e.memset(self.cst[:, 4:5], 0.5), writes=["cst"])
        S.add("dve", lambda e: e.memset(self.cst[:, 5:6], -1.0), writes=["cst"])
        S.add("pool", lambda e: e.memset(self.bd, 0.0), writes=["bd"])
        st, stt = self.stg()
        for name, (r0, n) in VEC_ROWS.items():
            S.add("sp", lambda e, name=name, r0=r0, n=n: e.dma_start(out=st[r0:r0 + n, 0:D], in_=d[name]),
                  writes=[stt], dma=True)
        self.tm_to_fm(st, stt, NVEC, lambda c0, n: self.vec[:, c0:c0 + n, :], ["vec"], evac="dve")
        for l in range(DEPTH):
            for g, wname in enumerate(("w_gate_a", "w_gate_x")):
                for hh in range(2):
                    src = d[wname][l].rearrange("(c h) k j -> h k c j", h=2)[hh]
                    dst = self.bd[hh * 64:(hh + 1) * 64, l * 2 + g, :, hh * 64:(hh + 1) * 64]
                    S.add("pool", lambda e, dst=dst, src=src: e.dma_start(out=dst, in_=src),
                          reads=[], writes=["bd"], dma=True)
        for l in range(DEPTH):
            lam = self.vec[:, :, VEC_ROWS["lam"][0] + l]
            bga = self.vec[:, :, VEC_ROWS["bga"][0] + l]
            bgx = self.vec[:, :, VEC_ROWS["bgx"][0] + l]
            t_abs = self.dtmp[:, :, 0]
            t_e = self.dtmp[:, :, 1]
            t_l = self.dtmp[:, :, 2]
            t_r = self.dtmp[:, :, 3]
            S.add("dve", lambda e, bga=bga, l=l: e.tensor_scalar_mul(out=self.der[:, :, 4 * l + 0], in0=bga, scalar1=-1.0),
                  reads=["vec"], writes=["der"])
            S.add("dve", lambda e, bgx=bgx, l=l: e.tensor_scalar_mul(out=self.der[:, :, 4 * l + 1], in0=bgx, scalar1=-1.0),
                  reads=["vec"], writes=["der"])
            S.add("act", lambda e, lam=lam: e.activation(out=t_abs, in_=lam, func=AF.Abs), reads=["vec"], writes=["dt0"])
            S.add("act", lambda e: e.activation(out=t_e, in_=t_abs, func=AF.Exp, scale=-1.0), reads=["dt0"], writes=["dt1"])
            S.add("act", lambda e: e.activation(out=t_l, in_=t_e, func=AF.Ln, bias=self.cst[:, 1:2]),
                  reads=["dt1", "cst"], writes=["dt2"])
            S.add("dve", lambda e, lam=lam: e.tensor_scalar(out=t_r, in0=lam, scalar1=-1.0, scalar2=0.0,
                                                            op0=ALU.mult, op1=ALU.max), reads=["vec"], writes=["dt3"])
            S.add("dve", lambda e: e.tensor_tensor(out=t_r, in0=t_r, in1=t_l, op=ALU.add), reads=["dt3", "dt2"], writes=["dt3"])
            S.add("dve", lambda e, l=l: e.tensor_scalar_mul(out=self.der[:, :, 4 * l + 2], in0=t_r, scalar1=-8.0),
                  reads=["dt3"], writes=["der"])
            S.add("dve", lambda e, l=l: e.tensor_scalar_mul(out=self.der[:, :, 4 * l + 3], in0=t_r, scalar1=-16.0),
                  reads=["dt3"], writes=["der"])
        S.add("pool", lambda e: e.memset(self.carA, 0.0), writes=["carA%d" % c_ for c_ in range(NCH)])
        S.add("pool", lambda e: e.memset(self.carB, 0.0), writes=["carB%d" % c_ for c_ in range(NCH)])
        S.add("pool", lambda e: e.memset(self.carH, 0.0), writes=["carH%d" % c_ for c_ in range(NCH)])
        self.load_x_direct(tile0)
        tile0["staged"] = []
        self.mem_kv()

    def mem_kv(self):
        S = self.S
        d = self.d
        memt = self.stage.rearrange("p a f -> p (a f)")[:, 0:2 * D].rearrange("p (a f) -> p a f", a=2)
        S.add("sp", lambda e: e.dma_start(out=memt, in_=d["memp"].rearrange("(a p) f -> p a f", p=128)),
              writes=["stg0"], dma=True)
        for a in range(2):
            for hf in range(2):
                t, tt = self.tmp()
                S.add("act", lambda e, a=a, hf=hf, t=t: e.activation(
                    out=t[:, 0:512], in_=memt[:, a, hf * 512:(hf + 1) * 512], func=AF.Square),
                    reads=["stg0"], writes=[tt])
                S.add("dve", lambda e, a=a, hf=hf, t=t: e.reduce_sum(
                    out=self.dtmp[:, 0, 4 + 2 * a + hf:5 + 2 * a + hf], in_=t[:, 0:512], axis=mybir.AxisListType.X),
                    reads=[tt], writes=["ssq%d%d" % (a, hf)])
        rs = self.dtmp[:, 1, 0:2]
        S.add("dve", lambda e: e.tensor_tensor(out=self.dtmp[:, 1, 2:4].rearrange("p (a o) -> p a o", o=1),
                                               in0=self.dtmp[:, 0, 4:8].rearrange("p (a h) -> p a h", h=2)[:, :, 0:1],
                                               in1=self.dtmp[:, 0, 4:8].rearrange("p (a h) -> p a h", h=2)[:, :, 1:2],
                                               op=ALU.add),
              reads=["ssq00", "ssq01", "ssq10", "ssq11"], writes=["ssum"])
        S.add("act", lambda e: e.activation(out=self.dtmp[:, 1, 4:6], in_=self.dtmp[:, 1, 2:4], func=AF.Sqrt,
                                            scale=1.0 / D, bias=self.cst[:, 0:1]), reads=["ssum", "cst"], writes=["srt"])
        S.add("dve", lambda e: e.reciprocal(out=rs, in_=self.dtmp[:, 1, 4:6]), reads=["srt"], writes=["mrs"])
        for a in range(2):
            S.add("dve", lambda e, a=a: e.tensor_scalar(out=memt[:, a, :], in0=memt[:, a, :], scalar1=rs[:, a:a + 1],
                                                        scalar2=None, op0=ALU.mult), reads=["mrs", "stg0"], writes=["stg0"])
        mT0 = self.m.rearrange("p c t -> p (c t)")[:, 0:NCH * NMEM].rearrange("p (c t) -> p c t", c=NCH)
        for a in range(2):
            for half in range(2):
                bk, bt = self.bank()

                def tr(e, a=a, half=half, bk=bk):
                    ins = None
                    for j in range(4):
                        c = half * 4 + j
                        ins = e.transpose(bk[:, j * 128:(j + 1) * 128], memt[:, a, c * 128:(c + 1) * 128], self.ident)
                    return ins
                S.add("pe", tr, reads=["stg0", "ident"], writes=[bt])
                S.add("act", lambda e, a=a, half=half, bk=bk: e.copy(
                    out=mT0[:, half * 4:half * 4 + 4, a * 128:(a + 1) * 128],
                    in_=bk.rearrange("p (j r) -> p j r", j=4)), reads=[bt], writes=["m%d" % c_ for c_ in range(NCH)])
        mTl = self.xn.rearrange("p c t -> p (c t)")[:, 0:NCH * NMEM].rearrange("p (c t) -> p c t", c=NCH)
        for l in range(DEPTH):
            for c in range(NCH):
                S.add("dve", lambda e, c=c, l=l: e.tensor_scalar(out=mTl[:, c, :], in0=mT0[:, c, :],
                                                                 scalar1=self.vcol("gains", 7 * l + 6, c), scalar2=None,
                                                                 op0=ALU.mult), reads=["m%d" % c_ for c_ in range(NCH)] + ["vec"], writes=["xn%d" % c_ for c_ in range(NCH)])
            for b in range(8):
                w, wt = self.wget()
                for a in range(2):
                    bk, bt = self.bank()

                    def mm(e, a=a, bk=bk, w=w):
                        ins = None
                        for k in range(NCH):
                            ins = e.matmul(bk[:, 0:256], mTl[:, k, a * 128:(a + 1) * 128], w[:, k, :],
                                           start=(k == 0), stop=(k == NCH - 1))
                        return ins
                    S.add("pe", mm, reads=["xn%d" % c_ for c_ in range(NCH)] + [wt], writes=[bt])
                    t, tt = self.tmp()
                    S.add("act", lambda e, t=t, bk=bk: e.copy(out=t[:, 0:256], in_=bk[:, 0:256]), reads=[bt], writes=[tt])
                    if b < 4:
                        dst = d["pk"][l, a * 128:(a + 1) * 128, b * 256:(b + 1) * 256]
                    else:
                        dst = d["pv"][l, a * 128:(a + 1) * 128, (b - 4) * 256:(b - 3) * 256]
                        S.add("dve", lambda e, t=t, a=a, b=b, l=l: e.tensor_copy(
                            out=self.V[:, l, a, (b - 4) * 256:(b - 3) * 256], in_=t[:, 0:256]), reads=[tt], writes=["V%d" % l])
                    S.add("sp", lambda e, dst=dst, t=t: e.dma_start(out=dst, in_=t[:, 0:256]), reads=[tt], dma=True)
                if b < 4:
                    for j in range(2):
                        bk, bt = self.bank()

                        def mm2(e, j=j, bk=bk, w=w):
                            ins = None
                            for k in range(NCH):
                                ins = e.matmul(bk[:, 0:256], w[:, k, j * 128:(j + 1) * 128], mTl[:, k, :],
                                               start=(k == 0), stop=(k == NCH - 1))
                            return ins
                        S.add("pe", mm2, reads=["xn%d" % c_ for c_ in range(NCH)] + [wt], writes=[bt])
                        S.add("act", lambda e, j=j, b=b, l=l, bk=bk: e.copy(out=self.KT[:, l, 2 * b + j, :], in_=bk[:, 0:256]),
                              reads=[bt], writes=["KT%d" % l])
                self.wrel()

    def rstd_from(self, srcs, T, scale=1.0):
        S = self.S
        bk, bt = self.bank()
        for c, (src, stoks) in enumerate(srcs):
            sq, sqt = self.tmpb()
            S.add("act", lambda e, sq=sq, src=src: e.activation(out=sq[:, 0:T], in_=src, func=AF.Square, scale=scale),
                  reads=stoks, writes=[sqt])
            S.add("pe", lambda e, sq=sq, c=c, bk=bk: e.matmul(bk[:, 0:T], self.ones, sq[:, 0:T], start=(c == 0),
                                                             stop=(c == NCH - 1)), reads=[sqt, "ones"], writes=[bt])
        ms, mst = self.tmp()
        S.add("act", lambda e: e.activation(out=ms[:, 0:T], in_=bk[:, 0:T], func=AF.Ln, scale=1.0 / D,
                                            bias=self.cst[:, 0:1]), reads=[bt, "cst"], writes=[mst])
        rs, rst = self.tmp()
        S.add("act", lambda e: e.activation(out=rs[:, 0:T], in_=ms[:, 0:T], func=AF.Exp, scale=-0.5),
              reads=[mst], writes=[rst])
        return rs, rst

    def norm_begin(self, T):
        bk, bt = self.bank(pin=True)
        return dict(bk=bk, bt=bt, T=T, n=0)

    def norm_add(self, acc, src, stoks, lag=True):
        S = self.S
        T, bk, bt, c = acc["T"], acc["bk"], acc["bt"], acc["n"]
        acc["n"] = c + 1
        sq, sqt = self.tmpb()
        S.add("act", lambda e: e.activation(out=sq[:, 0:T], in_=src, func=AF.Square), reads=stoks, writes=[sqt])

        def emit_pe():
            S.add("pe", lambda e: e.matmul(bk[:, 0:T], self.ones, sq[:, 0:T], start=(c == 0), stop=(c == NCH - 1)),
                  reads=[sqt, "ones"], writes=[bt])
        prev = acc.get("pend")
        if prev is not None:
            prev()
        if lag:
            acc["pend"] = emit_pe
        else:
            acc["pend"] = None
            emit_pe()

    def norm_finish(self, acc, pin=False):
        S = self.S
        T, bk, bt = acc["T"], acc["bk"], acc["bt"]
        assert acc["n"] == NCH
        if acc.get("pend") is not None:
            acc["pend"]()
            acc["pend"] = None
        ms, mst = self.tmp()
        S.add("act", lambda e: e.activation(out=ms[:, 0:T], in_=bk[:, 0:T], func=AF.Ln, scale=1.0 / D,
                                            bias=self.cst[:, 0:1]), reads=[bt, "cst"], writes=[mst])
        rs, rst = self.tmp(pin=pin)
        S.add("act", lambda e: e.activation(out=rs[:, 0:T], in_=ms[:, 0:T], func=AF.Exp, scale=-0.5),
              reads=[mst], writes=[rst])
        self.unpin(bt)
        return rs, rst

    def pre_norm(self, T, gidx):
        S = self.S
        rs, rst = self.rstd_from([(self.x[:, c, 0:T], ["x%d" % c]) for c in range(NCH)], T)
        for c in range(NCH):
            S.add("dve", lambda e, c=c: e.scalar_tensor_tensor(out=self.xn[:, c, 0:T], in0=self.x[:, c, 0:T],
                                                               scalar=self.vcol("gains", gidx, c), in1=rs[:, 0:T],
                                                               op0=ALU.mult, op1=ALU.mult),
                  reads=["x%d" % c, rst, "vec"], writes=["xn%d" % c])

    def post_norm_residual(self, T, gidx, acc):
        S = self.S
        rs, rst = self.norm_finish(acc)
        for c in range(NCH):
            t, tt = self.tmp()
            S.add("dve", lambda e, c=c, t=t: e.scalar_tensor_tensor(out=t[:, 0:T], in0=self.m[:, c, 0:T],
                                                                    scalar=self.vcol("gains", gidx, c), in1=rs[:, 0:T],
                                                                    op0=ALU.mult, op1=ALU.mult),
                  reads=["m%d" % c, rst, "vec"], writes=[tt])
            S.add(self.aux if c % 4 == 1 else "dve", lambda e, c=c, t=t: e.tensor_tensor(
                out=self.x[:, c, 0:T], in0=self.x[:, c, 0:T], in1=t[:, 0:T], op=ALU.add),
                reads=[tt, "x%d" % c], writes=["x%d" % c])

    def proj_group(self, w, wt, col0, rhs_fn, rhs_tokens, T, nk=NCH):
        bk, bt = self.bank()

        def mm(e):
            ins = None
            for k in range(nk):
                ins = e.matmul(bk[:, 0:T], w[:, k, col0:col0 + 128], rhs_fn(k), start=(k == 0), stop=(k == nk - 1))
            return ins
        self.S.add("pe", mm, reads=list(rhs_tokens) + [wt], writes=[bt])
        return bk, bt

    def load_x_dma(self, tile, use_R=False):
        S = self.S
        T = tile["T"]
        staged = []
        Rf = self.R.bitcast(F32)
        for tb in range(T // 128):
            src = tile["xsrc"][tb * 128:(tb + 1) * 128, :]
            assert use_R
            st = Rf[:, 4 * tb:4 * tb + 4, :].rearrange("p a f -> p (a f)")
            toks = ["R%d" % u for u in range(4 * tb, 4 * tb + 4)]
            S.add("sp", lambda e, st=st, src=src: e.dma_start(out=st[:, 0:D], in_=src), writes=toks, dma=True)
            staged.append((st, toks))
        return staged

    def load_x_direct(self, tile):
        T = tile["T"]
        for tb in range(T // 128):
            src = tile["xsrc"][tb * 128:(tb + 1) * 128, :]
            self.load_tm_to_fm(src, 128, lambda c0, n, tb=tb: self.x[:, c0:c0 + n, tb * 128:(tb + 1) * 128],
                               ["x%d" % c for c in range(NCH)], evac="act" if tb % 2 == 0 else "dve")

    def load_x_tr(self, tile, staged):
        for tb, (st, toks) in enumerate(staged):
            self.tm_to_fm(st, toks, 128, lambda c0, n, tb=tb: self.x[:, c0:c0 + n, tb * 128:(tb + 1) * 128],
                          ["x%d" % c for c in range(NCH)], evac="act" if tb % 2 == 0 else "dve")

    def store_y(self, tile):
        T = tile["T"]
        for tb in range(T // 128):
            self.store_fm_to_tm(lambda c, tb=tb: self.x[:, c, tb * 128:(tb + 1) * 128], ["x%d" % c for c in range(NCH)],
                                128, tile["ydst"][tb * 128:(tb + 1) * 128, :])

    def mix(self, tile, l):
        S = self.S
        T, nseq, L = tile["T"], tile["nseq"], tile["L"]
        sample = tile["kind"] == "s"
        first = tile.get("first", False)
        xn_toks = ["xn%d" % c for c in range(NCH)]
        rhs_xn = lambda k: self.xn[:, k, 0:T]
        self.pre_norm(T, 7 * l + 0)
        if sample:
            self.load_tm_to_fm(self.d["sta"][l], NSEQ_S * 2, lambda c0, n: self.stA[:, c0:c0 + n, :], ["stA"], evac="dve")
            self.load_tm_to_fm(self.d["stb"][l], NSEQ_S * 3, lambda c0, n: self.stB[:, c0:c0 + n, :], ["stB"], evac="dve")
            self.load_tm_to_fm(self.d["sth"][l], NSEQ_S, lambda c0, n: self.stH[:, c0:c0 + n, :], ["stH"], evac="dve")
        WA, WB = 2, 3
        aux = self.aux
        ctx = {}

        def stage1(c, col, wb_, wbt, wc_, wct, wh_, wht, wu_, wut):
            bu, but = self.proj_group(wu_, wut, col, rhs_xn, xn_toks, T)
            bhc, bhct = self.proj_group(wc_, wct, col, rhs_xn, xn_toks, T)
            bhh, bhht = self.proj_group(wh_, wht, col, rhs_xn, xn_toks, T)
            bhb, bhbt = self.proj_group(wb_, wbt, col, rhs_xn, xn_toks, T)
            U, Ut = self.ktmp("U")
            U3 = U[:, 0:nseq * (WB + L)].rearrange("p (s w) -> p s w", s=nseq)
            if sample:
                S.add(aux, lambda e: e.tensor_copy(out=U3[:, :, 0:WB], in_=self.stB[:, c, :].rearrange("p (s k) -> p s k", k=WB)),
                      reads=["stB"], writes=[Ut])
            else:
                S.add(aux, lambda e: e.tensor_copy(out=U3[:, 0, 0:WB], in_=self.carB[:, l, c, :]), reads=["carB%d" % c], writes=[Ut])
            bu3 = bu[:, 0:T].rearrange("p (s t) -> p s t", s=nseq)
            S.add("act", lambda e: e.copy(out=U3[:, :, WB:WB + L], in_=bu3), reads=[but, Ut], writes=[Ut])
            uc, uct = self.ktmp("uc")
            uc3 = uc[:, 0:T].rearrange("p (s t) -> p s t", s=nseq)
            S.add("act", lambda e: e.activation(out=uc3, in_=bu3, func=AF.Identity, scale=self.vcol("cbw", 4 * l + 3, c),
                                                bias=self.vcol("cbb", l, c)), reads=[but, "vec"], writes=[uct])
            if sample:
                S.add(aux, lambda e: e.tensor_copy(out=self.oB[:, c, :].rearrange("p (s k) -> p s k", k=WB), in_=U3[:, :, L:L + WB]),
                      reads=[Ut], writes=["oB"])
            else:
                S.add(aux, lambda e: e.tensor_copy(out=self.carB[:, l, c, :], in_=U3[:, 0, L:L + WB]), reads=[Ut], writes=["carB%d" % c])
            hcs, hcst = self.ktmp("hcs")
            S.add("act", lambda e: e.copy(out=hcs[:, 0:T], in_=bhc[:, 0:T]), reads=[bhct], writes=[hcst])
            G, Gt = self.ktmp("G")
            G3 = G[:, 0:nseq * (WA + L)].rearrange("p (s w) -> p s w", s=nseq)
            if sample:
                S.add(aux, lambda e: e.tensor_copy(out=G3[:, :, 0:WA], in_=self.stA[:, c, :].rearrange("p (s k) -> p s k", k=WA)),
                      reads=["stA"], writes=[Gt])
            else:
                S.add(aux, lambda e: e.tensor_copy(out=G3[:, 0, 0:WA], in_=self.carA[:, l, c, :]), reads=["carA%d" % c], writes=[Gt])
            S.add("dve", lambda e: e.tensor_tensor(out=G3[:, :, WA:WA + L], in0=bhh[:, 0:T].rearrange("p (s t) -> p s t", s=nseq),
                                                   in1=hcs[:, 0:T].rearrange("p (s t) -> p s t", s=nseq), op=ALU.mult),
                  reads=[bhht, hcst, Gt], writes=[Gt])
            if sample:
                S.add(aux, lambda e: e.tensor_copy(out=self.oA[:, c, :].rearrange("p (s k) -> p s k", k=WA), in_=G3[:, :, L:L + WA]),
                      reads=[Gt], writes=["oA"])
            else:
                S.add(aux, lambda e: e.tensor_copy(out=self.carA[:, l, c, :], in_=G3[:, 0, L:L + WA]), reads=[Gt], writes=["carA%d" % c])
            ya, yat = self.ktmp("ya")
            ya3 = ya[:, 0:T].rearrange("p (s t) -> p s t", s=nseq)
            S.add("act", lambda e: e.activation(out=ya3, in_=G3[:, :, 2:2 + L], func=AF.Copy, scale=self.vcol("caw", 3 * l + 2, c)),
                  reads=[Gt, "vec"], writes=[yat])
            for k in (1, 0):
                S.add("dve", lambda e, k=k: e.scalar_tensor_tensor(out=ya3, in0=G3[:, :, k:k + L], scalar=self.vcol("caw", 3 * l + k, c),
                                                                   in1=ya3, op0=ALU.mult, op1=ALU.add), reads=[Gt, yat, "vec"], writes=[yat])
            S.add("dve", lambda e: e.tensor_tensor(out=self.R[:, c, 0:T], in0=bhb[:, 0:T], in1=ya[:, 0:T], op=ALU.mult),
                  reads=[bhbt, yat], writes=["R%d" % c])
            for k in (2, 1, 0):
                S.add("dve", lambda e, k=k: e.scalar_tensor_tensor(out=uc3, in0=U3[:, :, k:k + L], scalar=self.vcol("cbw", 4 * l + k, c),
                                                                   in1=uc3, op0=ALU.mult, op1=ALU.add), reads=[Ut, uct, "vec"], writes=[uct])
            ucb, ucbt = self.tmpb()
            S.add("dve", lambda e: e.tensor_copy(out=ucb[:, 0:T], in_=uc[:, 0:T]), reads=[uct], writes=[ucbt])
            ctx[c] = dict(uc=uc, uct=uct, ucb=ucb, ucbt=ucbt)

        def stage2(c):
            uc, uct, ucb, ucbt = ctx[c]["uc"], ctx[c]["uct"], ctx[c]["ucb"], ctx[c]["ucbt"]
            bga_, bgat = self.bank()
            S.add("pe", lambda e: e.matmul(bga_[:, 0:T], self.bd[:, l * 2 + 0, c, :], ucb[:, 0:T], start=True, stop=True),
                  reads=[ucbt, "bd"], writes=[bgat])
            bgx_, bgxt = self.bank()
            S.add("pe", lambda e: e.matmul(bgx_[:, 0:T], self.bd[:, l * 2 + 1, c, :], ucb[:, 0:T], start=True, stop=True),
                  reads=[ucbt, "bd"], writes=[bgxt])

            def sigm(dst, bk, bias_ap, rd, wr):
                S.add("act", lambda e: e.activation(out=dst[:, 0:T], in_=bk[:, 0:T], func=AF.Exp, scale=-1.0, bias=bias_ap),
                      reads=rd + ["der"], writes=[wr])
                S.add("act", lambda e: e.activation(out=dst[:, 0:T], in_=dst[:, 0:T], func=AF.Ln, bias=self.cst[:, 1:2]),
                      reads=[wr, "cst"], writes=[wr])
                S.add("act", lambda e: e.activation(out=dst[:, 0:T], in_=dst[:, 0:T], func=AF.Exp, scale=-1.0), reads=[wr], writes=[wr])
            tr_, trt = self.ktmp("tr")
            sigm(tr_, bga_, self.der[:, c, 4 * l + 0:4 * l + 1], [bgat], trt)
            ti_, tit = self.ktmp("ti")
            sigm(ti_, bgx_, self.der[:, c, 4 * l + 1:4 * l + 2], [bgxt], tit)
            a_, at = self.ktmp("a")
            S.add("act", lambda e: e.activation(out=a_[:, 0:T], in_=tr_[:, 0:T], func=AF.Exp, scale=self.der[:, c, 4 * l + 2:4 * l + 3]),
                  reads=[trt, "der"], writes=[at])
            a2_, a2t = self.ktmp("a2")
            S.add("act", lambda e: e.activation(out=a2_[:, 0:T], in_=tr_[:, 0:T], func=AF.Exp, scale=self.der[:, c, 4 * l + 3:4 * l + 4]),
                  reads=[trt, "der"], writes=[a2t])
            S.add("act", lambda e: e.activation(out=a2_[:, 0:T], in_=a2_[:, 0:T], func=AF.Ln, scale=-1.0, bias=self.cst[:, 1:2]),
                  reads=[a2t, "cst"], writes=[a2t])
            S.add("act", lambda e: e.activation(out=a2_[:, 0:T], in_=a2_[:, 0:T], func=AF.Exp, scale=0.5), reads=[a2t], writes=[a2t])
            ctx[c].update(ti=ti_, tit=tit, a=a_, at=at, a2=a2_, a2t=a2t)

        def stage3(c):
            k = ctx.pop(c)
            uc, uct, ti_, tit, a_, at, a2_, a2t = k["uc"], k["uct"], k["ti"], k["tit"], k["a"], k["at"], k["a2"], k["a2t"]
            iu, iut = self.ktmp("iu")
            S.add("dve", lambda e: e.tensor_tensor(out=iu[:, 0:T], in0=ti_[:, 0:T], in1=uc[:, 0:T], op=ALU.mult),
                  reads=[tit, uct], writes=[iut])
            S.add("dve", lambda e: e.tensor_tensor(out=iu[:, 0:T], in0=iu[:, 0:T], in1=a2_[:, 0:T], op=ALU.mult),
                  reads=[iut, a2t], writes=[iut])
            hs, hst = self.ktmp("hs")
            if sample:
                a3 = a_[:, 0:T].rearrange("p (s t) -> p s t", s=nseq)
                b3 = iu[:, 0:T].rearrange("p (s t) -> p s t", s=nseq)
                t0, t0t = self.ktmp("t0")
                S.add("dve", lambda e: e.tensor_tensor(out=t0[:, 0:nseq], in0=a3[:, :, 0], in1=self.stH[:, c, :], op=ALU.mult),
                      reads=[at, "stH"], writes=[t0t])
                S.add("dve", lambda e: e.tensor_tensor(out=b3[:, :, 0], in0=b3[:, :, 0], in1=t0[:, 0:nseq], op=ALU.add),
                      reads=[t0t, iut], writes=[iut])
                S.add("dve", lambda e: e.memset(a3[:, :, 0], 0.0), reads=[t0t], writes=[at])
                init, init_toks = 0.0, []
            else:
                init, init_toks = self.carH[:, l, c, :], ["carH%d" % c]
            S.add("dve", lambda e: e.tensor_tensor_scan(out=hs[:, 0:T], data0=a_[:, 0:T], data1=iu[:, 0:T], initial=init,
                                                        op0=ALU.mult, op1=ALU.add), reads=[at, iut] + init_toks, writes=[hst])
            if sample:
                S.add(aux, lambda e: e.tensor_copy(out=self.oH[:, c, :], in_=hs[:, 0:T].rearrange("p (s t) -> p s t", s=nseq)[:, :, L - 1]),
                      reads=[hst], writes=["oH"])
            else:
                S.add(aux, lambda e: e.tensor_copy(out=self.carH[:, l, c, :], in_=hs[:, T - 1:T]), reads=[hst], writes=["carH%d" % c])
            S.add("act", lambda e: e.copy(out=self.R[:, 8 + c, 0:T], in_=hs[:, 0:T]), reads=[hst], writes=["R%d" % (8 + c)])

        for cp in range(4):
            wb_, wbt = self.wget()
            wc_, wct = self.wget()
            wh_, wht = self.wget()
            wu_, wut = self.wget()
            for j in range(2):
                c = cp * 2 + j
                stage1(c, j * 128, wb_, wbt, wc_, wct, wh_, wht, wu_, wut)
                if c >= 1:
                    stage2(c - 1)
                if c >= 2:
                    stage3(c - 2)
            self.wrel(4)
        stage2(NCH - 1)
        stage3(NCH - 2)
        stage3(NCH - 1)
        ba_toks = ["R%d" % c for c in range(NCH)]
        hs_toks = ["R%d" % (8 + c) for c in range(NCH)]
        for cp in range(4):
            wco, wcot = self.wget()
            wro, wrot = self.wget()
            wgc, wgct = self.wget()
            wgr, wgrt = self.wget()
            def sigm0(dst, bk, rd, wr):
                S.add("act", lambda e: e.activation(out=dst[:, 0:T], in_=bk[:, 0:T], func=AF.Exp, scale=-1.0),
                      reads=rd, writes=[wr])
                S.add("act", lambda e: e.activation(out=dst[:, 0:T], in_=dst[:, 0:T], func=AF.Ln, bias=self.cst[:, 1:2]),
                      reads=[wr, "cst"], writes=[wr])
                S.add("act", lambda e: e.activation(out=dst[:, 0:T], in_=dst[:, 0:T], func=AF.Exp, scale=-1.0),
                      reads=[wr], writes=[wr])
            part = {}
            for j in range(2):
                col = j * 128
                bgc, bgct = self.proj_group(wgc, wgct, col, rhs_xn, xn_toks, T)
                bgr, bgrt = self.proj_group(wgr, wgrt, col, rhs_xn, xn_toks, T)
                byc, byct = self.proj_group(wco, wcot, col, lambda k: self.R[:, k, 0:T], ba_toks, T)
                tc_, tct = self.tmp()
                sigm0(tc_, bgc, [bgct], tct)
                tg_, tgt = self.tmp()
                sigm0(tg_, bgr, [bgrt], tgt)
                S.add("dve", lambda e, tc_=tc_, bk=byc: e.tensor_tensor(
                    out=tc_[:, 0:T], in0=bk[:, 0:T], in1=tc_[:, 0:T], op=ALU.mult), reads=[tct, byct], writes=[tct])
                part[j] = (tc_, tct, tg_, tgt)
            for j in range(2):
                c = cp * 2 + j
                tc_, tct, tg_, tgt = part[j]
                byr, byrt = self.proj_group(wro, wrot, j * 128, lambda k: self.R[:, 8 + k, 0:T], hs_toks, T)
                S.add("dve", lambda e, tg_=tg_, bk=byr: e.tensor_tensor(
                    out=tg_[:, 0:T], in0=bk[:, 0:T], in1=tg_[:, 0:T], op=ALU.mult), reads=[tgt, byrt], writes=[tgt])
                S.add("dve", lambda e, tc_=tc_, tg_=tg_, c=c: e.tensor_tensor(
                    out=self.R[:, 16 + c, 0:T], in0=tc_[:, 0:T], in1=tg_[:, 0:T], op=ALU.add),
                    reads=[tct, tgt], writes=["R%d" % (16 + c)])
            self.wrel(4)
        z_toks = ["R%d" % (16 + c) for c in range(NCH)]
        acc = self.norm_begin(T)
        for cp in range(4):
            w, wt = self.wget()
            for j in range(2):
                c = cp * 2 + j
                bk, bt = self.proj_group(w, wt, j * 128, lambda k: self.R[:, 16 + k, 0:T], z_toks, T)
                S.add("act", lambda e, bk=bk, c=c: e.copy(out=self.m[:, c, 0:T], in_=bk[:, 0:T]),
                      reads=[bt], writes=["m%d" % c])
                self.norm_add(acc, bk[:, 0:T], [bt])
            self.wrel()
        self.post_norm_residual(T, 7 * l + 1, acc)
        if sample:
            self.store_fm_to_tm(lambda c: self.oA[:, c, :], ["oA"], NSEQ_S * 2, self.d["sca"][l])
            self.store_fm_to_tm(lambda c: self.oB[:, c, :], ["oB"], NSEQ_S * 3, self.d["scb"][l])
            self.store_fm_to_tm(lambda c: self.oH[:, c, :], ["oH"], NSEQ_S, self.d["sh"][l])
        elif tile.get("last", False):
            self.store_fm_to_tm(lambda c: self.carA[:, l, c, :], ["carA%d" % c_ for c_ in range(NCH)], 2, self.d["pca"][l])
            self.store_fm_to_tm(lambda c: self.carB[:, l, c, :], ["carB%d" % c_ for c_ in range(NCH)], 3, self.d["pcb"][l])
            self.store_fm_to_tm(lambda c: self.carH[:, l, c, :], ["carH%d" % c_ for c_ in range(NCH)], 1, self.d["ph"][l])

    def attn(self, tile, l):
        S = self.S
        T = tile["T"]
        sample = tile["kind"] == "s"
        xn_toks = ["xn%d" % c for c in range(NCH)]
        rhs_xn = lambda k: self.xn[:, k, 0:T]
        acc = self.norm_begin(T)
        for c in range(NCH):
            self.norm_add(acc, self.x[:, c, 0:T], ["x%d" % c])
            S.add("act", lambda e, c=c: e.activation(out=self.xn[:, c, 0:T], in_=self.x[:, c, 0:T], func=AF.Copy,
                                                     scale=self.vcol("gains", 7 * l + 2, c)),
                  reads=["x%d" % c, "vec"], writes=["xn%d" % c])
        rs, rst = self.norm_finish(acc)
        for cp in range(4):
            w, wt = self.wget()
            for j in range(2):
                c = cp * 2 + j
                bk, bt = self.proj_group(w, wt, j * 128, rhs_xn, xn_toks, T)
                S.add("dve", lambda e, bk=bk, c=c: e.scalar_tensor_tensor(
                    out=self.R[:, c, 0:T], in0=bk[:, 0:T], scalar=1.0 / 16.0, in1=rs[:, 0:T], op0=ALU.mult, op1=ALU.mult),
                    reads=[bt, rst], writes=["R%d" % c])
            self.wrel()
        if not sample:
            def head_a(h):
                pts = []
                for mc in range(2):
                    bk, bt = self.bank()

                    def mm(e, bk=bk, mc=mc):
                        ins = None
                        for dc in range(2):
                            ins = e.matmul(bk[:, 0:T], self.KT[:, l, 2 * h + dc, mc * 128:(mc + 1) * 128],
                                           self.R[:, 2 * h + dc, 0:T], start=(dc == 0), stop=(dc == 1))
                        return ins
                    S.add("pe", mm, reads=["KT%d" % l, "R%d" % (2 * h), "R%d" % (2 * h + 1)], writes=[bt])
                    ru = 8 + 2 * h + mc
                    S.add("act", lambda e, bk=bk, ru=ru: e.activation(out=self.R[:, ru, 0:T], in_=bk[:, 0:T], func=AF.Exp),
                          reads=[bt], writes=["R%d" % ru])
                    pts.append(ru)
                return pts

            def head_b(h, pts):
                bs, bst = self.bank()

                def mms(e):
                    ins = None
                    for mc in range(2):
                        ins = e.matmul(bs[:, 0:T], self.ones, self.R[:, pts[mc], 0:T], start=(mc == 0), stop=(mc == 1))
                    return ins
                S.add("pe", mms, reads=["R%d" % p for p in pts] + ["ones"], writes=[bst])
                ri, rit = self.tmp()
                S.add("act", lambda e: e.activation(out=ri[:, 0:T], in_=bs[:, 0:T], func=AF.Ln), reads=[bst], writes=[rit])
                S.add("act", lambda e: e.activation(out=ri[:, 0:T], in_=ri[:, 0:T], func=AF.Exp, scale=-1.0),
                      reads=[rit], writes=[rit])
                for dc in range(2):
                    bo, bot = self.bank()

                    def mmo(e, bo=bo, dc=dc):
                        ins = None
                        for mc in range(2):
                            ins = e.matmul(bo[:, 0:T], self.V[:, l, mc, h * 256 + dc * 128:h * 256 + (dc + 1) * 128],
                                           self.R[:, pts[mc], 0:T], start=(mc == 0), stop=(mc == 1))
                        return ins
                    S.add("pe", mmo, reads=["R%d" % p for p in pts] + ["V%d" % l], writes=[bot])
                    ou = 16 + 2 * h + dc
                    S.add("dve", lambda e, bo=bo, ou=ou: e.tensor_tensor(out=self.R[:, ou, 0:T], in0=bo[:, 0:T],
                                                                         in1=ri[:, 0:T], op=ALU.mult),
                          reads=[bot, rit], writes=["R%d" % ou])
            prev = None
            for h in range(4):
                pts = head_a(h)
                if prev is not None:
                    head_b(*prev)
                prev = (h, pts)
            head_b(*prev)
        else:
            self.attn_sample(l)
        o_toks = ["R%d" % (16 + c) for c in range(NCH)]
        acc = self.norm_begin(T)
        for cp in range(4):
            w, wt = self.wget()
            for j in range(2):
                c = cp * 2 + j
                bk, bt = self.proj_group(w, wt, j * 128, lambda k: self.R[:, 16 + k, 0:T], o_toks, T)
                S.add("act", lambda e, bk=bk, c=c: e.copy(out=self.m[:, c, 0:T], in_=bk[:, 0:T]), reads=[bt], writes=["m%d" % c])
                self.norm_add(acc, bk[:, 0:T], [bt])
            self.wrel()
        self.post_norm_residual(T, 7 * l + 3, acc)

    def attn_sample(self, l):
        S = self.S
        d = self.d
        T = TS
        for grp in range(2):
            bS, bSt = self.bank(pin=True)
            bO, bOt = self.bank(pin=True)
            ptu = 8 + grp
            PT = self.R[:, ptu, :]
            kvs = []
            for sl in range(8):
                s = grp * 8 + sl
                i = self.kv_rr
                self.kv_rr = (i + 1) % 2
                kt = self.kts[:, i]
                v = self.vs[:, i]
                st, stt = self.stg()
                kst = st.rearrange("p (a f) -> p a f", a=2)
                S.add("sp", lambda e, kst=kst, s=s: e.dma_start(out=kst, in_=d["ck"][l, s].rearrange("(a p) f -> p a f", p=128)),
                      writes=[stt], dma=True)
                S.add("pool", lambda e, v=v, s=s: e.dma_start(out=v, in_=d["cv"][l, s].rearrange("(a p) f -> p a f", p=128)),
                      writes=["V%d" % i], dma=True)
                for dp in range(4):
                    bk, bt = self.bank()

                    def tr(e, bk=bk, dp=dp, kst=kst):
                        ins = None
                        for dj in range(2):
                            dc = dp * 2 + dj
                            for a in range(2):
                                ins = e.transpose(bk[:, dj * 256 + a * 128:dj * 256 + (a + 1) * 128],
                                                  kst[:, a, dc * 128:(dc + 1) * 128], self.ident)
                        return ins
                    S.add("pe", tr, reads=[stt, "ident"], writes=[bt])
                    eng = "act" if dp % 2 == 0 else "dve"
                    if eng == "act":
                        S.add("act", lambda e, bk=bk, dp=dp, kt=kt: e.copy(out=kt[:, 2 * dp:2 * dp + 2, :],
                                                                          in_=bk.rearrange("p (j m) -> p j m", j=2)),
                              reads=[bt], writes=["KT%d" % i])
                    else:
                        S.add("dve", lambda e, bk=bk, dp=dp, kt=kt: e.tensor_copy(out=kt[:, 2 * dp:2 * dp + 2, :],
                                                                                 in_=bk.rearrange("p (j m) -> p j m", j=2)),
                              reads=[bt], writes=["KT%d" % i])
                def mm(e, kt=kt, sl=sl, s=s, bS=bS):
                    ins = None
                    for h in range(4):
                        for mc in range(2):
                            col = mc * 256 + sl * 32 + h * 8
                            for dc in range(2):
                                ins = e.matmul(bS[:, col:col + 8], kt[:, 2 * h + dc, mc * 128:(mc + 1) * 128],
                                               self.R[:, 2 * h + dc, s * 8:(s + 1) * 8], start=(dc == 0), stop=(dc == 1))
                    return ins
                S.add("pe", mm, reads=["KT%d" % i] + ["R%d" % c for c in range(NCH)], writes=[bSt])
                kvs.append((v, "V%d" % i, sl, s))
                S.add("act", lambda e, sl=sl, bS=bS, PT=PT: e.activation(
                    out=PT.rearrange("p (a c) -> p a c", a=2)[:, :, sl * 32:(sl + 1) * 32],
                    in_=bS.rearrange("p (a c) -> p a c", a=2)[:, :, sl * 32:(sl + 1) * 32], func=AF.Exp),
                    reads=[bSt], writes=["R%d" % ptu])
                def mmo(e, v=v, sl=sl, bO=bO, PT=PT):
                    ins = None
                    for h in range(4):
                        for dc in range(2):
                            c = 2 * h + dc
                            for mc in range(2):
                                ins = e.matmul(bO[:, c * 64 + sl * 8:c * 64 + sl * 8 + 8],
                                               v[:, mc, h * 256 + dc * 128:h * 256 + (dc + 1) * 128],
                                               PT[:, mc * 256 + sl * 32 + h * 8:mc * 256 + sl * 32 + h * 8 + 8],
                                               start=(mc == 0), stop=(mc == 1))
                    return ins
                S.add("pe", mmo, reads=["V%d" % i, "R%d" % ptu], writes=[bOt])
            bs, bst = self.bank()

            def mms(e, bs=bs, PT=PT):
                ins = None
                for mc in range(2):
                    ins = e.matmul(bs[:, 0:256], self.ones, PT[:, mc * 256:(mc + 1) * 256], start=(mc == 0), stop=(mc == 1))
                return ins
            S.add("pe", mms, reads=["R%d" % ptu, "ones"], writes=[bst])
            ri, rit = self.tmp()
            S.add("act", lambda e, ri=ri, bs=bs: e.activation(out=ri[:, 0:256], in_=bs[:, 0:256], func=AF.Ln), reads=[bst], writes=[rit])
            S.add("act", lambda e, ri=ri: e.activation(out=ri[:, 0:256], in_=ri[:, 0:256], func=AF.Exp, scale=-1.0),
                  reads=[rit], writes=[rit])
            for h in range(4):
                for dc in range(2):
                    c = 2 * h + dc
                    S.add("dve", lambda e, c=c, h=h, ri=ri, bO=bO, grp=grp: e.tensor_tensor(
                        out=self.R[:, 16 + c, grp * 64:(grp + 1) * 64].rearrange("p (s t) -> p s t", s=8),
                        in0=bO[:, c * 64:(c + 1) * 64].rearrange("p (s t) -> p s t", s=8),
                        in1=ri[:, 0:256].rearrange("p (s h t) -> p s h t", s=8, h=4)[:, :, h, :], op=ALU.mult),
                        reads=[bOt, rit], writes=["R%d" % (16 + c)])
            self.unpin(bSt)
            self.unpin(bOt)

    def ffn(self, tile, l):
        S = self.S
        T = tile["T"]
        xn_toks = ["xn%d" % c for c in range(NCH)]
        rhs_xn = lambda k: self.xn[:, k, 0:T]
        self.pre_norm(T, 7 * l + 4)
        for jp in range(11):
            wg, wgt = self.wget()
            wu, wut = self.wget()
            for jj in range(2):
                j = jp * 2 + jj
                bg, bgt = self.proj_group(wg, wgt, jj * 128, rhs_xn, xn_toks, T)
                bu, but = self.proj_group(wu, wut, jj * 128, rhs_xn, xn_toks, T)
                sg, sgt = self.tmp()
                S.add("act", lambda e, sg=sg, bg=bg: e.activation(out=sg[:, 0:T], in_=bg[:, 0:T], func=AF.Silu),
                      reads=[bgt], writes=[sgt])
                S.add("dve", lambda e, sg=sg, bu=bu, j=j: e.tensor_tensor(out=self.R[:, j, 0:T], in0=bu[:, 0:T], in1=sg[:, 0:T],
                                                                         op=ALU.mult), reads=[but, sgt], writes=["R%d" % j])
            self.wrel(2)
        h_toks = ["R%d" % j for j in range(NJ)]
        acc = self.norm_begin(T)
        for c in range(NCH):
            w0, wt0 = self.wget()
            w1, wt1 = self.wget()
            bk, bt = self.bank()

            def mm(e, bk=bk, w0=w0, w1=w1):
                ins = None
                for k in range(NJ):
                    w = w0 if k < 11 else w1
                    ins = e.matmul(bk[:, 0:T], w[:, k % 11, :], self.R[:, k, 0:T], start=(k == 0), stop=(k == NJ - 1))
                return ins
            S.add("pe", mm, reads=h_toks + [wt0, wt1], writes=[bt])
            S.add("act", lambda e, bk=bk, c=c: e.copy(out=self.m[:, c, 0:T], in_=bk[:, 0:T]), reads=[bt], writes=["m%d" % c])
            self.norm_add(acc, bk[:, 0:T], [bt])
            self.wrel(2)
        if l == DEPTH - 1 and tile.get("next") is not None:
            tile["next"]["staged"] = self.load_x_dma(tile["next"], use_R=True)
        self.post_norm_residual(T, 7 * l + 5, acc)

    def build(self):
        self.plan_weights()
        tiles = []
        for i in range(NPT):
            tiles.append(dict(kind="p", T=TP, nseq=1, L=TP, first=(i == 0), last=(i == NPT - 1),
                              xsrc=self.d["xp"][i * TP:(i + 1) * TP, :], ydst=self.d["yp"][i * TP:(i + 1) * TP, :]))
        tiles.append(dict(kind="s", T=TS, nseq=NSEQ_S, L=L_S, xsrc=self.d["xs"], ydst=self.d["ys"]))
        for ti in range(len(tiles) - 1):
            tiles[ti]["next"] = tiles[ti + 1]
        self.prologue(tiles[0])
        for ti, tile in enumerate(tiles):
            self.aux = "dve" if ti == 0 else "pool"
            if ti > 0:
                self.load_x_tr(tile, tile["staged"])
            for l in range(DEPTH):
                self.mix(tile, l)
                self.attn(tile, l)
                self.ffn(tile, l)
            self.store_y(tile)
        assert self.w_cons == len(self.wplan), (self.w_cons, len(self.wplan))
        self.S.emit()
        return self.nc


_CACHE = {}


def _get_program():
    if "nc" not in _CACHE:
        _CACHE["nc"] = Builder().build()
    return _CACHE["nc"]


def kernel(x_prompt, x_sample, state_conv_a, state_conv_b, state_rglru, cache_mem_k, cache_mem_v, mem_prompt,
           norm_gains, w_in, conv_a_w, w_conv_out, conv_b_w, conv_b_b, w_gate_a, b_gate_a, w_gate_x, b_gate_x,
           lru_lambda, w_rnn_out, w_mix_out, w_kv_x, w_q_x, w_o_x, w_ffn_in, w_ffn_out):
    f = lambda a: np.ascontiguousarray(np.asarray(a, dtype=np.float32))
    x_prompt, x_sample = f(x_prompt), f(x_sample)
    state_conv_a, state_conv_b, state_rglru = f(state_conv_a), f(state_conv_b), f(state_rglru)
    cache_mem_k, cache_mem_v, mem_prompt = f(cache_mem_k), f(cache_mem_v), f(mem_prompt)
    n = 8
    shared = {
        "gains": f(norm_gains).reshape(14, D), "caw": f(conv_a_w).reshape(6, D), "cbw": f(conv_b_w).reshape(8, D),
        "cbb": f(conv_b_b), "bga": f(b_gate_a), "bgx": f(b_gate_x), "lam": f(lru_lambda),
        "ident": np.eye(128, dtype=np.float32),
        "w_in": f(w_in), "w_conv_out": f(w_conv_out), "w_gate_a": f(w_gate_a), "w_gate_x": f(w_gate_x),
        "w_rnn_out": f(w_rnn_out), "w_mix_out": f(w_mix_out), "w_kv": f(w_kv_x), "w_q": f(w_q_x), "w_o": f(w_o_x),
        "w_ffn_in": f(w_ffn_in), "w_ffn_out": f(w_ffn_out),
    }
    in_maps = []
    for b in range(n):
        sl = slice(b * NSEQ_S, (b + 1) * NSEQ_S)
        m = dict(shared)
        m["xp"] = x_prompt[b]
        m["xs"] = x_sample[sl].reshape(TS, D)
        m["sta"] = np.ascontiguousarray(state_conv_a[:, sl]).reshape(DEPTH, NSEQ_S * 2, D)
        m["stb"] = np.ascontiguousarray(state_conv_b[:, sl]).reshape(DEPTH, NSEQ_S * 3, D)
        m["sth"] = np.ascontiguousarray(state_rglru[:, sl]).reshape(DEPTH, NSEQ_S, D)
        m["ck"] = np.ascontiguousarray(cache_mem_k[:, sl]).reshape(DEPTH, NSEQ_S, NMEM, D)
        m["cv"] = np.ascontiguousarray(cache_mem_v[:, sl]).reshape(DEPTH, NSEQ_S, NMEM, D)
        m["memp"] = mem_prompt[b]
        in_maps.append(m)
    nc = _get_program()
    res = run_bass_kernel_spmd(nc, in_maps, core_ids=list(range(n)))
    r = res.results
    y_prompt = np.stack([r[b]["yp"] for b in range(n)], axis=0)
    y_sample = np.concatenate([r[b]["ys"].reshape(NSEQ_S, L_S, D) for b in range(n)], axis=0)
    p_conv_a = np.stack([r[b]["pca"] for b in range(n)], axis=1)
    p_conv_b = np.stack([r[b]["pcb"] for b in range(n)], axis=1)
    p_rglru = np.stack([r[b]["ph"].reshape(DEPTH, D) for b in range(n)], axis=1)
    p_mem_k = np.stack([r[b]["pk"].reshape(DEPTH, NMEM, 4, 256) for b in range(n)], axis=1)
    p_mem_v = np.stack([r[b]["pv"].reshape(DEPTH, NMEM, 4, 256) for b in range(n)], axis=1)
    s_conv_a = np.concatenate([r[b]["sca"].reshape(DEPTH, NSEQ_S, 2, D) for b in range(n)], axis=1)
    s_conv_b = np.concatenate([r[b]["scb"].reshape(DEPTH, NSEQ_S, 3, D) for b in range(n)], axis=1)
    s_rglru = np.concatenate([r[b]["sh"].reshape(DEPTH, NSEQ_S, D) for b in range(n)], axis=1)
    return (y_prompt.astype(np.float32), y_sample.astype(np.float32), p_conv_a.astype(np.float32),
            p_conv_b.astype(np.float32), p_rglru.astype(np.float32), p_mem_k.astype(np.float32),
            p_mem_v.astype(np.float32), s_conv_a.astype(np.float32), s_conv_b.astype(np.float32),
            s_rglru.astype(np.float32))
```

```python
import contextlib
import numpy as np
import concourse.bass as bass
import concourse.mybir as mybir
from concourse.bass_utils import run_bass_kernel_spmd

F32 = mybir.dt.float32
BF16 = mybir.dt.bfloat16
ALU = mybir.AluOpType
AF = mybir.ActivationFunctionType

ENGS = ("pe", "act", "dve", "pool", "sp")
N_DMA_SEMS = 12

D = 1024
NCH = 8
DFF = 2816
NJ = 22
DEPTH = 2
SEQ = 2048
TP = 512
NPT = SEQ // TP
NSEQ_S = 16
L_S = 8
TS = NSEQ_S * L_S
NMEM = 256
EPS = 1e-6


class Sched:
    def __init__(self, nc):
        self.nc = nc
        self.ops = {e: [] for e in ENGS}
        self.cnt = {e: 0 for e in ENGS}
        self.seen = {e: {} for e in ENGS}
        self.last_w = {}
        self.readers = {}
        self.dma_cnt = [0] * N_DMA_SEMS
        self.dma_rr = {"pool": 0, "sp": 0}

    def _need(self, deps, eng, ev, raw):
        semkey, val, src = ev
        if src == eng:
            if eng == "pe":
                return
        if deps.get(semkey, 0) < val:
            deps[semkey] = val

    def add(self, eng, fn, reads=(), writes=(), dma=False):
        deps = {}
        deng = None if dma else eng
        for t in reads:
            for ev in self.last_w.get(t, ()):
                self._need(deps, deng, ev, True)
        for t in writes:
            for ev in self.last_w.get(t, ()):
                self._need(deps, deng, ev, False)
            for sk, (v, src) in self.readers.get(t, {}).items():
                self._need(deps, deng, (sk, v, src), False)
        if dma:
            half = N_DMA_SEMS // 2
            k = self.dma_rr[eng]
            self.dma_rr[eng] = (k + 1) % half
            i = k + (0 if eng == "pool" else half)
            c = self.dma_cnt[i] + 1
            self.dma_cnt[i] = c
            if c > 1:
                sk = ("dma", i)
                if deps.get(sk, 0) < 16 * (c - 1):
                    deps[sk] = 16 * (c - 1)
            ev = (("dma", i), 16 * c, None)
        else:
            self.cnt[eng] += 1
            ev = (eng, self.cnt[eng], eng)
        waits = []
        seen = self.seen[eng]
        for sk, v in deps.items():
            if seen.get(sk, 0) < v:
                seen[sk] = v
                waits.append((sk, v))
        self.ops[eng].append((fn, waits, ev[0]))
        for t in writes:
            prev = self.last_w.get(t)
            if dma and prev and all(p[2] is None for p in prev):
                self.last_w[t] = (prev + [ev])[-8:]
            else:
                self.last_w[t] = [ev]
            self.readers[t] = {}
        for t in reads:
            if t in writes:
                continue
            r = self.readers.setdefault(t, {})
            old = r.get(ev[0])
            if old is None or old[0] < ev[1]:
                r[ev[0]] = (ev[1], ev[2])
        return ev

    def emit(self):
        nc = self.nc
        waits = []
        for i, c in enumerate(self.dma_cnt):
            if c > 0 and self.seen["sp"].get(("dma", i), 0) < 16 * c:
                waits.append((("dma", i), 16 * c))
        self.ops["sp"].append((None, waits, None))
        with contextlib.ExitStack() as st:
            sems = {}
            for e in ENGS[:4]:
                sems[e] = st.enter_context(nc.semaphore("sem_" + e))
            for i in range(N_DMA_SEMS):
                sems[("dma", i)] = st.enter_context(nc.semaphore("sem_dma%d" % i))
            block = st.enter_context(nc.Block())

            def run(engobj, name):
                for fn, waits, inc in self.ops[name]:
                    for sk, v in waits:
                        engobj.wait_ge(sems[sk], v)
                    if fn is None:
                        continue
                    ins = fn(engobj)
                    if inc is not None:
                        ins.then_inc(sems[inc], 16 if isinstance(inc, tuple) else 1)

            @block.tensor
            def _(e):
                run(e, "pe")

            @block.scalar
            def _(e):
                run(e, "act")

            @block.vector
            def _(e):
                run(e, "dve")

            @block.gpsimd
            def _(e):
                run(e, "pool")

            @block.sync
            def _(e):
                run(e, "sp")


VEC_ROWS = {"gains": (0, 14), "caw": (14, 6), "cbw": (20, 8), "cbb": (28, 2), "bga": (30, 2),
            "bgx": (32, 2), "lam": (34, 2)}
NVEC = 36
NW = 10
WSLOT = 2048
NTMP = 24
TMPW = 528


class Builder:
    def __init__(self):
        nc = bass.Bass("TRN2", target_bir_lowering=False)
        self.nc = nc
        self.S = Sched(nc)
        S = self.S

        def din(name, shape):
            return nc.dram_tensor(name, list(shape), F32, kind="ExternalInput").ap()

        def dout(name, shape):
            return nc.dram_tensor(name, list(shape), F32, kind="ExternalOutput").ap()

        self.d = {}
        d = self.d
        d["xp"] = din("xp", [SEQ, D])
        d["xs"] = din("xs", [TS, D])
        d["sta"] = din("sta", [DEPTH, NSEQ_S * 2, D])
        d["stb"] = din("stb", [DEPTH, NSEQ_S * 3, D])
        d["sth"] = din("sth", [DEPTH, NSEQ_S, D])
        d["ck"] = din("ck", [DEPTH, NSEQ_S, NMEM, D])
        d["cv"] = din("cv", [DEPTH, NSEQ_S, NMEM, D])
        d["memp"] = din("memp", [NMEM, D])
        d["gains"] = din("gains", [14, D])
        d["caw"] = din("caw", [6, D])
        d["cbw"] = din("cbw", [8, D])
        d["cbb"] = din("cbb", [2, D])
        d["bga"] = din("bga", [2, D])
        d["bgx"] = din("bgx", [2, D])
        d["lam"] = din("lam", [2, D])
        d["ident"] = din("ident", [128, 128])
        d["w_in"] = din("w_in", [DEPTH, D, 6 * D])
        d["w_conv_out"] = din("w_conv_out", [DEPTH, D, D])
        d["w_gate_a"] = din("w_gate_a", [DEPTH, 16, 64, 64])
        d["w_gate_x"] = din("w_gate_x", [DEPTH, 16, 64, 64])
        d["w_rnn_out"] = din("w_rnn_out", [DEPTH, D, D])
        d["w_mix_out"] = din("w_mix_out", [DEPTH, D, D])
        d["w_kv"] = din("w_kv", [DEPTH, D, 2 * D])
        d["w_q"] = din("w_q", [DEPTH, D, D])
        d["w_o"] = din("w_o", [DEPTH, D, D])
        d["w_ffn_in"] = din("w_ffn_in", [DEPTH, D, 2 * DFF])
        d["w_ffn_out"] = din("w_ffn_out", [DEPTH, DFF, D])
        d["yp"] = dout("yp", [SEQ, D])
        d["ys"] = dout("ys", [TS, D])
        d["pca"] = dout("pca", [DEPTH, 2, D])
        d["pcb"] = dout("pcb", [DEPTH, 3, D])
        d["ph"] = dout("ph", [DEPTH, 1, D])
        d["pk"] = dout("pk", [DEPTH, NMEM, D])
        d["pv"] = dout("pv", [DEPTH, NMEM, D])
        d["sca"] = dout("sca", [DEPTH, NSEQ_S * 2, D])
        d["scb"] = dout("scb", [DEPTH, NSEQ_S * 3, D])
        d["sh"] = dout("sh", [DEPTH, NSEQ_S, D])

        def sb(name, shape, dt=F32):
            return nc.alloc_sbuf_tensor("sb_" + name, list(shape), dt).ap()

        self.ps = nc.alloc_psum_tensor("ps", [128, 8, 512], F32).ap()
        self.bank_rr = 0
        self.pinned = set()
        self.ident = sb("ident", [128, 128])
        self.ones = sb("ones", [128, 128], BF16)
        self.cst = sb("cst", [128, 8])
        self.vec = sb("vec", [128, NCH, NVEC])
        self.der = sb("der", [128, NCH, 12])
        self.dtmp = sb("dtmp", [128, NCH, 8])
        self.bd = sb("bd", [128, DEPTH * 2, NCH, 128], BF16)
        self.KT = sb("KT", [128, DEPTH, NCH, NMEM], BF16)
        self.V = sb("V", [128, DEPTH, 2, D], BF16)
        self.x = sb("x", [128, NCH, TP])
        self.xn = sb("xn", [128, NCH, TP], BF16)
        self.R = sb("R", [128, 24, TP], BF16)
        self.m = sb("m", [128, NCH, TP])
        self.tmpf = sb("tmpf", [128, NTMP, TMPW])
        self.tmp_rr = 0
        self.kcnt = {}
        self.tmp_pinned = set()
        self.aux = "dve"
        self.tmpb_ = sb("tmpb", [128, 4, TP], BF16)
        self.tmpb_rr = 0
        self.wring = sb("wring", [128, NW, WSLOT], BF16)
        self.stage = sb("stage", [128, 2, 2048])
        self.stage_rr = 0
        self.kts = self.KT
        self.vs = self.V
        self.kv_rr = 0
        self.carA = sb("carA", [128, DEPTH, NCH, 2])
        self.carB = sb("carB", [128, DEPTH, NCH, 3])
        self.carH = sb("carH", [128, DEPTH, NCH, 1])
        self.stA = sb("stA", [128, NCH, NSEQ_S * 2])
        self.stB = sb("stB", [128, NCH, NSEQ_S * 3])
        self.stH = sb("stH", [128, NCH, NSEQ_S])
        self.oA = sb("oA", [128, NCH, NSEQ_S * 2])
        self.oB = sb("oB", [128, NCH, NSEQ_S * 3])
        self.oH = sb("oH", [128, NCH, NSEQ_S])
        self.wplan = []
        self.w_next_load = 0
        self.w_cons = 0
        self.w_released = 0

    def bank(self, pin=False):
        while self.bank_rr in self.pinned:
            self.bank_rr = (self.bank_rr + 1) % 8
        i = self.bank_rr
        self.bank_rr = (i + 1) % 8
        if pin:
            self.pinned.add(i)
        return self.ps[:, i, :], "ps%d" % i

    def unpin(self, tok):
        self.pinned.discard(int(tok[2:]))

    def tmp(self, pin=False):
        while self.tmp_rr in self.tmp_pinned:
            self.tmp_rr = (self.tmp_rr + 1) % NTMP
        i = self.tmp_rr
        self.tmp_rr = (i + 1) % NTMP
        if pin:
            self.tmp_pinned.add(i)
        return self.tmpf[:, i, :], "tf%d" % i

    def tmp_unpin(self, tok):
        self.tmp_pinned.discard(int(tok[2:]))

    KINDS = {"hcs": (0, 1), "G": (1, 2), "ya": (3, 1), "U": (4, 2), "uc": (6, 3), "tr": (9, 1), "ti": (10, 2),
             "a": (12, 2), "a2": (14, 2), "iu": (16, 2), "hs": (18, 2), "t0": (20, 1)}

    def ktmp(self, kind):
        base, depth = self.KINDS[kind]
        k = self.kcnt.get(kind, 0)
        self.kcnt[kind] = k + 1
        i = base + k % depth
        return self.tmpf[:, i, :], "tf%d" % i

    def tmpb(self):
        i = self.tmpb_rr
        self.tmpb_rr = (i + 1) % 4
        return self.tmpb_[:, i, :], "tb%d" % i

    def stg(self):
        i = self.stage_rr
        self.stage_rr = (i + 1) % 2
        return self.stage[:, i, :], "stg%d" % i

    def vcol(self, name, row, c):
        base = VEC_ROWS[name][0] + row
        return self.vec[:, c, base:base + 1]

    def plan_weights(self):
        d = self.d
        plan = []

        def blk(w, l, c0, n):
            return (w[l, :, c0:c0 + n].rearrange("(c p) n -> p c n", p=128), int(w.shape[1]) // 128, n)

        def layer_blocks(l):
            out = []
            for cp in range(4):
                for sec in (0, 1, 2, 3):
                    out.append(blk(d["w_in"], l, sec * D + cp * 256, 256))
            for cp in range(4):
                out.append(blk(d["w_conv_out"], l, cp * 256, 256))
                out.append(blk(d["w_rnn_out"], l, cp * 256, 256))
                out.append(blk(d["w_in"], l, 4 * D + cp * 256, 256))
                out.append(blk(d["w_in"], l, 5 * D + cp * 256, 256))
            for cp in range(4):
                out.append(blk(d["w_mix_out"], l, cp * 256, 256))
            for cp in range(4):
                out.append(blk(d["w_q"], l, cp * 256, 256))
            for cp in range(4):
                out.append(blk(d["w_o"], l, cp * 256, 256))
            for jp in range(11):
                out.append(blk(d["w_ffn_in"], l, jp * 256, 256))
                out.append(blk(d["w_ffn_in"], l, DFF + jp * 256, 256))
            for c in range(8):
                for hf in range(2):
                    out.append((d["w_ffn_out"][l, hf * 1408:(hf + 1) * 1408, c * 128:(c + 1) * 128]
                                .rearrange("(c p) n -> p c n", p=128), 11, 128))
            return out

        for l in range(DEPTH):
            for b in range(8):
                plan.append(blk(d["w_kv"], l, b * 256, 256) + (None, True))
        for t in range(NPT + 1):
            i = 0
            for l in range(DEPTH):
                for b_ in layer_blocks(l):
                    plan.append(b_ + (i, t == 0))
                    i += 1
        self.n_scr = i
        self.wscr = self.nc.dram_tensor("wscr", [self.n_scr, 128, WSLOT], BF16, kind="Internal").ap()
        self.wplan = plan

    def _w_prefetch(self):
        while self.w_next_load < len(self.wplan) and self.w_next_load < self.w_released + NW:
            j = self.w_next_load
            src, K, N, scr, first = self.wplan[j]
            slot = j % NW
            if first:
                dst = self.wring[:, slot, 0:K * N].rearrange("p (c n) -> p c n", c=K)
                self.S.add("pool", lambda e, dst=dst, src=src: e.dma_start(out=dst, in_=src),
                           writes=["w%d" % slot], dma=True)
                if scr is not None:
                    self.S.add("sp", lambda e, slot=slot, scr=scr, n=K * N: e.dma_start(
                        out=self.wscr[scr, :, 0:n], in_=self.wring[:, slot, 0:n]),
                        reads=["w%d" % slot], writes=["scr%d" % scr], dma=True)
            else:
                self.S.add("sp", lambda e, slot=slot, scr=scr, n=K * N: e.dma_start(
                    out=self.wring[:, slot, 0:n], in_=self.wscr[scr, :, 0:n]),
                    reads=["scr%d" % scr], writes=["w%d" % slot], dma=True)
            self.w_next_load += 1

    def wget(self):
        j = self.w_cons
        self.w_cons += 1
        assert j < self.w_released + NW, "weight ring too small"
        self._w_prefetch()
        assert self.w_next_load > j
        src, K, N, scr, first = self.wplan[j]
        slot = j % NW
        ap = self.wring[:, slot, 0:K * N].rearrange("p (c n) -> p c n", c=K)
        return ap, "w%d" % slot

    def wrel(self, n=1):
        self.w_released += n
        self._w_prefetch()

    def load_tm_to_fm(self, src_rows, R, dst_fn, dst_tokens, evac="act", scale_fn=None):
        S = self.S
        st, stt = self.stg()
        S.add("sp", lambda e: e.dma_start(out=st[0:R, 0:D], in_=src_rows), writes=[stt], dma=True)
        self.tm_to_fm(st, stt, R, dst_fn, dst_tokens, evac, scale_fn)

    def tm_to_fm(self, st, stt, R, dst_fn, dst_tokens, evac="act", scale_fn=None):
        S = self.S
        stts = list(stt) if isinstance(stt, (list, tuple)) else [stt]
        for half in range(2):
            bk, bt = self.bank()

            def tr(e, half=half, bk=bk):
                ins = None
                for j in range(4):
                    c = half * 4 + j
                    ins = e.transpose(bk[:, j * R:(j + 1) * R], st[0:R, c * 128:(c + 1) * 128], self.ident[0:R, 0:R])
                return ins
            S.add("pe", tr, reads=stts + ["ident"], writes=[bt])
            if scale_fn is None:
                dst = dst_fn(half * 4, 4)
                src = bk[:, 0:4 * R].rearrange("p (j r) -> p j r", j=4)
                if evac == "act":
                    S.add("act", lambda e, dst=dst, src=src: e.copy(out=dst, in_=src), reads=[bt], writes=dst_tokens)
                else:
                    S.add("dve", lambda e, dst=dst, src=src: e.tensor_copy(out=dst, in_=src), reads=[bt], writes=dst_tokens)
            else:
                for j in range(4):
                    c = half * 4 + j
                    dst = dst_fn(c, 1)
                    S.add("act", lambda e, dst=dst, j=j, bk=bk, c=c: e.activation(
                        out=dst, in_=bk[:, j * R:(j + 1) * R].rearrange("p (j r) -> p j r", j=1), func=AF.Copy,
                        scale=scale_fn(c)), reads=[bt, "vec"], writes=dst_tokens)

    def store_fm_to_tm(self, src_fn, src_tokens, R, dst_rows):
        S = self.S
        st, stt = self.stg()
        for half in range(2):
            bk, bt = self.bank()

            def tr(e, half=half, bk=bk):
                ins = None
                for j in range(4):
                    c = half * 4 + j
                    ins = e.transpose(bk[0:R, j * 128:(j + 1) * 128], src_fn(c), self.ident)
                return ins
            S.add("pe", tr, reads=list(src_tokens) + ["ident"], writes=[bt])
            S.add("act", lambda e, half=half, bk=bk: e.copy(out=st[0:R, half * 512:(half + 1) * 512], in_=bk[0:R, :]),
                  reads=[bt], writes=[stt])
        S.add("sp", lambda e: e.dma_start(out=dst_rows, in_=st[0:R, 0:D]), reads=[stt], dma=True)

    def prologue(self, tile0):
        S = self.S
        d = self.d
        nc = self.nc
        S.add("sp", lambda e: e.dma_start(out=self.ident, in_=d["ident"]), writes=["ident"], dma=True)
        S.add("dve", lambda e: e.memset(self.ones, 1.0), writes=["ones"])
        S.add("dve", lambda e: e.memset(self.cst[:, 0:1], EPS), writes=["cst"])
        S.add("dve", lambda e: e.memset(self.cst[:, 1:2], 1.0), writes=["cst"])
        S.add("dve", lambda e: e.memset(self.cst[:, 2:3], 0.0), writes=["cst"])
        S.add("dve", lambda e: e.memset(self.cst[:, 3:4], -0.5), writes=["cst"])
        S.add("dve", lambda e: e.memset(self.cst[:, 4:5], 0.5), writes=["cst"])
        S.add("dve", lambda e: e.memset(self.cst[:, 5:6], -1.0), writes=["cst"])
        S.add("pool", lambda e: e.memset(self.bd, 0.0), writes=["bd"])
        st, stt = self.stg()
        for name, (r0, n) in VEC_ROWS.items():
            S.add("sp", lambda e, name=name, r0=r0, n=n: e.dma_start(out=st[r0:r0 + n, 0:D], in_=d[name]),
                  writes=[stt], dma=True)
        self.tm_to_fm(st, stt, NVEC, lambda c0, n: self.vec[:, c0:c0 + n, :], ["vec"], evac="dve")
        for l in range(DEPTH):
            for g, wname in enumerate(("w_gate_a", "w_gate_x")):
                for hh in range(2):
                    src = d[wname][l].rearrange("(c h) k j -> h k c j", h=2)[hh]
                    dst = self.bd[hh * 64:(hh + 1) * 64, l * 2 + g, :, hh * 64:(hh + 1) * 64]
                    S.add("pool", lambda e, dst=dst, src=src: e.dma_start(out=dst, in_=src),
                          reads=[], writes=["bd"], dma=True)
        for l in range(DEPTH):
            lam = self.vec[:, :, VEC_ROWS["lam"][0] + l]
            bga = self.vec[:, :, VEC_ROWS["bga"][0] + l]
            bgx = self.vec[:, :, VEC_ROWS["bgx"][0] + l]
            t_abs = self.dtmp[:, :, 0]
            t_e = self.dtmp[:, :, 1]
            t_l = self.dtmp[:, :, 2]
            t_r = self.dtmp[:, :, 3]
            S.add("dve", lambda e, bga=bga, l=l: e.tensor_scalar_mul(out=self.der[:, :, 4 * l + 0], in0=bga, scalar1=-1.0),
                  reads=["vec"], writes=["der"])
            S.add("dve", lambda e, bgx=bgx, l=l: e.tensor_scalar_mul(out=self.der[:, :, 4 * l + 1], in0=bgx, scalar1=-1.0),
                  reads=["vec"], writes=["der"])
            S.add("act", lambda e, lam=lam: e.activation(out=t_abs, in_=lam, func=AF.Abs), reads=["vec"], writes=["dt0"])
            S.add("act", lambda e: e.activation(out=t_e, in_=t_abs, func=AF.Exp, scale=-1.0), reads=["dt0"], writes=["dt1"])
            S.add("act", lambda e: e.activation(out=t_l, in_=t_e, func=AF.Ln, bias=self.cst[:, 1:2]),
                  reads=["dt1", "cst"], writes=["dt2"])
            S.add("dve", lambda e, lam=lam: e.tensor_scalar(out=t_r, in0=lam, scalar1=-1.0, scalar2=0.0,
                                                            op0=ALU.mult, op1=ALU.max), reads=["vec"], writes=["dt3"])
            S.add("dve", lambda e: e.tensor_tensor(out=t_r, in0=t_r, in1=t_l, op=ALU.add), reads=["dt3", "dt2"], writes=["dt3"])
            S.add("dve", lambda e, l=l: e.tensor_scalar_mul(out=self.der[:, :, 4 * l + 2], in0=t_r, scalar1=-8.0),
                  reads=["dt3"], writes=["der"])
            S.add("dve", lambda e, l=l: e.tensor_scalar_mul(out=self.der[:, :, 4 * l + 3], in0=t_r, scalar1=-16.0),
                  reads=["dt3"], writes=["der"])
        S.add("pool", lambda e: e.memset(self.carA, 0.0), writes=["carA%d" % c_ for c_ in range(NCH)])
        S.add("pool", lambda e: e.memset(self.carB, 0.0), writes=["carB%d" % c_ for c_ in range(NCH)])
        S.add("pool", lambda e: e.memset(self.carH, 0.0), writes=["carH%d" % c_ for c_ in range(NCH)])
        self.load_x_direct(tile0)
        tile0["staged"] = []
        self.mem_kv()

    def mem_kv(self):
        S = self.S
        d = self.d
        memt = self.stage.rearrange("p a f -> p (a f)")[:, 0:2 * D].rearrange("p (a f) -> p a f", a=2)
        S.add("sp", lambda e: e.dma_start(out=memt, in_=d["memp"].rearrange("(a p) f -> p a f", p=128)),
              writes=["stg0"], dma=True)
        for a in range(2):
            for hf in range(2):
                t, tt = self.tmp()
                S.add("act", lambda e, a=a, hf=hf, t=t: e.activation(
                    out=t[:, 0:512], in_=memt[:, a, hf * 512:(hf + 1) * 512], func=AF.Square),
                    reads=["stg0"], writes=[tt])
                S.add("dve", lambda e, a=a, hf=hf, t=t: e.reduce_sum(
                    out=self.dtmp[:, 0, 4 + 2 * a + hf:5 + 2 * a + hf], in_=t[:, 0:512], axis=mybir.AxisListType.X),
                    reads=[tt], writes=["ssq%d%d" % (a, hf)])
        rs = self.dtmp[:, 1, 0:2]
        S.add("dve", lambda e: e.tensor_tensor(out=self.dtmp[:, 1, 2:4].rearrange("p (a o) -> p a o", o=1),
                                               in0=self.dtmp[:, 0, 4:8].rearrange("p (a h) -> p a h", h=2)[:, :, 0:1],
                                               in1=self.dtmp[:, 0, 4:8].rearrange("p (a h) -> p a h", h=2)[:, :, 1:2],
                                               op=ALU.add),
              reads=["ssq00", "ssq01", "ssq10", "ssq11"], writes=["ssum"])
        S.add("act", lambda e: e.activation(out=self.dtmp[:, 1, 4:6], in_=self.dtmp[:, 1, 2:4], func=AF.Sqrt,
                                            scale=1.0 / D, bias=self.cst[:, 0:1]), reads=["ssum", "cst"], writes=["srt"])
        S.add("dve", lambda e: e.reciprocal(out=rs, in_=self.dtmp[:, 1, 4:6]), reads=["srt"], writes=["mrs"])
        for a in range(2):
            S.add("dve", lambda e, a=a: e.tensor_scalar(out=memt[:, a, :], in0=memt[:, a, :], scalar1=rs[:, a:a + 1],
                                                        scalar2=None, op0=ALU.mult), reads=["mrs", "stg0"], writes=["stg0"])
        mT0 = self.m.rearrange("p c t -> p (c t)")[:, 0:NCH * NMEM].rearrange("p (c t) -> p c t", c=NCH)
        for a in range(2):
            for half in range(2):
                bk, bt = self.bank()

                def tr(e, a=a, half=half, bk=bk):
                    ins = None
                    for j in range(4):
                        c = half * 4 + j
                        ins = e.transpose(bk[:, j * 128:(j + 1) * 128], memt[:, a, c * 128:(c + 1) * 128], self.ident)
                    return ins
                S.add("pe", tr, reads=["stg0", "ident"], writes=[bt])
                S.add("act", lambda e, a=a, half=half, bk=bk: e.copy(
                    out=mT0[:, half * 4:half * 4 + 4, a * 128:(a + 1) * 128],
                    in_=bk.rearrange("p (j r) -> p j r", j=4)), reads=[bt], writes=["m%d" % c_ for c_ in range(NCH)])
        mTl = self.xn.rearrange("p c t -> p (c t)")[:, 0:NCH * NMEM].rearrange("p (c t) -> p c t", c=NCH)
        for l in range(DEPTH):
            for c in range(NCH):
                S.add("dve", lambda e, c=c, l=l: e.tensor_scalar(out=mTl[:, c, :], in0=mT0[:, c, :],
                                                                 scalar1=self.vcol("gains", 7 * l + 6, c), scalar2=None,
                                                                 op0=ALU.mult), reads=["m%d" % c_ for c_ in range(NCH)] + ["vec"], writes=["xn%d" % c_ for c_ in range(NCH)])
            for b in range(8):
                w, wt = self.wget()
                for a in range(2):
                    bk, bt = self.bank()

                    def mm(e, a=a, bk=bk, w=w):
                        ins = None
                        for k in range(NCH):
                            ins = e.matmul(bk[:, 0:256], mTl[:, k, a * 128:(a + 1) * 128], w[:, k, :],
                                           start=(k == 0), stop=(k == NCH - 1))
                        return ins
                    S.add("pe", mm, reads=["xn%d" % c_ for c_ in range(NCH)] + [wt], writes=[bt])
                    t, tt = self.tmp()
                    S.add("act", lambda e, t=t, bk=bk: e.copy(out=t[:, 0:256], in_=bk[:, 0:256]), reads=[bt], writes=[tt])
                    if b < 4:
                        dst = d["pk"][l, a * 128:(a + 1) * 128, b * 256:(b + 1) * 256]
                    else:
                        dst = d["pv"][l, a * 128:(a + 1) * 128, (b - 4) * 256:(b - 3) * 256]
                        S.add("dve", lambda e, t=t, a=a, b=b, l=l: e.tensor_copy(
                            out=self.V[:, l, a, (b - 4) * 256:(b - 3) * 256], in_=t[:, 0:256]), reads=[tt], writes=["V%d" % l])
                    S.add("sp", lambda e, dst=dst, t=t: e.dma_start(out=dst, in_=t[:, 0:256]), reads=[tt], dma=True)
                if b < 4:
                    for j in range(2):
                        bk, bt = self.bank()

                        def mm2(e, j=j, bk=bk, w=w):
                            ins = None
                            for k in range(NCH):
                                ins = e.matmul(bk[:, 0:256], w[:, k, j * 128:(j + 1) * 128], mTl[:, k, :],
                                               start=(k == 0), stop=(k == NCH - 1))
                            return ins
                        S.add("pe", mm2, reads=["xn%d" % c_ for c_ in range(NCH)] + [wt], writes=[bt])
                        S.add("act", lambda e, j=j, b=b, l=l, bk=bk: e.copy(out=self.KT[:, l, 2 * b + j, :], in_=bk[:, 0:256]),
                              reads=[bt], writes=["KT%d" % l])
                self.wrel()

    def rstd_from(self, srcs, T, scale=1.0):
        S = self.S
        bk, bt = self.bank()
        for c, (src, stoks) in enumerate(srcs):
            sq, sqt = self.tmpb()
            S.add("act", lambda e, sq=sq, src=src: e.activation(out=sq[:, 0:T], in_=src, func=AF.Square, scale=scale),
                  reads=stoks, writes=[sqt])
            S.add("pe", lambda e, sq=sq, c=c, bk=bk: e.matmul(bk[:, 0:T], self.ones, sq[:, 0:T], start=(c == 0),
                                                             stop=(c == NCH - 1)), reads=[sqt, "ones"], writes=[bt])
        ms, mst = self.tmp()
        S.add("act", lambda e: e.activation(out=ms[:, 0:T], in_=bk[:, 0:T], func=AF.Ln, scale=1.0 / D,
                                            bias=self.cst[:, 0:1]), reads=[bt, "cst"], writes=[mst])
        rs, rst = self.tmp()
        S.add("act", lambda e: e.activation(out=rs[:, 0:T], in_=ms[:, 0:T], func=AF.Exp, scale=-0.5),
              reads=[mst], writes=[rst])
        return rs, rst

    def norm_begin(self, T):
        bk, bt = self.bank(pin=True)
        return dict(bk=bk, bt=bt, T=T, n=0)

    def norm_add(self, acc, src, stoks, lag=True):
        S = self.S
        T, bk, bt, c = acc["T"], acc["bk"], acc["bt"], acc["n"]
        acc["n"] = c + 1
        sq, sqt = self.tmpb()
        S.add("act", lambda e: e.activation(out=sq[:, 0:T], in_=src, func=AF.Square), reads=stoks, writes=[sqt])

        def emit_pe():
            S.add("pe", lambda e: e.matmul(bk[:, 0:T], self.ones, sq[:, 0:T], start=(c == 0), stop=(c == NCH - 1)),
                  reads=[sqt, "ones"], writes=[bt])
        prev = acc.get("pend")
        if prev is not None:
            prev()
        if lag:
            acc["pend"] = emit_pe
        else:
            acc["pend"] = None
            emit_pe()

    def norm_finish(self, acc, pin=False):
        S = self.S
        T, bk, bt = acc["T"], acc["bk"], acc["bt"]
        assert acc["n"] == NCH
        if acc.get("pend") is not None:
            acc["pend"]()
            acc["pend"] = None
        ms, mst = self.tmp()
        S.add("act", lambda e: e.activation(out=ms[:, 0:T], in_=bk[:, 0:T], func=AF.Ln, scale=1.0 / D,
                                            bias=self.cst[:, 0:1]), reads=[bt, "cst"], writes=[mst])
        rs, rst = self.tmp(pin=pin)
        S.add("act", lambda e: e.activation(out=rs[:, 0:T], in_=ms[:, 0:T], func=AF.Exp, scale=-0.5),
              reads=[mst], writes=[rst])
        self.unpin(bt)
        return rs, rst

    def pre_norm(self, T, gidx):
        S = self.S
        rs, rst = self.rstd_from([(self.x[:, c, 0:T], ["x%d" % c]) for c in range(NCH)], T)
        for c in range(NCH):
            S.add("dve", lambda e, c=c: e.scalar_tensor_tensor(out=self.xn[:, c, 0:T], in0=self.x[:, c, 0:T],
                                                               scalar=self.vcol("gains", gidx, c), in1=rs[:, 0:T],
                                                               op0=ALU.mult, op1=ALU.mult),
                  reads=["x%d" % c, rst, "vec"], writes=["xn%d" % c])

    def post_norm_residual(self, T, gidx, acc):
        S = self.S
        rs, rst = self.norm_finish(acc)
        for c in range(NCH):
            t, tt = self.tmp()
            S.add("dve", lambda e, c=c, t=t: e.scalar_tensor_tensor(out=t[:, 0:T], in0=self.m[:, c, 0:T],
                                                                    scalar=self.vcol("gains", gidx, c), in1=rs[:, 0:T],
                                                                    op0=ALU.mult, op1=ALU.mult),
                  reads=["m%d" % c, rst, "vec"], writes=[tt])
            S.add(self.aux if c % 4 == 1 else "dve", lambda e, c=c, t=t: e.tensor_tensor(
                out=self.x[:, c, 0:T], in0=self.x[:, c, 0:T], in1=t[:, 0:T], op=ALU.add),
                reads=[tt, "x%d" % c], writes=["x%d" % c])

    def proj_group(self, w, wt, col0, rhs_fn, rhs_tokens, T, nk=NCH):
        bk, bt = self.bank()

        def mm(e):
            ins = None
            for k in range(nk):
                ins = e.matmul(bk[:, 0:T], w[:, k, col0:col0 + 128], rhs_fn(k), start=(k == 0), stop=(k == nk - 1))
            return ins
        self.S.add("pe", mm, reads=list(rhs_tokens) + [wt], writes=[bt])
        return bk, bt

    def load_x_dma(self, tile, use_R=False):
        S = self.S
        T = tile["T"]
        staged = []
        Rf = self.R.bitcast(F32)
        for tb in range(T // 128):
            src = tile["xsrc"][tb * 128:(tb + 1) * 128, :]
            assert use_R
            st = Rf[:, 4 * tb:4 * tb + 4, :].rearrange("p a f -> p (a f)")
            toks = ["R%d" % u for u in range(4 * tb, 4 * tb + 4)]
            S.add("sp", lambda e, st=st, src=src: e.dma_start(out=st[:, 0:D], in_=src), writes=toks, dma=True)
            staged.append((st, toks))
        return staged

    def load_x_direct(self, tile):
        T = tile["T"]
        for tb in range(T // 128):
            src = tile["xsrc"][tb * 128:(tb + 1) * 128, :]
            self.load_tm_to_fm(src, 128, lambda c0, n, tb=tb: self.x[:, c0:c0 + n, tb * 128:(tb + 1) * 128],
                               ["x%d" % c for c in range(NCH)], evac="act" if tb % 2 == 0 else "dve")

    def load_x_tr(self, tile, staged):
        for tb, (st, toks) in enumerate(staged):
            self.tm_to_fm(st, toks, 128, lambda c0, n, tb=tb: self.x[:, c0:c0 + n, tb * 128:(tb + 1) * 128],
                          ["x%d" % c for c in range(NCH)], evac="act" if tb % 2 == 0 else "dve")

    def store_y(self, tile):
        T = tile["T"]
        for tb in range(T // 128):
            self.store_fm_to_tm(lambda c, tb=tb: self.x[:, c, tb * 128:(tb + 1) * 128], ["x%d" % c for c in range(NCH)],
                                128, tile["ydst"][tb * 128:(tb + 1) * 128, :])

    def mix(self, tile, l):
        S = self.S
        T, nseq, L = tile["T"], tile["nseq"], tile["L"]
        sample = tile["kind"] == "s"
        first = tile.get("first", False)
        xn_toks = ["xn%d" % c for c in range(NCH)]
        rhs_xn = lambda k: self.xn[:, k, 0:T]
        self.pre_norm(T, 7 * l + 0)
        if sample:
            self.load_tm_to_fm(self.d["sta"][l], NSEQ_S * 2, lambda c0, n: self.stA[:, c0:c0 + n, :], ["stA"], evac="dve")
            self.load_tm_to_fm(self.d["stb"][l], NSEQ_S * 3, lambda c0, n: self.stB[:, c0:c0 + n, :], ["stB"], evac="dve")
            self.load_tm_to_fm(self.d["sth"][l], NSEQ_S, lambda c0, n: self.stH[:, c0:c0 + n, :], ["stH"], evac="dve")
        WA, WB = 2, 3
        aux = self.aux
        ctx = {}

        def stage1(c, col, wb_, wbt, wc_, wct, wh_, wht, wu_, wut):
            bu, but = self.proj_group(wu_, wut, col, rhs_xn, xn_toks, T)
            bhc, bhct = self.proj_group(wc_, wct, col, rhs_xn, xn_toks, T)
            bhh, bhht = self.proj_group(wh_, wht, col, rhs_xn, xn_toks, T)
            bhb, bhbt = self.proj_group(wb_, wbt, col, rhs_xn, xn_toks, T)
            U, Ut = self.ktmp("U")
            U3 = U[:, 0:nseq * (WB + L)].rearrange("p (s w) -> p s w", s=nseq)
            if sample:
                S.add(aux, lambda e: e.tensor_copy(out=U3[:, :, 0:WB], in_=self.stB[:, c, :].rearrange("p (s k) -> p s k", k=WB)),
                      reads=["stB"], writes=[Ut])
            else:
                S.add(aux, lambda e: e.tensor_copy(out=U3[:, 0, 0:WB], in_=self.carB[:, l, c, :]), reads=["carB%d" % c], writes=[Ut])
            bu3 = bu[:, 0:T].rearrange("p (s t) -> p s t", s=nseq)
            S.add("act", lambda e: e.copy(out=U3[:, :, WB:WB + L], in_=bu3), reads=[but, Ut], writes=[Ut])
            uc, uct = self.ktmp("uc")
            uc3 = uc[:, 0:T].rearrange("p (s t) -> p s t", s=nseq)
            S.add("act", lambda e: e.activation(out=uc3, in_=bu3, func=AF.Identity, scale=self.vcol("cbw", 4 * l + 3, c),
                                                bias=self.vcol("cbb", l, c)), reads=[but, "vec"], writes=[uct])
            if sample:
                S.add(aux, lambda e: e.tensor_copy(out=self.oB[:, c, :].rearrange("p (s k) -> p s k", k=WB), in_=U3[:, :, L:L + WB]),
                      reads=[Ut], writes=["oB"])
            else:
                S.add(aux, lambda e: e.tensor_copy(out=self.carB[:, l, c, :], in_=U3[:, 0, L:L + WB]), reads=[Ut], writes=["carB%d" % c])
            hcs, hcst = self.ktmp("hcs")
            S.add("act", lambda e: e.copy(out=hcs[:, 0:T], in_=bhc[:, 0:T]), reads=[bhct], writes=[hcst])
            G, Gt = self.ktmp("G")
            G3 = G[:, 0:nseq * (WA + L)].rearrange("p (s w) -> p s w", s=nseq)
            if sample:
                S.add(aux, lambda e: e.tensor_copy(out=G3[:, :, 0:WA], in_=self.stA[:, c, :].rearrange("p (s k) -> p s k", k=WA)),
                      reads=["stA"], writes=[Gt])
            else:
                S.add(aux, lambda e: e.tensor_copy(out=G3[:, 0, 0:WA], in_=self.carA[:, l, c, :]), reads=["carA%d" % c], writes=[Gt])
            S.add("dve", lambda e: e.tensor_tensor(out=G3[:, :, WA:WA + L], in0=bhh[:, 0:T].rearrange("p (s t) -> p s t", s=nseq),
                                                   in1=hcs[:, 0:T].rearrange("p (s t) -> p s t", s=nseq), op=ALU.mult),
                  reads=[bhht, hcst, Gt], writes=[Gt])
            if sample:
                S.add(aux, lambda e: e.tensor_copy(out=self.oA[:, c, :].rearrange("p (s k) -> p s k", k=WA), in_=G3[:, :, L:L + WA]),
                      reads=[Gt], writes=["oA"])
            else:
                S.add(aux, lambda e: e.tensor_copy(out=self.carA[:, l, c, :], in_=G3[:, 0, L:L + WA]), reads=[Gt], writes=["carA%d" % c])
            ya, yat = self.ktmp("ya")
            ya3 = ya[:, 0:T].rearrange("p (s t) -> p s t", s=nseq)
            S.add("act", lambda e: e.activation(out=ya3, in_=G3[:, :, 2:2 + L], func=AF.Copy, scale=self.vcol("caw", 3 * l + 2, c)),
                  reads=[Gt, "vec"], writes=[yat])
            for k in (1, 0):
                S.add("dve", lambda e, k=k: e.scalar_tensor_tensor(out=ya3, in0=G3[:, :, k:k + L], scalar=self.vcol("caw", 3 * l + k, c),
                                                                   in1=ya3, op0=ALU.mult, op1=ALU.add), reads=[Gt, yat, "vec"], writes=[yat])
            S.add("dve", lambda e: e.tensor_tensor(out=self.R[:, c, 0:T], in0=bhb[:, 0:T], in1=ya[:, 0:T], op=ALU.mult),
                  reads=[bhbt, yat], writes=["R%d" % c])
            for k in (2, 1, 0):
                S.add("dve", lambda e, k=k: e.scalar_tensor_tensor(out=uc3, in0=U3[:, :, k:k + L], scalar=self.vcol("cbw", 4 * l + k, c),
                                                                   in1=uc3, op0=ALU.mult, op1=ALU.add), reads=[Ut, uct, "vec"], writes=[uct])
            ucb, ucbt = self.tmpb()
            S.add("dve", lambda e: e.tensor_copy(out=ucb[:, 0:T], in_=uc[:, 0:T]), reads=[uct], writes=[ucbt])
            ctx[c] = dict(uc=uc, uct=uct, ucb=ucb, ucbt=ucbt)

        def stage2(c):
            uc, uct, ucb, ucbt = ctx[c]["uc"], ctx[c]["uct"], ctx[c]["ucb"], ctx[c]["ucbt"]
            bga_, bgat = self.bank()
            S.add("pe", lambda e: e.matmul(bga_[:, 0:T], self.bd[:, l * 2 + 0, c, :], ucb[:, 0:T], start=True, stop=True),
                  reads=[ucbt, "bd"], writes=[bgat])
            bgx_, bgxt = self.bank()
            S.add("pe", lambda e: e.matmul(bgx_[:, 0:T], self.bd[:, l * 2 + 1, c, :], ucb[:, 0:T], start=True, stop=True),
                  reads=[ucbt, "bd"], writes=[bgxt])

            def sigm(dst, bk, bias_ap, rd, wr):
                S.add("act", lambda e: e.activation(out=dst[:, 0:T], in_=bk[:, 0:T], func=AF.Exp, scale=-1.0, bias=bias_ap),
                      reads=rd + ["der"], writes=[wr])
                S.add("act", lambda e: e.activation(out=dst[:, 0:T], in_=dst[:, 0:T], func=AF.Ln, bias=self.cst[:, 1:2]),
                      reads=[wr, "cst"], writes=[wr])
                S.add("act", lambda e: e.activation(out=dst[:, 0:T], in_=dst[:, 0:T], func=AF.Exp, scale=-1.0), reads=[wr], writes=[wr])
            tr_, trt = self.ktmp("tr")
            sigm(tr_, bga_, self.der[:, c, 4 * l + 0:4 * l + 1], [bgat], trt)
            ti_, tit = self.ktmp("ti")
            sigm(ti_, bgx_, self.der[:, c, 4 * l + 1:4 * l + 2], [bgxt], tit)
            a_, at = self.ktmp("a")
            S.add("act", lambda e: e.activation(out=a_[:, 0:T], in_=tr_[:, 0:T], func=AF.Exp, scale=self.der[:, c, 4 * l + 2:4 * l + 3]),
                  reads=[trt, "der"], writes=[at])
            a2_, a2t = self.ktmp("a2")
            S.add("act", lambda e: e.activation(out=a2_[:, 0:T], in_=tr_[:, 0:T], func=AF.Exp, scale=self.der[:, c, 4 * l + 3:4 * l + 4]),
                  reads=[trt, "der"], writes=[a2t])
            S.add("act", lambda e: e.activation(out=a2_[:, 0:T], in_=a2_[:, 0:T], func=AF.Ln, scale=-1.0, bias=self.cst[:, 1:2]),
                  reads=[a2t, "cst"], writes=[a2t])
            S.add("act", lambda e: e.activation(out=a2_[:, 0:T], in_=a2_[:, 0:T], func=AF.Exp, scale=0.5), reads=[a2t], writes=[a2t])
            ctx[c].update(ti=ti_, tit=tit, a=a_, at=at, a2=a2_, a2t=a2t)

        def stage3(c):
            k = ctx.pop(c)
            uc, uct, ti_, tit, a_, at, a2_, a2t = k["uc"], k["uct"], k["ti"], k["tit"], k["a"], k["at"], k["a2"], k["a2t"]
            iu, iut = self.ktmp("iu")
            S.add("dve", lambda e: e.tensor_tensor(out=iu[:, 0:T], in0=ti_[:, 0:T], in1=uc[:, 0:T], op=ALU.mult),
                  reads=[tit, uct], writes=[iut])
            S.add("dve", lambda e: e.tensor_tensor(out=iu[:, 0:T], in0=iu[:, 0:T], in1=a2_[:, 0:T], op=ALU.mult),
                  reads=[iut, a2t], writes=[iut])
            hs, hst = self.ktmp("hs")
            if sample:
                a3 = a_[:, 0:T].rearrange("p (s t) -> p s t", s=nseq)
                b3 = iu[:, 0:T].rearrange("p (s t) -> p s t", s=nseq)
                t0, t0t = self.ktmp("t0")
                S.add("dve", lambda e: e.tensor_tensor(out=t0[:, 0:nseq], in0=a3[:, :, 0], in1=self.stH[:, c, :], op=ALU.mult),
                      reads=[at, "stH"], writes=[t0t])
                S.add("dve", lambda e: e.tensor_tensor(out=b3[:, :, 0], in0=b3[:, :, 0], in1=t0[:, 0:nseq], op=ALU.add),
                      reads=[t0t, iut], writes=[iut])
                S.add("dve", lambda e: e.memset(a3[:, :, 0], 0.0), reads=[t0t], writes=[at])
                init, init_toks = 0.0, []
            else:
                init, init_toks = self.carH[:, l, c, :], ["carH%d" % c]
            S.add("dve", lambda e: e.tensor_tensor_scan(out=hs[:, 0:T], data0=a_[:, 0:T], data1=iu[:, 0:T], initial=init,
                                                        op0=ALU.mult, op1=ALU.add), reads=[at, iut] + init_toks, writes=[hst])
            if sample:
                S.add(aux, lambda e: e.tensor_copy(out=self.oH[:, c, :], in_=hs[:, 0:T].rearrange("p (s t) -> p s t", s=nseq)[:, :, L - 1]),
                      reads=[hst], writes=["oH"])
            else:
                S.add(aux, lambda e: e.tensor_copy(out=self.carH[:, l, c, :], in_=hs[:, T - 1:T]), reads=[hst], writes=["carH%d" % c])
            S.add("act", lambda e: e.copy(out=self.R[:, 8 + c, 0:T], in_=hs[:, 0:T]), reads=[hst], writes=["R%d" % (8 + c)])

        for cp in range(4):
            wb_, wbt = self.wget()
            wc_, wct = self.wget()
            wh_, wht = self.wget()
            wu_, wut = self.wget()
            for j in range(2):
                c = cp * 2 + j
                stage1(c, j * 128, wb_, wbt, wc_, wct, wh_, wht, wu_, wut)
                if c >= 1:
                    stage2(c - 1)
                if c >= 2:
                    stage3(c - 2)
            self.wrel(4)
        stage2(NCH - 1)
        stage3(NCH - 2)
        stage3(NCH - 1)
        ba_toks = ["R%d" % c for c in range(NCH)]
        hs_toks = ["R%d" % (8 + c) for c in range(NCH)]
        for cp in range(4):
            wco, wcot = self.wget()
            wro, wrot = self.wget()
            wgc, wgct = self.wget()
            wgr, wgrt = self.wget()
            def sigm0(dst, bk, rd, wr):
                S.add("act", lambda e: e.activation(out=dst[:, 0:T], in_=bk[:, 0:T], func=AF.Exp, scale=-1.0),
                      reads=rd, writes=[wr])
                S.add("act", lambda e: e.activation(out=dst[:, 0:T], in_=dst[:, 0:T], func=AF.Ln, bias=self.cst[:, 1:2]),
                      reads=[wr, "cst"], writes=[wr])
                S.add("act", lambda e: e.activation(out=dst[:, 0:T], in_=dst[:, 0:T], func=AF.Exp, scale=-1.0),
                      reads=[wr], writes=[wr])
            part = {}
            for j in range(2):
                col = j * 128
                bgc, bgct = self.proj_group(wgc, wgct, col, rhs_xn, xn_toks, T)
                bgr, bgrt = self.proj_group(wgr, wgrt, col, rhs_xn, xn_toks, T)
                byc, byct = self.proj_group(wco, wcot, col, lambda k: self.R[:, k, 0:T], ba_toks, T)
                tc_, tct = self.tmp()
                sigm0(tc_, bgc, [bgct], tct)
                tg_, tgt = self.tmp()
                sigm0(tg_, bgr, [bgrt], tgt)
                S.add("dve", lambda e, tc_=tc_, bk=byc: e.tensor_tensor(
                    out=tc_[:, 0:T], in0=bk[:, 0:T], in1=tc_[:, 0:T], op=ALU.mult), reads=[tct, byct], writes=[tct])
                part[j] = (tc_, tct, tg_, tgt)
            for j in range(2):
                c = cp * 2 + j
                tc_, tct, tg_, tgt = part[j]
                byr, byrt = self.proj_group(wro, wrot, j * 128, lambda k: self.R[:, 8 + k, 0:T], hs_toks, T)
                S.add("dve", lambda e, tg_=tg_, bk=byr: e.tensor_tensor(
                    out=tg_[:, 0:T], in0=bk[:, 0:T], in1=tg_[:, 0:T], op=ALU.mult), reads=[tgt, byrt], writes=[tgt])
                S.add("dve", lambda e, tc_=tc_, tg_=tg_, c=c: e.tensor_tensor(
                    out=self.R[:, 16 + c, 0:T], in0=tc_[:, 0:T], in1=tg_[:, 0:T], op=ALU.add),
                    reads=[tct, tgt], writes=["R%d" % (16 + c)])
            self.wrel(4)
        z_toks = ["R%d" % (16 + c) for c in range(NCH)]
        acc = self.norm_begin(T)
        for cp in range(4):
            w, wt = self.wget()
            for j in range(2):
                c = cp * 2 + j
                bk, bt = self.proj_group(w, wt, j * 128, lambda k: self.R[:, 16 + k, 0:T], z_toks, T)
                S.add("act", lambda e, bk=bk, c=c: e.copy(out=self.m[:, c, 0:T], in_=bk[:, 0:T]),
                      reads=[bt], writes=["m%d" % c])
                self.norm_add(acc, bk[:, 0:T], [bt])
            self.wrel()
        self.post_norm_residual(T, 7 * l + 1, acc)
        if sample:
            self.store_fm_to_tm(lambda c: self.oA[:, c, :], ["oA"], NSEQ_S * 2, self.d["sca"][l])
            self.store_fm_to_tm(lambda c: self.oB[:, c, :], ["oB"], NSEQ_S * 3, self.d["scb"][l])
            self.store_fm_to_tm(lambda c: self.oH[:, c, :], ["oH"], NSEQ_S, self.d["sh"][l])
        elif tile.get("last", False):
            self.store_fm_to_tm(lambda c: self.carA[:, l, c, :], ["carA%d" % c_ for c_ in range(NCH)], 2, self.d["pca"][l])
            self.store_fm_to_tm(lambda c: self.carB[:, l, c, :], ["carB%d" % c_ for c_ in range(NCH)], 3, self.d["pcb"][l])
            self.store_fm_to_tm(lambda c: self.carH[:, l, c, :], ["carH%d" % c_ for c_ in range(NCH)], 1, self.d["ph"][l])

    def attn(self, tile, l):
        S = self.S
        T = tile["T"]
        sample = tile["kind"] == "s"
        xn_toks = ["xn%d" % c for c in range(NCH)]
        rhs_xn = lambda k: self.xn[:, k, 0:T]
        acc = self.norm_begin(T)
        for c in range(NCH):
            self.norm_add(acc, self.x[:, c, 0:T], ["x%d" % c])
            S.add("act", lambda e, c=c: e.activation(out=self.xn[:, c, 0:T], in_=self.x[:, c, 0:T], func=AF.Copy,
                                                     scale=self.vcol("gains", 7 * l + 2, c)),
                  reads=["x%d" % c, "vec"], writes=["xn%d" % c])
        rs, rst = self.norm_finish(acc)
        for cp in range(4):
            w, wt = self.wget()
            for j in range(2):
                c = cp * 2 + j
                bk, bt = self.proj_group(w, wt, j * 128, rhs_xn, xn_toks, T)
                S.add("dve", lambda e, bk=bk, c=c: e.scalar_tensor_tensor(
                    out=self.R[:, c, 0:T], in0=bk[:, 0:T], scalar=1.0 / 16.0, in1=rs[:, 0:T], op0=ALU.mult, op1=ALU.mult),
                    reads=[bt, rst], writes=["R%d" % c])
            self.wrel()
        if not sample:
            def head_a(h):
                pts = []
                for mc in range(2):
                    bk, bt = self.bank()

                    def mm(e, bk=bk, mc=mc):
                        ins = None
                        for dc in range(2):
                            ins = e.matmul(bk[:, 0:T], self.KT[:, l, 2 * h + dc, mc * 128:(mc + 1) * 128],
                                           self.R[:, 2 * h + dc, 0:T], start=(dc == 0), stop=(dc == 1))
                        return ins
                    S.add("pe", mm, reads=["KT%d" % l, "R%d" % (2 * h), "R%d" % (2 * h + 1)], writes=[bt])
                    ru = 8 + 2 * h + mc
                    S.add("act", lambda e, bk=bk, ru=ru: e.activation(out=self.R[:, ru, 0:T], in_=bk[:, 0:T], func=AF.Exp),
                          reads=[bt], writes=["R%d" % ru])
                    pts.append(ru)
                return pts

            def head_b(h, pts):
                bs, bst = self.bank()

                def mms(e):
                    ins = None
                    for mc in range(2):
                        ins = e.matmul(bs[:, 0:T], self.ones, self.R[:, pts[mc], 0:T], start=(mc == 0), stop=(mc == 1))
                    return ins
                S.add("pe", mms, reads=["R%d" % p for p in pts] + ["ones"], writes=[bst])
                ri, rit = self.tmp()
                S.add("act", lambda e: e.activation(out=ri[:, 0:T], in_=bs[:, 0:T], func=AF.Ln), reads=[bst], writes=[rit])
                S.add("act", lambda e: e.activation(out=ri[:, 0:T], in_=ri[:, 0:T], func=AF.Exp, scale=-1.0),
                      reads=[rit], writes=[rit])
                for dc in range(2):
                    bo, bot = self.bank()

                    def mmo(e, bo=bo, dc=dc):
                        ins = None
                        for mc in range(2):
                            ins = e.matmul(bo[:, 0:T], self.V[:, l, mc, h * 256 + dc * 128:h * 256 + (dc + 1) * 128],
                                           self.R[:, pts[mc], 0:T], start=(mc == 0), stop=(mc == 1))
                        return ins
                    S.add("pe", mmo, reads=["R%d" % p for p in pts] + ["V%d" % l], writes=[bot])
                    ou = 16 + 2 * h + dc
                    S.add("dve", lambda e, bo=bo, ou=ou: e.tensor_tensor(out=self.R[:, ou, 0:T], in0=bo[:, 0:T],
                                                                         in1=ri[:, 0:T], op=ALU.mult),
                          reads=[bot, rit], writes=["R%d" % ou])
            prev = None
            for h in range(4):
                pts = head_a(h)
                if prev is not None:
                    head_b(*prev)
                prev = (h, pts)
            head_b(*prev)
        else:
            self.attn_sample(l)
        o_toks = ["R%d" % (16 + c) for c in range(NCH)]
        acc = self.norm_begin(T)
        for cp in range(4):
            w, wt = self.wget()
            for j in range(2):
                c = cp * 2 + j
                bk, bt = self.proj_group(w, wt, j * 128, lambda k: self.R[:, 16 + k, 0:T], o_toks, T)
                S.add("act", lambda e, bk=bk, c=c: e.copy(out=self.m[:, c, 0:T], in_=bk[:, 0:T]), reads=[bt], writes=["m%d" % c])
                self.norm_add(acc, bk[:, 0:T], [bt])
            self.wrel()
        self.post_norm_residual(T, 7 * l + 3, acc)

    def attn_sample(self, l):
        S = self.S
        d = self.d
        T = TS
        for grp in range(2):
            bS, bSt = self.bank(pin=True)
            bO, bOt = self.bank(pin=True)
            ptu = 8 + grp
            PT = self.R[:, ptu, :]
            kvs = []
            for sl in range(8):
                s = grp * 8 + sl
                i = self.kv_rr
                self.kv_rr = (i + 1) % 2
                kt = self.kts[:, i]
                v = self.vs[:, i]
                st, stt = self.stg()
                kst = st.rearrange("p (a f) -> p a f", a=2)
                S.add("sp", lambda e, kst=kst, s=s: e.dma_start(out=kst, in_=d["ck"][l, s].rearrange("(a p) f -> p a f", p=128)),
                      writes=[stt], dma=True)
                S.add("pool", lambda e, v=v, s=s: e.dma_start(out=v, in_=d["cv"][l, s].rearrange("(a p) f -> p a f", p=128)),
                      writes=["V%d" % i], dma=True)
                for dp in range(4):
                    bk, bt = self.bank()

                    def tr(e, bk=bk, dp=dp, kst=kst):
                        ins = None
                        for dj in range(2):
                            dc = dp * 2 + dj
                            for a in range(2):
                                ins = e.transpose(bk[:, dj * 256 + a * 128:dj * 256 + (a + 1) * 128],
                                                  kst[:, a, dc * 128:(dc + 1) * 128], self.ident)
                        return ins
                    S.add("pe", tr, reads=[stt, "ident"], writes=[bt])
                    eng = "act" if dp % 2 == 0 else "dve"
                    if eng == "act":
                        S.add("act", lambda e, bk=bk, dp=dp, kt=kt: e.copy(out=kt[:, 2 * dp:2 * dp + 2, :],
                                                                          in_=bk.rearrange("p (j m) -> p j m", j=2)),
                              reads=[bt], writes=["KT%d" % i])
                    else:
                        S.add("dve", lambda e, bk=bk, dp=dp, kt=kt: e.tensor_copy(out=kt[:, 2 * dp:2 * dp + 2, :],
                                                                                 in_=bk.rearrange("p (j m) -> p j m", j=2)),
                              reads=[bt], writes=["KT%d" % i])
                def mm(e, kt=kt, sl=sl, s=s, bS=bS):
                    ins = None
                    for h in range(4):
                        for mc in range(2):
                            col = mc * 256 + sl * 32 + h * 8
                            for dc in range(2):
                                ins = e.matmul(bS[:, col:col + 8], kt[:, 2 * h + dc, mc * 128:(mc + 1) * 128],
                                               self.R[:, 2 * h + dc, s * 8:(s + 1) * 8], start=(dc == 0), stop=(dc == 1))
                    return ins
                S.add("pe", mm, reads=["KT%d" % i] + ["R%d" % c for c in range(NCH)], writes=[bSt])
                kvs.append((v, "V%d" % i, sl, s))
                S.add("act", lambda e, sl=sl, bS=bS, PT=PT: e.activation(
                    out=PT.rearrange("p (a c) -> p a c", a=2)[:, :, sl * 32:(sl + 1) * 32],
                    in_=bS.rearrange("p (a c) -> p a c", a=2)[:, :, sl * 32:(sl + 1) * 32], func=AF.Exp),
                    reads=[bSt], writes=["R%d" % ptu])
                def mmo(e, v=v, sl=sl, bO=bO, PT=PT):
                    ins = None
                    for h in range(4):
                        for dc in range(2):
                            c = 2 * h + dc
                            for mc in range(2):
                                ins = e.matmul(bO[:, c * 64 + sl * 8:c * 64 + sl * 8 + 8],
                                               v[:, mc, h * 256 + dc * 128:h * 256 + (dc + 1) * 128],
                                               PT[:, mc * 256 + sl * 32 + h * 8:mc * 256 + sl * 32 + h * 8 + 8],
                                               start=(mc == 0), stop=(mc == 1))
                    return ins
                S.add("pe", mmo, reads=["V%d" % i, "R%d" % ptu], writes=[bOt])
            bs, bst = self.bank()

            def mms(e, bs=bs, PT=PT):
                ins = None
                for mc in range(2):
                    ins = e.matmul(bs[:, 0:256], self.ones, PT[:, mc * 256:(mc + 1) * 256], start=(mc == 0), stop=(mc == 1))
                return ins
            S.add("pe", mms, reads=["R%d" % ptu, "ones"], writes=[bst])
            ri, rit = self.tmp()
            S.add("act", lambda e, ri=ri, bs=bs: e.activation(out=ri[:, 0:256], in_=bs[:, 0:256], func=AF.Ln), reads=[bst], writes=[rit])
            S.add("act", lambda e, ri=ri: e.activation(out=ri[:, 0:256], in_=ri[:, 0:256], func=AF.Exp, scale=-1.0),
                  reads=[rit], writes=[rit])
            for h in range(4):
                for dc in range(2):
                    c = 2 * h + dc
                    S.add("dve", lambda e, c=c, h=h, ri=ri, bO=bO, grp=grp: e.tensor_tensor(
                        out=self.R[:, 16 + c, grp * 64:(grp + 1) * 64].rearrange("p (s t) -> p s t", s=8),
                        in0=bO[:, c * 64:(c + 1) * 64].rearrange("p (s t) -> p s t", s=8),
                        in1=ri[:, 0:256].rearrange("p (s h t) -> p s h t", s=8, h=4)[:, :, h, :], op=ALU.mult),
                        reads=[bOt, rit], writes=["R%d" % (16 + c)])
            self.unpin(bSt)
            self.unpin(bOt)

    def ffn(self, tile, l):
        S = self.S
        T = tile["T"]
        xn_toks = ["xn%d" % c for c in range(NCH)]
        rhs_xn = lambda k: self.xn[:, k, 0:T]
        acc0 = self.norm_begin(T)
        wg0, wgt0 = self.wget()
        wu0, wut0 = self.wget()
        first = [self.bank() for _ in range(4)]
        fspec = [(wg0, 0), (wu0, 0), (wg0, 128), (wu0, 128)]
        for c in range(NCH):
            self.norm_add(acc0, self.x[:, c, 0:T], ["x%d" % c])
            S.add("act", lambda e, c=c: e.activation(out=self.xn[:, c, 0:T], in_=self.x[:, c, 0:T], func=AF.Copy,
                                                     scale=self.vcol("gains", 7 * l + 4, c)),
                  reads=["x%d" % c, "vec"], writes=["xn%d" % c])

            def mmf(e, c=c):
                ins = None
                for (bk, bt), (w, col0) in zip(first, fspec):
                    ins = e.matmul(bk[:, 0:T], w[:, c, col0:col0 + 128], self.xn[:, c, 0:T], start=(c == 0), stop=(c == NCH - 1))
                return ins
            S.add("pe", mmf, reads=["xn%d" % c, wgt0, wut0], writes=[bt for _, bt in first])
        rs, rst = self.norm_finish(acc0, pin=True)
        for jp in range(11):
            if jp == 0:
                wg, wgt, wu, wut = wg0, wgt0, wu0, wut0
            else:
                wg, wgt = self.wget()
                wu, wut = self.wget()
            for jj in range(2):
                j = jp * 2 + jj
                if jp == 0:
                    bg, bgt = first[2 * jj]
                    bu, but = first[2 * jj + 1]
                else:
                    bg, bgt = self.proj_group(wg, wgt, jj * 128, rhs_xn, xn_toks, T)
                    bu, but = self.proj_group(wu, wut, jj * 128, rhs_xn, xn_toks, T)
                sg, sgt = self.tmp()
                S.add("dve", lambda e, sg=sg, bg=bg: e.tensor_tensor(out=sg[:, 0:T], in0=bg[:, 0:T], in1=rs[:, 0:T], op=ALU.mult),
                      reads=[bgt, rst], writes=[sgt])
                S.add("act", lambda e, sg=sg: e.activation(out=sg[:, 0:T], in_=sg[:, 0:T], func=AF.Silu),
                      reads=[sgt], writes=[sgt])
                tu, tut = self.tmp()
                S.add("dve", lambda e, tu=tu, bu=bu: e.tensor_tensor(out=tu[:, 0:T], in0=bu[:, 0:T], in1=rs[:, 0:T], op=ALU.mult),
                      reads=[but, rst], writes=[tut])
                S.add("dve", lambda e, sg=sg, tu=tu, j=j: e.tensor_tensor(out=self.R[:, j, 0:T], in0=tu[:, 0:T], in1=sg[:, 0:T],
                                                                         op=ALU.mult), reads=[tut, sgt], writes=["R%d" % j])
            self.wrel(2)
        self.tmp_unpin(rst)
        h_toks = ["R%d" % j for j in range(NJ)]
        acc = self.norm_begin(T)
        for c in range(NCH):
            w0, wt0 = self.wget()
            w1, wt1 = self.wget()
            bk, bt = self.bank()

            def mm(e, bk=bk, w0=w0, w1=w1):
                ins = None
                for k in range(NJ):
                    w = w0 if k < 11 else w1
                    ins = e.matmul(bk[:, 0:T], w[:, k % 11, :], self.R[:, k, 0:T], start=(k == 0), stop=(k == NJ - 1))
                return ins
            S.add("pe", mm, reads=h_toks + [wt0, wt1], writes=[bt])
            S.add("act", lambda e, bk=bk, c=c: e.copy(out=self.m[:, c, 0:T], in_=bk[:, 0:T]), reads=[bt], writes=["m%d" % c])
            self.norm_add(acc, bk[:, 0:T], [bt])
            self.wrel(2)
        if l == DEPTH - 1 and tile.get("next") is not None:
            tile["next"]["staged"] = self.load_x_dma(tile["next"], use_R=True)
        self.post_norm_residual(T, 7 * l + 5, acc)

    def build(self):
        self.plan_weights()
        tiles = []
        for i in range(NPT):
            tiles.append(dict(kind="p", T=TP, nseq=1, L=TP, first=(i == 0), last=(i == NPT - 1),
                              xsrc=self.d["xp"][i * TP:(i + 1) * TP, :], ydst=self.d["yp"][i * TP:(i + 1) * TP, :]))
        tiles.append(dict(kind="s", T=TS, nseq=NSEQ_S, L=L_S, xsrc=self.d["xs"], ydst=self.d["ys"]))
        for ti in range(len(tiles) - 1):
            tiles[ti]["next"] = tiles[ti + 1]
        self.prologue(tiles[0])
        for ti, tile in enumerate(tiles):
            self.aux = "dve" if ti == 0 else "pool"
            if ti > 0:
                self.load_x_tr(tile, tile["staged"])
            for l in range(DEPTH):
                self.mix(tile, l)
                self.attn(tile, l)
                self.ffn(tile, l)
            self.store_y(tile)
        assert self.w_cons == len(self.wplan), (self.w_cons, len(self.wplan))
        self.S.emit()
        return self.nc


_CACHE = {}


def _get_program():
    if "nc" not in _CACHE:
        _CACHE["nc"] = Builder().build()
    return _CACHE["nc"]


def kernel(x_prompt, x_sample, state_conv_a, state_conv_b, state_rglru, cache_mem_k, cache_mem_v, mem_prompt,
           norm_gains, w_in, conv_a_w, w_conv_out, conv_b_w, conv_b_b, w_gate_a, b_gate_a, w_gate_x, b_gate_x,
           lru_lambda, w_rnn_out, w_mix_out, w_kv_x, w_q_x, w_o_x, w_ffn_in, w_ffn_out):
    f = lambda a: np.ascontiguousarray(np.asarray(a, dtype=np.float32))
    x_prompt, x_sample = f(x_prompt), f(x_sample)
    state_conv_a, state_conv_b, state_rglru = f(state_conv_a), f(state_conv_b), f(state_rglru)
    cache_mem_k, cache_mem_v, mem_prompt = f(cache_mem_k), f(cache_mem_v), f(mem_prompt)
    n = 8
    shared = {
        "gains": f(norm_gains).reshape(14, D), "caw": f(conv_a_w).reshape(6, D), "cbw": f(conv_b_w).reshape(8, D),
        "cbb": f(conv_b_b), "bga": f(b_gate_a), "bgx": f(b_gate_x), "lam": f(lru_lambda),
        "ident": np.eye(128, dtype=np.float32),
        "w_in": f(w_in), "w_conv_out": f(w_conv_out), "w_gate_a": f(w_gate_a), "w_gate_x": f(w_gate_x),
        "w_rnn_out": f(w_rnn_out), "w_mix_out": f(w_mix_out), "w_kv": f(w_kv_x), "w_q": f(w_q_x), "w_o": f(w_o_x),
        "w_ffn_in": f(w_ffn_in), "w_ffn_out": f(w_ffn_out),
    }
    in_maps = []
    for b in range(n):
        sl = slice(b * NSEQ_S, (b + 1) * NSEQ_S)
        m = dict(shared)
        m["xp"] = x_prompt[b]
        m["xs"] = x_sample[sl].reshape(TS, D)
        m["sta"] = np.ascontiguousarray(state_conv_a[:, sl]).reshape(DEPTH, NSEQ_S * 2, D)
        m["stb"] = np.ascontiguousarray(state_conv_b[:, sl]).reshape(DEPTH, NSEQ_S * 3, D)
        m["sth"] = np.ascontiguousarray(state_rglru[:, sl]).reshape(DEPTH, NSEQ_S, D)
        m["ck"] = np.ascontiguousarray(cache_mem_k[:, sl]).reshape(DEPTH, NSEQ_S, NMEM, D)
        m["cv"] = np.ascontiguousarray(cache_mem_v[:, sl]).reshape(DEPTH, NSEQ_S, NMEM, D)
        m["memp"] = mem_prompt[b]
        in_maps.append(m)
    nc = _get_program()
    res = run_bass_kernel_spmd(nc, in_maps, core_ids=list(range(n)))
    r = res.results
    y_prompt = np.stack([r[b]["yp"] for b in range(n)], axis=0)
    y_sample = np.concatenate([r[b]["ys"].reshape(NSEQ_S, L_S, D) for b in range(n)], axis=0)
    p_conv_a = np.stack([r[b]["pca"] for b in range(n)], axis=1)
    p_conv_b = np.stack([r[b]["pcb"] for b in range(n)], axis=1)
    p_rglru = np.stack([r[b]["ph"].reshape(DEPTH, D) for b in range(n)], axis=1)
    p_mem_k = np.stack([r[b]["pk"].reshape(DEPTH, NMEM, 4, 256) for b in range(n)], axis=1)
    p_mem_v = np.stack([r[b]["pv"].reshape(DEPTH, NMEM, 4, 256) for b in range(n)], axis=1)
    s_conv_a = np.concatenate([r[b]["sca"].reshape(DEPTH, NSEQ_S, 2, D) for b in range(n)], axis=1)
    s_conv_b = np.concatenate([r[b]["scb"].reshape(DEPTH, NSEQ_S, 3, D) for b in range(n)], axis=1)
    s_rglru = np.concatenate([r[b]["sh"].reshape(DEPTH, NSEQ_S, D) for b in range(n)], axis=1)
    return (y_prompt.astype(np.float32), y_sample.astype(np.float32), p_conv_a.astype(np.float32),
            p_conv_b.astype(np.float32), p_rglru.astype(np.float32), p_mem_k.astype(np.float32),
            p_mem_v.astype(np.float32), s_conv_a.astype(np.float32), s_conv_b.astype(np.float32),
            s_rglru.astype(np.float32))
```
